# Optimizing a Trainium2 kernel written in Bass

```python
import jax, jax.numpy as jnp
from jax import lax
import numpy as np

D_MODEL = 1024
BATCH = 8
SEQ = 2048
DEPTH = 1
DEC_BATCH = 128
DEC_SEQ = 8
PAST_LEN = 16384
PAGE_SIZE = 128

MIX_WIDTH = D_MODEL
A_WIDTH = MIX_WIDTH // 2
B_WIDTH = MIX_WIDTH - A_WIDTH
H_A = 4
DK_A = A_WIDTH // H_A
DV_A = A_WIDTH // H_A
H_B = 4
DK_B = B_WIDTH // H_B
DV_B = B_WIDTH // H_B
IN_WIDTH = 4 * A_WIDTH + 4 * B_WIDTH
D_FF = 2816
CONV_W = 3
CHUNK = 64
ROPE_BASE = 10000.0
EPS = 1e-6

kernel_name = "hgrn2_retention_convffn_hybrid_step"

F32 = jnp.float32


def _rmsnorm(x, w):
    xf = x.astype(F32)
    return xf * lax.rsqrt(jnp.mean(xf * xf, axis=-1, keepdims=True) + EPS) * w.astype(F32)


def _groupnorm(x, w):
    xf = x.astype(F32)
    mu = jnp.mean(xf, axis=-1, keepdims=True)
    xc = xf - mu
    return xc * lax.rsqrt(jnp.mean(xc * xc, axis=-1, keepdims=True) + EPS) * w.astype(F32)


def _heads(a, h):
    return a.reshape(a.shape[0], a.shape[1], h, -1)


def _rope(x, pos):
    half = x.shape[-1] // 2
    inv = 1.0 / (ROPE_BASE ** (jnp.arange(half, dtype=F32) / half))
    ang = pos.astype(F32)[:, None] * inv[None, :]
    cos = jnp.cos(ang)[None, :, None, :]
    sin = jnp.sin(ang)[None, :, None, :]
    x1, x2 = x[..., :half], x[..., half:]
    return jnp.concatenate([x1 * cos - x2 * sin, x1 * sin + x2 * cos], axis=-1)


def _chunk_len(t):
    return CHUNK if t % CHUNK == 0 else t


def _to_chunks(a, c):
    nb, nt, h, d = a.shape
    return a.reshape(nb, nt // c, c, h, d).transpose(1, 0, 3, 2, 4)


def _from_chunks(a):
    n, nb, h, c, d = a.shape
    return a.transpose(1, 0, 3, 2, 4).reshape(nb, n * c, h, d)


def _hgrn2_chunked(q, k, log_f, v, s0):
    c = _chunk_len(q.shape[1])
    mask = jnp.tril(jnp.ones((c, c), dtype=bool))[:, :, None]

    def step(S, inp):
        qq, kk, gg, vv = inp
        b = jnp.cumsum(gg, axis=2)
        rel = b[:, :, :, None, :] - b[:, :, None, :, :]
        dec = jnp.where(mask, jnp.exp(jnp.where(mask, rel, 0.0)), 0.0)
        scores = jnp.einsum('bhtd,bhsd,bhtsd->bhts', qq, kk, dec)
        o = (jnp.einsum('bhts,bhsv->bhtv', scores, vv)
             + jnp.einsum('bhtd,bhdv->bhtv', qq * jnp.exp(b), S))
        b_last = b[:, :, -1:, :]
        S_new = (jnp.exp(b_last[:, :, 0, :])[..., None] * S
                 + jnp.einsum('bhsd,bhsv->bhdv', kk * jnp.exp(b_last - b), vv))
        return S_new, o

    S, o = lax.scan(step, s0, (_to_chunks(q, c), _to_chunks(k, c), _to_chunks(log_f, c), _to_chunks(v, c)))
    return _from_chunks(o), S


def _retention_chunked(q, k, v, s0, log_gamma):
    c = _chunk_len(q.shape[1])
    idx = jnp.arange(c, dtype=F32)
    rel = idx[:, None] - idx[None, :]
    causal = rel >= 0
    lg = log_gamma[:, None, None]
    dec = jnp.where(causal[None], jnp.exp(jnp.where(causal, rel, 0.0)[None] * lg), 0.0)
    inner = jnp.exp((idx + 1.0)[None, :] * log_gamma[:, None])[..., None]
    sdec = jnp.exp((c - 1.0 - idx)[None, :] * log_gamma[:, None])[..., None]
    cdec = jnp.exp(c * log_gamma)[:, None, None]

    def step(S, inp):
        qq, kk, vv = inp
        scores = jnp.einsum('bhtd,bhsd->bhts', qq, kk) * dec
        o = jnp.einsum('bhts,bhsv->bhtv', scores, vv) + jnp.einsum('bhtd,bhdv->bhtv', qq, S) * inner
        S_new = cdec * S + jnp.einsum('bhsd,bhsv->bhdv', kk * sdec, vv)
        return S_new, o

    S, o = lax.scan(step, s0, (_to_chunks(q, c), _to_chunks(k, c), _to_chunks(v, c)))
    return _from_chunks(o), S


def _token_mix(h, pos, s_a, s_b, w_in, lb, norm_a, norm_b, w_out):
    nb, nt, _ = h.shape
    proj = h @ w_in
    A, Bw = A_WIDTH, B_WIDTH
    qa, fa, ia, ga, qb, kb, vb, gb = jnp.split(
        proj, [A, 2 * A, 3 * A, 4 * A, 4 * A + Bw, 4 * A + 2 * Bw, 4 * A + 3 * Bw], axis=-1)
    lbh = lb.reshape(H_A, DK_A)
    f = lbh + (1.0 - lbh) * jax.nn.sigmoid(_heads(fa.astype(F32), H_A))
    oa, s_a_new = _hgrn2_chunked(_heads(qa.astype(F32), H_A), 1.0 - f, jnp.log(f),
                                 _heads(ia.astype(F32), H_A), s_a.astype(F32))
    oa = _rmsnorm(oa, norm_a) * jax.nn.silu(_heads(ga.astype(F32), H_A))
    log_gamma = jnp.log1p(-jnp.exp2(-5.0 - jnp.arange(H_B, dtype=F32)))
    qr = _rope(_heads(qb.astype(F32), H_B), pos)
    kr = _rope(_heads(kb.astype(F32), H_B), pos) * (DK_B ** -0.5)
    ob, s_b_new = _retention_chunked(qr, kr, _heads(vb.astype(F32), H_B), s_b.astype(F32), log_gamma)
    ob = _groupnorm(ob, norm_b) * jax.nn.silu(_heads(gb.astype(F32), H_B))
    o = jnp.concatenate([oa.reshape(nb, nt, A), ob.reshape(nb, nt, Bw)], axis=-1).astype(h.dtype)
    return o @ w_out, s_a_new, s_b_new


def _conv_ffn(h, buf, w_up, conv_w, conv_b, w_down):
    nt = h.shape[1]
    up = h @ w_up
    ext = jnp.concatenate([buf.astype(up.dtype), up], axis=1)
    c = conv_b + sum(ext[:, j:j + nt] * conv_w[j] for j in range(CONV_W))
    u, g = jnp.split(c, 2, axis=-1)
    return (jax.nn.silu(g) * u) @ w_down, ext[:, nt:]


def _trunk(x, pos, s_a, s_b, s_c, w_norm1, w_in, hgrn_lb, hgrn_norm_w, ret_norm_w, w_out,
           w_norm2, w_ffn_in, conv_w, conv_b, w_ffn_out, w_norm_f):
    lb_all = jnp.cumsum(jax.nn.softmax(hgrn_lb.astype(F32), axis=0), axis=0)
    na, nbs, nc = [], [], []
    for l in range(DEPTH):
        h = _rmsnorm(x, w_norm1[l]).astype(x.dtype)
        mix, sa, sb = _token_mix(h, pos, s_a[l], s_b[l], w_in[l], lb_all[l], hgrn_norm_w[l],
                                 ret_norm_w[l], w_out[l])
        x = x + mix
        h = _rmsnorm(x, w_norm2[l]).astype(x.dtype)
        ff, sc = _conv_ffn(h, s_c[l], w_ffn_in[l], conv_w[l], conv_b[l], w_ffn_out[l])
        x = x + ff
        na.append(sa); nbs.append(sb); nc.append(sc)
    y = _rmsnorm(x, w_norm_f).astype(x.dtype)
    return (y, jnp.stack(na).astype(x.dtype), jnp.stack(nbs).astype(x.dtype),
            jnp.stack(nc).astype(x.dtype))


def setup_inputs(seed: int = 0) -> dict:
    key = jax.random.key(seed)
    ks = jax.random.split(key, 20)
    nrm = jax.random.normal
    return {
        "x_prompt": nrm(ks[0], (BATCH, SEQ, D_MODEL), F32),
        "x_sample": nrm(ks[1], (DEC_BATCH, DEC_SEQ, D_MODEL), F32),
        "state_hgrn": 0.3 * nrm(ks[2], (DEPTH, DEC_BATCH, H_A, DK_A, DV_A), F32),
        "state_ret": 0.3 * nrm(ks[3], (DEPTH, DEC_BATCH, H_B, DK_B, DV_B), F32),
        "state_conv": nrm(ks[4], (DEPTH, DEC_BATCH, CONV_W - 1, 2 * D_FF), F32),
        "w_norm1": 1.0 + 0.02 * nrm(ks[5], (DEPTH, D_MODEL), F32),
        "w_in": nrm(ks[6], (DEPTH, D_MODEL, IN_WIDTH), F32) * D_MODEL ** -0.5,
        "hgrn_lb": 0.1 * nrm(ks[7], (DEPTH + 1, A_WIDTH), F32),
        "hgrn_norm_w": 1.0 + 0.02 * nrm(ks[8], (DEPTH, DV_A), F32),
        "ret_norm_w": 1.0 + 0.02 * nrm(ks[9], (DEPTH, DV_B), F32),
        "w_out": nrm(ks[10], (DEPTH, MIX_WIDTH, D_MODEL), F32) * MIX_WIDTH ** -0.5,
        "w_norm2": 1.0 + 0.02 * nrm(ks[11], (DEPTH, D_MODEL), F32),
        "w_ffn_in": nrm(ks[12], (DEPTH, D_MODEL, 2 * D_FF), F32) * D_MODEL ** -0.5,
        "conv_w": nrm(ks[13], (DEPTH, CONV_W, 2 * D_FF), F32) * CONV_W ** -0.5,
        "conv_b": 0.02 * nrm(ks[14], (DEPTH, 2 * D_FF), F32),
        "w_ffn_out": nrm(ks[15], (DEPTH, D_FF, D_MODEL), F32) * D_FF ** -0.5,
        "w_norm_f": 1.0 + 0.02 * nrm(ks[16], (D_MODEL,), F32),
    }


def reference(x_prompt, x_sample, state_hgrn, state_ret, state_conv, w_norm1, w_in, hgrn_lb,
              hgrn_norm_w, ret_norm_w, w_out, w_norm2, w_ffn_in, conv_w, conv_b, w_ffn_out,
              w_norm_f):
    nbp, ntp, _ = x_prompt.shape
    nts = x_sample.shape[1]
    pos_p = jnp.arange(ntp, dtype=jnp.int32)
    pos_s = PAST_LEN + jnp.arange(nts, dtype=jnp.int32)
    z_a = jnp.zeros((DEPTH, nbp, H_A, DK_A, DV_A), F32)
    z_b = jnp.zeros((DEPTH, nbp, H_B, DK_B, DV_B), F32)
    z_c = jnp.zeros((DEPTH, nbp, CONV_W - 1, 2 * D_FF), x_prompt.dtype)
    y_prompt, ha_p, rb_p, cv_p = _trunk(
        x_prompt, pos_p, z_a, z_b, z_c, w_norm1, w_in, hgrn_lb, hgrn_norm_w, ret_norm_w, w_out,
        w_norm2, w_ffn_in, conv_w, conv_b, w_ffn_out, w_norm_f)
    y_sample, ha_s, rb_s, cv_s = _trunk(
        x_sample, pos_s, state_hgrn, state_ret, state_conv, w_norm1, w_in, hgrn_lb, hgrn_norm_w,
        ret_norm_w, w_out, w_norm2, w_ffn_in, conv_w, conv_b, w_ffn_out, w_norm_f)
    return (y_prompt, y_sample, ha_p, rb_p, cv_p, ha_s, rb_s, cv_s)
```

```python
import contextlib
import numpy as np
import ml_dtypes
import concourse.bass as bass
import concourse.mybir as mybir
from concourse.bass_utils import run_bass_kernel_spmd

F32 = mybir.dt.float32
BF16 = mybir.dt.bfloat16
AF = mybir.ActivationFunctionType
ALU = mybir.AluOpType

ENGS = ["pe", "act", "dve", "pool", "sp"]
SKIP_SAME_ENGINE_WAW = False
EPS = 1e-6
NT = 17
DFF = 2816
NFT = 22
PAST_LEN = 16384


class Buf:
    __slots__ = ("name", "w", "r", "sem", "semcnt", "excl")

    def __init__(self, name="", excl=False):
        self.name = name
        self.excl = excl
        self.w = None
        self.r = []
        self.sem = None
        self.semcnt = 0


class Op:
    __slots__ = ("eng", "fn", "deps", "dma", "inc", "incval", "dmasem", "dmaval", "dmadeps")

    def __init__(self, eng, fn, dma):
        self.eng = eng
        self.fn = fn
        self.deps = []
        self.dmadeps = {}
        self.dma = dma
        self.inc = False
        self.incval = 0
        self.dmasem = None
        self.dmaval = 0


class Prog:
    def __init__(self, nc):
        self.nc = nc
        self.ops = {e: [] for e in ENGS}
        self.dma_bufs = []
        self.all_ops = []
        self.pending = {e: [] for e in ENGS}
        self.recent_dmas = []

    def _add(self, eng, fn, reads, writes, dma=False, key=None):
        op = Op(eng, fn, dma)
        deps = list(self.pending[eng])
        self.pending[eng] = []
        for b in reads:
            if b.w is not None:
                deps.append(b.w)
            if b.excl:
                deps.extend(r for r in b.r if r.eng != eng)
        for b in writes:
            if b.w is not None and not (dma and b.w.dma) and (not SKIP_SAME_ENGINE_WAW or b.w.eng != eng or b.w.dma or dma):
                deps.append(b.w)
            deps.extend(r for r in b.r if (not SKIP_SAME_ENGINE_WAW or r.eng != eng or r.dma or dma))
        seen = set()
        for d in deps:
            if id(d) in seen:
                continue
            seen.add(id(d))
            if d.dma:
                kb = d.dmasem
                op.dmadeps[id(kb)] = (kb, kb.semcnt)
            else:
                op.deps.append(d)
        for b in reads:
            b.r.append(op)
        for b in writes:
            b.w = op
            b.r = []
        if dma:
            if key.sem is None:
                key.sem = len(self.dma_bufs)
                self.dma_bufs.append(key)
            key.semcnt += 16
            op.dmasem = key
            op.dmaval = key.semcnt
            self.recent_dmas.append(op)
        self.ops[eng].append(op)
        self.all_ops.append(op)
        return op

    def op(self, eng, fn, reads=(), writes=()):
        return self._add(eng, fn, list(reads), list(writes))

    def dma(self, eng, fn, reads=(), writes=(), key=None):
        return self._add(eng, fn, list(reads), list(writes), dma=True, key=key)

    def barrier(self, markers):
        mops = []
        for e, fn in markers.items():
            mops.append(self.op(e, fn, writes=[Buf("bar")]))
        for e in ENGS:
            self.pending[e] = self.pending[e] + mops + self.recent_dmas
        self.recent_dmas = []

    def emit(self):
        nc = self.nc
        for op in self.all_ops:
            for d in op.deps:
                if d.eng == op.eng and op.eng in ("pe", "sp") and not op.dma:
                    continue
                d.inc = True
        for e in ENGS:
            c = 0
            for op in self.ops[e]:
                if op.inc and not op.dma:
                    c += 1
                    op.incval = c
        with contextlib.ExitStack() as st:
            esem = {e: st.enter_context(nc.semaphore("s_" + e)) for e in ENGS}
            dsem = [st.enter_context(nc.semaphore("d%d" % i)) for i in range(len(self.dma_bufs))]
            block = st.enter_context(nc.Block())
            engobj = {"pe": "tensor", "act": "scalar", "dve": "vector", "pool": "gpsimd", "sp": "sync"}
            ops = self.ops
            dma_bufs = self.dma_bufs

            def make(e):
                def body(eng):
                    waited = {}
                    for op in ops[e]:
                        need = {}
                        for (kb, v) in op.dmadeps.values():
                            need[("d", kb.sem)] = (dsem[kb.sem], v)
                        for d in op.deps:
                            if not d.inc:
                                continue
                            if d.eng == e and e in ("pe", "sp") and not op.dma:
                                continue
                            k = ("e", d.eng)
                            if k not in need or need[k][1] < d.incval:
                                need[k] = (esem[d.eng], d.incval)
                        for k, (s, v) in need.items():
                            if waited.get(k, 0) >= v:
                                continue
                            eng.wait_ge(s, v)
                            waited[k] = v
                        ins = op.fn(eng)
                        if op.dma:
                            ins.then_inc(dsem[op.dmasem.sem], 16)
                        elif op.inc:
                            ins.then_inc(esem[e], 1)
                    if e == "sp":
                        for b in dma_bufs:
                            eng.wait_ge(dsem[b.sem], b.semcnt)
                        for e2 in ENGS:
                            if e2 == "sp":
                                continue
                            tot = sum(1 for o in ops[e2] if o.inc and not o.dma)
                            if tot:
                                eng.wait_ge(esem[e2], tot)
                return body

            for e in ENGS:
                getattr(block, engobj[e])(make(e))


class Mem:
    def __init__(self, nc, base=16640, limit=229312):
        self.nc = nc
        self.off = base
        self.limit = limit
        self.n = 0
        self.peak = 0

    def alloc(self, name, shape, dtype):
        sz = int(np.prod(shape[1:])) * mybir.dt.size(dtype)
        off = (self.off + 63) // 64 * 64
        self.n += 1
        t = self.nc.alloc_sbuf_tensor_at("%s_%d" % (name, self.n), list(shape), dtype, offset=off)
        self.off = off + sz
        self.peak = max(self.peak, self.off)
        assert self.off <= self.limit, (name, self.off, self.limit)
        return t

    def mark(self):
        return self.off

    def reset(self, m):
        self.off = m


GAM = [1.0 - 2.0 ** (-5 - h) for h in range(4)]


class _Stop(Exception):
    pass


def build_program(dbg=None):
    dbg = dbg or {}

    def ck(n):
        if dbg.get("stop") == n:
            raise _Stop()

    nc = bass.Bass("TRN2", target_bir_lowering=False)

    def din(name, shape, dt=F32):
        return nc.dram_tensor(name, list(shape), dt, kind="ExternalInput").ap()

    def dout(name, shape, dt=F32):
        return nc.dram_tensor(name, list(shape), dt, kind="ExternalOutput").ap()

    x_d = din("x", [NT * 128, 1024])
    sh_d = din("sh", [16, 4, 128, 128])
    sr_d = din("sr", [16, 4, 128, 128])
    sc_d = din("sc", [32, 2 * DFF])
    w_in_d = din("w_in", [1024, 4096])
    w_out_d = din("w_out", [1024, 1024])
    w_fi_d = din("w_fi", [1024, 2 * DFF])
    w_fo_d = din("w_fo", [DFF, 1024])
    small_d = din("small", [128, 64])
    convw_d = din("convw", [128, 4 * 44])
    wnf_d = din("wnf", [128, 1024])
    rope_d = din("rope", [NT, 128, 1024])
    cbf_d = din("cbf", [128, 128 * 3], BF16)
    smt_d = din("smt", [128, 2048], BF16)
    smb_d = din("smb", [128, 2048], BF16)
    idf_d = din("idf", [128, 128])

    y_d = dout("y", [NT * 128, 1024])
    nhp_d = dout("nhp", [4, 128, 128])
    nrp_d = dout("nrp", [4, 128, 128])
    ncp_d = dout("ncp", [2, 2 * DFF])
    nhs_d = dout("nhs", [16, 4, 128, 128])
    nrs_d = dout("nrs", [16, 4, 128, 128])
    ncs_d = dout("ncs", [32, 2 * DFF])

    mem = Mem(nc)
    P = Prog(nc)

    XS = 9
    X = mem.alloc("X", [128, XS, 1024], F32)
    bX = [Buf("X%d" % i) for i in range(XS)]
    small = mem.alloc("small", [128, 64], F32); bsmall = Buf("small")
    convw = mem.alloc("convw", [128, 176], F32); bconvw = Buf("convw")
    cbf = mem.alloc("cbf", [128, 384], BF16); bcbf = Buf("cbf")
    idf = mem.alloc("idf", [128, 128], F32); bidf = Buf("idf")
    cst = mem.alloc("cst", [128, 32], F32); bcst = Buf("cst")
    S = mem.alloc("S", [128, 8, 128], F32)
    Sb = mem.alloc("Sb", [128, 8, 128], BF16)
    bS = [Buf("S%d" % h) for h in range(8)]
    bSb = [Buf("Sb%d" % h) for h in range(8)]
    carry = mem.alloc("carry", [128, 2, 44, 2], F32)
    bcarry = [[Buf("cy%d_%d" % (q, i)) for i in range(44)] for q in range(2)]
    cpar = [0] * 44
    scrA = mem.alloc("scrA", [128, 2], F32)
    scrD = mem.alloc("scrD", [128, 2], F32)
    scrP = mem.alloc("scrP", [128, 2], F32)
    zeros = mem.alloc("zeros", [128, 128], F32); bzeros = Buf("zeros")
    ra_off = (mem.off + 63) // 64 * 64
    Win = nc.alloc_sbuf_tensor_at("RA_win", [128, 8, 4096], BF16, offset=ra_off)
    Wfi = nc.alloc_sbuf_tensor_at("RA_wfi", [128, 8, 2816], BF16, offset=ra_off)
    mem.off = ra_off + 65536
    mem.peak = max(mem.peak, mem.off)
    bRA = Buf("RA")
    bWin = bRA
    bWfi = bRA
    bWinHi = Buf("WinHi")
    pref = {"win": False, "wfi": False}

    def load_win(k0=0, k1=8, buf=None):
        buf = buf or bRA
        for (ca, cb_) in [(0, 1024), (2048, 3072), (1024, 2048), (3072, 4096)]:
            o_ = P.dma("pool", (lambda ca, cb_: lambda e: e.dma_start(
                out=Win[:, k0:k1, ca:cb_], in_=w_in_d[k0 * 128:k1 * 128, ca:cb_].rearrange("(k p) n -> p k n", p=128),
                max_dma_last_dim=8192))(ca, cb_), [], [buf], buf)
            P.recent_dmas.remove(o_)

    def load_wfi(half):
        for k in range(8):
            for part in (1, 0):
                c0_ = part * DFF + half * 1408
                o_ = P.dma("pool", (lambda k, part, c0_: lambda e: e.dma_start(
                    out=Wfi[:, k, part * 1408:(part + 1) * 1408], in_=w_fi_d[k * 128:(k + 1) * 128, c0_:c0_ + 1408],
                    max_dma_last_dim=8192))(k, part, c0_), [], [bRA], bRA)
                P.recent_dmas.remove(o_)

    identb = cbf[:, 0:128]
    maskP = cbf[:, 128:256]
    maskS = cbf[:, 256:384]
    wn1 = small[:, 0:8]
    wn2 = small[:, 8:16]
    nwa = small[:, 16:17]
    nwb = small[:, 17:18]
    lb0 = small[:, 18:22]
    lb1 = small[:, 22:26]
    c0 = cst[:, 0:4]
    c1 = cst[:, 4:8]
    nc1 = cst[:, 8:12]
    mhalf = cst[:, 12:13]

    PS = nc.alloc_psum_tensor("ps", [128, 4096], F32)
    PSb16 = PS[:].bitcast(BF16)
    bB = [Buf("B%d" % i, excl=True) for i in range(8)]

    def bank(i):
        return PS[:, i * 512:(i + 1) * 512]

    def bankb(i):
        return PSb16[:, i * 1024:(i + 1) * 1024]

    def mm(out, lhsT, rhs, start, stop, reads, writes):
        P.op("pe", lambda e: e.matmul(out, lhsT=lhsT, rhs=rhs, start=start, stop=stop), reads, writes)

    def tr(out, in_, ident, reads, writes):
        P.op("pe", lambda e: e.transpose(out=out, in_=in_, identity=ident), reads, writes)

    def act(out, in_, func, reads, writes, scale=None, bias=None, accum=None):
        kw = {}
        if scale is not None:
            kw["scale"] = scale
        if bias is not None:
            kw["bias"] = bias
        if accum is not None:
            kw["accum_out"] = accum
        P.op("act", lambda e: e.activation(out=out, in_=in_, func=func, **kw), reads, writes)

    def ts(eng, out, in0, s1, s2, op0, op1, reads, writes):
        if op1 is None:
            P.op(eng, lambda e: e.tensor_scalar(out=out, in0=in0, scalar1=s1, scalar2=None, op0=op0), reads, writes)
        else:
            P.op(eng, lambda e: e.tensor_scalar(out=out, in0=in0, scalar1=s1, scalar2=s2, op0=op0, op1=op1), reads, writes)

    def tt(eng, out, in0, in1, op, reads, writes):
        P.op(eng, lambda e: e.tensor_tensor(out=out, in0=in0, in1=in1, op=op), reads, writes)

    def stt(out, in0, scalar, in1, op0, op1, reads, writes):
        P.op("dve", lambda e: e.scalar_tensor_tensor(out=out, in0=in0, scalar=scalar, in1=in1, op0=op0, op1=op1), reads, writes)

    def cp(eng, out, in_, reads, writes):
        if eng == "act":
            P.op("act", lambda e: e.activation(out=out, in_=in_, func=AF.Copy), reads, writes)
        else:
            P.op(eng, lambda e: e.tensor_copy(out=out, in_=in_), reads, writes)

    def dma(out, in_, reads, writes, key, eng="sp", slow=False):
        if slow:
            P.dma(eng, lambda e: e.dma_start(out=out, in_=in_, allow_slow_non_contiguous=True), reads, writes, key)
        else:
            P.dma(eng, lambda e: e.dma_start(out=out, in_=in_), reads, writes, key)

    def dma_cast(out, in_, writes, key):
        P.dma("pool", lambda e: e.dma_start(out=out, in_=in_, max_dma_last_dim=8192), [], writes, key)

    def barrier():
        P.barrier({
            "act": lambda e: e.activation(out=scrA[:, 0:1], in_=scrA[:, 1:2], func=AF.Copy),
            "dve": lambda e: e.tensor_copy(out=scrD[:, 0:1], in_=scrD[:, 1:2]),
            "pool": lambda e: e.memset(scrP[:, 0:1], 0.0),
        })

    dma(small[:], small_d, [], [bsmall], bsmall)
    dma(convw[:], convw_d, [], [bconvw], bconvw)
    dma(cbf[:], cbf_d, [], [bcbf], bcbf)
    dma(idf[:], idf_d, [], [bidf], bidf)
    P.op("pool", lambda e: e.memset(zeros[:], 0.0), [], [bzeros])
    P.op("pool", lambda e: e.memset(scrA[:], 0.0), [], [])
    P.op("pool", lambda e: e.memset(scrD[:], 0.0), [], [])
    P.op("pool", lambda e: e.memset(scrP[:], 0.0), [], [])
    P.op("pool", lambda e: e.memset(cst[:], 0.0), [], [bcst])
    P.op("pool", lambda e: e.memset(cst[:, 12:13], -0.5), [], [bcst])
    for h in range(8):
        P.op("pool", (lambda hh: lambda e: e.memset(S[:, hh, :], 0.0))(h), [], [bS[h]])
        P.op("pool", (lambda hh: lambda e: e.memset(Sb[:, hh, :], 0.0))(h), [], [bSb[h]])
    P.op("pool", lambda e: e.memset(carry[:].rearrange("p a b c -> p (a b c)"), 0.0), [], [b for q in range(2) for b in bcarry[q]])
    tt("dve", cst[:, 24:28], lb0, lb1, ALU.subtract, [bsmall, bcst], [bcst])
    act(cst[:, 28:32], cst[:, 24:28], AF.Tanh, [bcst], [bcst], scale=0.5)
    ts("dve", c0, cst[:, 28:32], 0.25, 0.75, ALU.mult, ALU.add, [bcst], [bcst])
    ts("dve", c1, cst[:, 28:32], -0.25, 0.25, ALU.mult, ALU.add, [bcst], [bcst])
    ts("dve", nc1, cst[:, 28:32], 0.25, -0.25, ALU.mult, ALU.add, [bcst], [bcst])

    m0 = mem.mark()
    cast_rr = [0]

    def cast_scaled(out, in_, scal, reads, writes):
        i = cast_rr[0]
        cast_rr[0] += 1
        eng = ("dve", "pool", "act")[i % 3]
        if scal is None:
            cp(eng, out, in_, reads, writes)
        elif eng == "act":
            act(out, in_, AF.Identity, reads, writes, scale=scal)
        else:
            ts(eng, out, in_, scal, None, ALU.mult, None, reads, writes)

    def phase1(tiles):
        ntl = len(tiles)
        has_samp = any(gi == 16 for (_, gi) in tiles)
        Wout = mem.alloc("Wout", [128, 8, 1024], BF16); bWout = Buf("Wout")
        if not pref["win"]:
            load_win()
        pref["win"] = False
        dma_cast(Wout[:], w_out_d.rearrange("(k p) n -> p k n", p=128), [bWout], bWout)
        ck(1)
        rope1 = mem.alloc("rope", [128, 4, 256], F32); rope = [rope1, rope1]; brope1 = Buf("rope"); brope = [brope1, brope1]
        xn = mem.alloc("xn", [128, 1024], BF16); bxn = Buf("xn")
        hT = mem.alloc("hT", [128, 8, 128], BF16); bhT = Buf("hT")
        st = mem.alloc("st", [128, 16], F32); bst = Buf("st")
        th = mem.alloc("th", [128, 4, 128], F32); bth = Buf("th")
        ff = mem.alloc("ff", [128, 128], F32); bff = Buf("ff")
        kk = mem.alloc("kk", [128, 128], F32); bkk = Buf("kk")
        RR = mem.alloc("RR", [128, 128], F32); bRR = Buf("RR")
        qtok = mem.alloc("qtok", [128, 4, 128], BF16); bqtok = Buf("qtok")
        khT = mem.alloc("khT", [128, 4, 128], BF16); bkhT = [Buf("khT%d" % h) for h in range(4)]
        vvh = [mem.alloc("vvh", [128, 4, 128], BF16) for _ in range(2)]; bvvh = [Buf("vvh0"), Buf("vvh1")]
        rt = [mem.alloc("rt", [128, 256], BF16) for _ in range(8)]; brt = [Buf("rt%d" % i) for i in range(8)]
        scm = mem.alloc("scm", [128, 8, 128], BF16); bscm = [Buf("scm%d" % h) for h in range(8)]
        og = mem.alloc("og", [128, 1024], BF16); bog = Buf("og")
        on = og; bon = bog
        ogT = mem.alloc("ogT", [128, 8, 128], BF16); bogT = Buf("ogT")
        Pc = [mem.alloc("Pc", [128, 4, 128], F32) for _ in range(2)]
        bPc = [[Buf("Pc%d_%d" % (p, h)) for h in range(4)] for p in range(2)]
        qT = [mem.alloc("qT", [128, 8, 128], BF16) for _ in range(2)]
        bqT = [[Buf("qT") for h in range(8)] for p in range(2)]
        kT = [mem.alloc("kT", [128, 8, 128], BF16) for _ in range(2)]
        bkT = [[Buf("kT") for h in range(8)] for p in range(2)]
        ktok = [mem.alloc("ktok", [128, 8, 128], BF16) for _ in range(2)]
        bktok = [[Buf("ktok") for h in range(8)] for p in range(2)]
        vv = [mem.alloc("vv", [128, 8, 128], BF16) for _ in range(2)]
        bvv = [[Buf("vva"), Buf("vvb")] for p in range(2)]
        sg = [mem.alloc("sg", [128, 8, 128], BF16) for _ in range(2)]
        bsg = [[Buf("sga"), Buf("sgb")] for p in range(2)]
        if has_samp:
            smt = mem.alloc("smt", [128, 16, 128], BF16); bsmt = Buf("smt")
            smb = mem.alloc("smb", [128, 16, 128], BF16); bsmb = Buf("smb")
            dma(smt[:].rearrange("p j v -> p (j v)"), smt_d, [], [bsmt], bsmt)
            dma(smb[:].rearrange("p j v -> p (j v)"), smb_d, [], [bsmb], bsmb)
            Sin = [mem.alloc("Sin", [128, 4, 128], F32) for _ in range(4)]; bSin = [Buf("Sin%d" % i) for i in range(4)]
            Sinb = [mem.alloc("Sinb", [128, 4, 128], BF16) for _ in range(4)]; bSinb = [Buf("Sinb%d" % i) for i in range(4)]
            vblk = [mem.alloc("vblk", [128, 4, 128], BF16) for _ in range(4)]; bvblk = [Buf("vblk%d" % i) for i in range(4)]
            qblk = [mem.alloc("qblk", [128, 4, 128], BF16) for _ in range(4)]; bqblk = [Buf("qblk%d" % i) for i in range(4)]
        wn1b = wn1.rearrange("p (k o) -> p k o", o=1).broadcast_to([128, 8, 128])
        has_first = any(gi == 0 for (_, gi) in tiles)
        if has_first:
            q32 = mem.alloc("q32", [128, 8, 128], F32); bq32 = [Buf("q32_%d" % h) for h in range(8)]
            k32 = mem.alloc("k32", [128, 8, 128], F32); bk32 = [Buf("k32_%d" % h) for h in range(8)]
            q32tok = mem.alloc("q32tok", [128, 4, 128], F32); bq32tok = Buf("q32tok")
            k32tok = mem.alloc("k32tok", [128, 4, 128], F32); bk32tok = Buf("k32tok")
            rtf = [mem.alloc("rtf", [128, 256], F32) for _ in range(4)]; brtf = [Buf("rtf%d" % i) for i in range(4)]

        def front(idx):
            slot, gi = tiles[idx]
            p = idx % 2
            samp = (gi == 16)
            Xj = X[:, slot, :]
            rp = rope[p]

            def F1a():
                dma(Xj, x_d[gi * 128:(gi + 1) * 128, :], [], [bX[slot]], bX[slot])
                dma(rp[:].rearrange("p a c -> p (a c)"), rope_d[gi], [], [brope[p]], brope[p])
                act(xn[:], Xj, AF.Square, [bX[slot]], [bxn, bst], accum=st[:, 0:1])
                ts("dve", st[:, 1:2], st[:, 0:1], 1.0 / 1024, EPS, ALU.mult, ALU.add, [bst], [bst])
                tt("pool", st[:, 2:3], st[:, 1:2], mhalf, ALU.pow, [bst, bcst], [bst])
                act(xn[:], Xj, AF.Identity, [bX[slot], bst], [bxn], scale=st[:, 2:3])

            def F1b():
                for k in range(8):
                    tr(bankb(0)[:, k * 128:(k + 1) * 128], xn[:, k * 128:(k + 1) * 128], identb, [bxn, bcbf], [bB[0]])
                tt("dve", hT[:], bankb(0)[:, 0:1024].rearrange("p (k t) -> p k t", k=8), wn1b, ALU.mult, [bB[0], bsmall], [bhT])

            def F2():
                for c in (4, 5, 6, 7, 0, 1, 2, 3):
                    bk_ = 1 if c < 4 else 2
                    hh = c % 4
                    for k in range(8):
                        mm(bank(bk_)[:, hh * 128:(hh + 1) * 128], Win[:, k, c * 128:(c + 1) * 128], hT[:, k, :],
                           k == 0, k == 7, [bWin, bWinHi, bhT], [bB[bk_]])
                act(th[:].rearrange("p h t -> p (h t)"), bank(2)[:, 0:512], AF.Tanh, [bB[2]], [bth], scale=0.5)
                nseg = 16 if samp else 1
                seglen = 128 // nseg
                for h in range(4):
                    qa = bank(1)[:, h * 128:(h + 1) * 128]
                    act(ff[:], th[:, h, :], AF.Identity, [bth, bcst], [bff], scale=c1[:, h:h + 1], bias=c0[:, h:h + 1])
                    act(kk[:], th[:, h, :], AF.Identity, [bth, bcst], [bkk], scale=nc1[:, h:h + 1], bias=c1[:, h:h + 1])
                    for sgi in range(nseg):
                        a, b_ = sgi * seglen, (sgi + 1) * seglen
                        P.op("dve", (lambda a, b_, h: lambda e: e.tensor_tensor_scan(
                            out=Pc[p][:, h, a:b_], data0=ff[:, a:b_], data1=zeros[:, a:b_], initial=1.0,
                            op0=ALU.mult, op1=ALU.add))(a, b_, h), [bff, bzeros], [bPc[p][h]])
                    P.op("dve", (lambda h: lambda e: e.reciprocal(out=RR[:], in_=Pc[p][:, h, :]))(h), [bPc[p][h]], [bRR])
                    tt("dve", qT[p][:, h, :], qa, Pc[p][:, h, :], ALU.mult, [bB[1], bPc[p][h]], [bqT[p][h]])
                    tt("dve", kT[p][:, h, :], kk[:], RR[:], ALU.mult, [bkk, bRR], [bkT[p][h]])
                    if gi == 0:
                        tt("dve", q32[:, h, :], qa, Pc[p][:, h, :], ALU.mult, [bB[1], bPc[p][h]], [bq32[h]])
                        tt("dve", k32[:, h, :], kk[:], RR[:], ALU.mult, [bkk, bRR], [bk32[h]])
                    if not samp:
                        ts("dve", khT[:, h, :], kT[p][:, h, :], Pc[p][:, h, 127:128], None, ALU.mult, None, [bkT[p][h], bPc[p][h]], [bkhT[h]])

            def F3():
                for (c0_, bk_) in [(2048, 3), (2560, 0)]:
                    for k in range(8):
                        mm(bank(bk_)[:, 0:512], hT[:, k, :], Win[:, k, c0_:c0_ + 512], k == 0, k == 7, [bWin, bWinHi, bhT], [bB[bk_]])
                for h in range(4):
                    if samp:
                        tr(bankb(2)[:, h * 128:(h + 1) * 128], kT[p][:, h, :], identb, [bkT[p][h], bcbf], [bB[2]])
                    else:
                        tr(bankb(2)[:, h * 128:(h + 1) * 128], khT[:, h, :], identb, [bkhT[h], bcbf], [bB[2]])
                for h in range(4):
                    cp("act", ktok[p][:, h, :], bankb(2)[:, h * 128:(h + 1) * 128], [bB[2]], [bktok[p][h]])
                for (bk_, ci, si_, isq) in [(3, 0, 1, True), (0, 2, 3, False)]:
                    src = bank(bk_)[:, 0:512].rearrange("p (h d) -> p h d", h=4)
                    x1 = src[:, :, 0:64]
                    x2 = src[:, :, 64:128]
                    cs = rp[:, ci, :].rearrange("p (h d) -> p h d", h=4)
                    sn = rp[:, si_, :].rearrange("p (h d) -> p h d", h=4)
                    ro = 0 if isq else 4
                    tv = [rt[ro + i][:].rearrange("p (h d) -> p h d", h=4) for i in range(4)]
                    if isq:
                        d1, d2, bdst = qtok[:, :, 0:64], qtok[:, :, 64:128], [bqtok]
                    else:
                        d1, d2, bdst = ktok[p][:, 4:8, 0:64], ktok[p][:, 4:8, 64:128], bktok[p][4:8]
                    tt("dve", tv[0], x1, cs, ALU.mult, [bB[bk_], brope[p]], [brt[ro]])
                    tt("dve", tv[1], x2, sn, ALU.mult, [bB[bk_], brope[p]], [brt[ro + 1]])
                    tt("dve", tv[2], x1, sn, ALU.mult, [bB[bk_], brope[p]], [brt[ro + 2]])
                    tt("dve", tv[3], x2, cs, ALU.mult, [bB[bk_], brope[p]], [brt[ro + 3]])
                    tt("pool", d1, tv[0], tv[1], ALU.subtract, [brt[ro], brt[ro + 1]], bdst)
                    tt("pool", d2, tv[2], tv[3], ALU.add, [brt[ro + 2], brt[ro + 3]], bdst)
                    if gi == 0:
                        tf = [rtf[i][:].rearrange("p (h d) -> p h d", h=4) for i in range(4)]
                        dst32, bd32 = (q32tok, bq32tok) if isq else (k32tok, bk32tok)
                        tt("dve", tf[0], x1, cs, ALU.mult, [bB[bk_], brope[p]], [brtf[0]])
                        tt("dve", tf[1], x2, sn, ALU.mult, [bB[bk_], brope[p]], [brtf[1]])
                        tt("dve", tf[2], x1, sn, ALU.mult, [bB[bk_], brope[p]], [brtf[2]])
                        tt("dve", tf[3], x2, cs, ALU.mult, [bB[bk_], brope[p]], [brtf[3]])
                        tt("dve", dst32[:, :, 0:64], tf[0], tf[1], ALU.subtract, [brtf[0], brtf[1]], [bd32])
                        tt("dve", dst32[:, :, 64:128], tf[2], tf[3], ALU.add, [brtf[2], brtf[3]], [bd32])

            def F4():
                for (c0_, bk_) in [(1024, 1), (3072, 2)]:
                    for k in range(8):
                        mm(bank(bk_)[:, 0:512], hT[:, k, :], Win[:, k, c0_:c0_ + 512], k == 0, k == 7, [bWin, bWinHi, bhT], [bB[bk_]])
                cp("act", vv[p][:, 0:4, :].rearrange("p h d -> p (h d)"), bank(1)[:, 0:512], [bB[1]], [bvv[p][0]])
                cp("act", vv[p][:, 4:8, :].rearrange("p h d -> p (h d)"), bank(2)[:, 0:512], [bB[2]], [bvv[p][1]])
                if not samp:
                    for h in range(4):
                        act(vvh[p][:, h, :], bank(2)[:, h * 128:(h + 1) * 128], AF.Copy, [bB[2]], [bvvh[p]], scale=float(GAM[h] ** 128))
                for (c0_, bk_) in [(1536, 3), (3584, 0)]:
                    for k in range(8):
                        mm(bank(bk_)[:, 0:512], hT[:, k, :], Win[:, k, c0_:c0_ + 512], k == 0, k == 7, [bWin, bWinHi, bhT], [bB[bk_]])
                act(sg[p][:, 0:4, :].rearrange("p h d -> p (h d)"), bank(3)[:, 0:512], AF.Silu, [bB[3]], [bsg[p][0]])
                act(sg[p][:, 4:8, :].rearrange("p h d -> p (h d)"), bank(0)[:, 0:512], AF.Silu, [bB[0]], [bsg[p][1]])
                for h in range(4):
                    tr(bankb(1)[:, h * 128:(h + 1) * 128], qtok[:, h, :], identb, [bqtok, bcbf], [bB[1]])
                    tr(bankb(1)[:, (4 + h) * 128:(5 + h) * 128], ktok[p][:, 4 + h, :], identb, [bktok[p][4 + h], bcbf], [bB[1]])
                for h in range(4):
                    cp("act", qT[p][:, 4 + h, :], bankb(1)[:, h * 128:(h + 1) * 128], [bB[1]], [bqT[p][4 + h]])
                    cp("act", kT[p][:, 4 + h, :], bankb(1)[:, (4 + h) * 128:(5 + h) * 128], [bB[1]], [bkT[p][4 + h]])
                if gi == 0:
                    for h in range(4):
                        tr(bank(2)[:, h * 128:(h + 1) * 128], q32tok[:, h, :], idf[:], [bq32tok, bidf], [bB[2]])
                        tr(bank(3)[:, h * 128:(h + 1) * 128], k32tok[:, h, :], idf[:], [bk32tok, bidf], [bB[3]])
                    for h in range(4):
                        cp("dve", q32[:, 4 + h, :], bank(2)[:, h * 128:(h + 1) * 128], [bB[2]], [bq32[4 + h]])
                        cp("dve", k32[:, 4 + h, :], bank(3)[:, h * 128:(h + 1) * 128], [bB[3]], [bk32[4 + h]])

            return [F1a, F1b, F2, F3, F4]

        def back(idx):
            slot, gi = tiles[idx]
            p = idx % 2
            samp = (gi == 16)
            Xj = X[:, slot, :]
            mask = maskS if samp else maskP

            def K1():
                for h in range(8):
                    bk_ = 4 + h // 4
                    sc = bank(bk_)[:, (h % 4) * 128:(h % 4 + 1) * 128]
                    if gi == 0:
                        mm(sc, k32[:, h, :], q32[:, h, :], True, True, [bk32[h], bq32[h]], [bB[bk_]])
                    else:
                        mm(sc, kT[p][:, h, :], qT[p][:, h, :], True, True, [bkT[p][h], bqT[p][h]], [bB[bk_]])
                for h in range(8):
                    bk_ = 4 + h // 4
                    sc = bank(bk_)[:, (h % 4) * 128:(h % 4 + 1) * 128]
                    tt("dve", scm[:, h, :], sc, mask, ALU.mult, [bB[bk_], bcbf], [bscm[h]])

            def K2():
                if not samp:
                    for h in range(8):
                        bo = 6 + h // 4
                        ov = bank(bo)[:, (h % 4) * 128:(h % 4 + 1) * 128]
                        mm(ov, scm[:, h, :], vv[p][:, h, :], True, False, [bscm[h], bvv[p][h // 4]], [bB[bo]])
                        mm(ov, qT[p][:, h, :], Sb[:, h, :], False, True, [bqT[p][h], bSb[h]], [bB[bo]])
                    for h in range(8):
                        bu = 4 + h // 4
                        uv = bank(bu)[:, (h % 4) * 128:(h % 4 + 1) * 128]
                        if h < 4:
                            mm(uv, ktok[p][:, h, :], vv[p][:, h, :], True, True, [bktok[p][h], bvv[p][0]], [bB[bu]])
                        else:
                            mm(uv, ktok[p][:, h, :], vvh[p][:, h - 4, :], True, True, [bktok[p][h], bvvh[p]], [bB[bu]])
                    for h in range(8):
                        bu = 4 + h // 4
                        uv = bank(bu)[:, (h % 4) * 128:(h % 4 + 1) * 128]
                        E = Pc[p][:, h, 127:128] if h < 4 else float(GAM[h - 4] ** 128)
                        rd = [bS[h], bB[bu]] + ([bPc[p][h]] if h < 4 else [])
                        stt(S[:, h, :], S[:, h, :], E, uv, ALU.mult, ALU.add, rd, [bS[h]])
                        cp("pool", Sb[:, h, :], S[:, h, :], [bS[h]], [bSb[h]])
                else:
                    def src_of(i):
                        h, q = i // 4, i % 4
                        st_d = sh_d if h < 4 else sr_d
                        return st_d[4 * q:4 * q + 4, h % 4].rearrange("j d v -> d j v")

                    def load(i):
                        sl = i % 4
                        sv = src_of(i)
                        dma(Sin[sl][:], sv, [], [bSin[sl]], bSin[sl])

                    def cast(i):
                        sl = i % 4
                        cp("act", Sinb[sl][:].rearrange("p j v -> p (j v)"), Sin[sl][:].rearrange("p j v -> p (j v)"), [bSin[sl]], [bSinb[sl]])

                    def pre(i):
                        h, q = i // 4, i % 4
                        sl = i % 4
                        tt("dve", qblk[sl][:], qT[p][:, h:h + 1, :].broadcast_to([128, 4, 128]), smb[:, 4 * q:4 * q + 4, :], ALU.mult,
                           [bqT[p][h], bsmb], [bqblk[sl]])
                        tt("dve", vblk[sl][:], vv[p][:, h:h + 1, :].broadcast_to([128, 4, 128]), smt[:, 4 * q:4 * q + 4, :], ALU.mult,
                           [bvv[p][h // 4], bsmt], [bvblk[sl]])

                    load(0)
                    load(1)
                    cast(0)
                    pre(0)
                    for i in range(32):
                        h, q = i // 4, i % 4
                        sl = i % 4
                        if i + 2 < 32:
                            load(i + 2)
                        if i + 1 < 32:
                            cast(i + 1)
                            pre(i + 1)
                        bo = 6 + h // 4
                        ov = bank(bo)[:, (h % 4) * 128:(h % 4 + 1) * 128]
                        ns_d = nhs_d if h < 4 else nrs_d
                        if q == 0:
                            mm(ov, scm[:, h, :], vv[p][:, h, :], True, False, [bscm[h], bvv[p][h // 4]], [bB[bo]])
                        for j in range(4):
                            mm(ov, qblk[sl][:, j, :], Sinb[sl][:, j, :], False, (q == 3 and j == 3), [bqblk[sl], bSinb[sl]], [bB[bo]])
                        mm(bank(q)[:, 0:512], ktok[p][:, h, :], vblk[sl][:].rearrange("p j v -> p (j v)"), True, True,
                           [bktok[p][h], bvblk[sl]], [bB[q]])
                        Sf = Sin[sl][:].rearrange("p j v -> p (j v)")
                        tt("dve", Sf, bank(q)[:, 0:512], Sf, ALU.add, [bB[q], bSin[sl]], [bSin[sl]])
                        if h < 4:
                            Eb = Pc[p][:, h, :].rearrange("p (j t) -> p j t", t=8)[:, 4 * q:4 * q + 4, 7:8].broadcast_to([128, 4, 128])
                            tt("dve", Sin[sl][:], Sin[sl][:], Eb, ALU.mult, [bSin[sl], bPc[p][h]], [bSin[sl]])
                        else:
                            act(Sf, Sf, AF.Identity, [bSin[sl]], [bSin[sl]], scale=float(GAM[h - 4] ** 8))
                        dma(ns_d[4 * q:4 * q + 4, h % 4].rearrange("j d v -> d j v"), Sin[sl][:], [bSin[sl]], [], bSin[sl])

            def K3a():
                mv = mem_mv
                o_all = PS[:, 6 * 512:8 * 512]
                act(sq[:], o_all, AF.Square, [bB[6], bB[7]], [bsq])
                P.op("dve", lambda e: e.tensor_reduce(out=mv[:, 0:8], in_=o_all.rearrange("p (h d) -> p h d", h=8),
                                                       axis=mybir.AxisListType.X, op=ALU.add), [bB[6], bB[7]], [bmv])
                P.op("dve", lambda e: e.tensor_reduce(out=mv[:, 8:16], in_=sq[:].rearrange("p (h d) -> p h d", h=8),
                                                       axis=mybir.AxisListType.X, op=ALU.add), [bsq], [bmv])
            def K3a1():
                mv = mem_mv
                o_all = PS[:, 6 * 512:8 * 512]
                ts("dve", mv[:, 16:24], mv[:, 0:8], 1.0 / 128, None, ALU.mult, None, [bmv], [bmv])
                ts("dve", mv[:, 24:32], mv[:, 8:16], 1.0 / 128, EPS, ALU.mult, ALU.add, [bmv], [bmv])
                tt("dve", mv[:, 32:36], mv[:, 20:24], mv[:, 20:24], ALU.mult, [bmv], [bmv])
                tt("dve", mv[:, 28:32], mv[:, 28:32], mv[:, 32:36], ALU.subtract, [bmv], [bmv])
                tt("pool", mv[:, 36:44], mv[:, 24:32], mhalf.broadcast_to([128, 8]), ALU.pow, [bmv, bcst], [bmv])
                stt(mv[:, 48:52], mv[:, 20:24], -1.0, mv[:, 40:44], ALU.mult, ALU.mult, [bmv], [bmv])
                og3 = og[:].rearrange("p (h d) -> p h d", h=8)
                rs_b = mv[:, 36:44].rearrange("p (h o) -> p h o", o=1).broadcast_to([128, 8, 128])
                nm_b = mv[:, 48:52].rearrange("p (h o) -> p h o", o=1).broadcast_to([128, 4, 128])
                tt("dve", og3, o_all.rearrange("p (h d) -> p h d", h=8), rs_b, ALU.mult, [bB[6], bB[7], bmv], [bog])
                tt("dve", og3[:, 4:8, :], og3[:, 4:8, :], nm_b, ALU.add, [bog, bmv], [bog])
                tt("dve", og[:, 0:512], og[:, 0:512], sg[p][:, 0:4, :].rearrange("p h d -> p (h d)"), ALU.mult, [bog, bsg[p][0]], [bog])
                tt("dve", og[:, 512:1024], og[:, 512:1024], sg[p][:, 4:8, :].rearrange("p h d -> p (h d)"), ALU.mult, [bog, bsg[p][1]], [bog])

            def K3b():
                for k in range(8):
                    tr(bankb(7)[:, k * 128:(k + 1) * 128], og[:, k * 128:(k + 1) * 128], identb, [bog, bcbf], [bB[7]])
                act(ogT[:, 0:4, :].rearrange("p k t -> p (k t)"), bankb(7)[:, 0:512], AF.Identity, [bB[7], bsmall], [bogT], scale=nwa)
                act(ogT[:, 4:8, :].rearrange("p k t -> p (k t)"), bankb(7)[:, 512:1024], AF.Identity, [bB[7], bsmall], [bogT], scale=nwb)

            def K4():
                for hf in range(2):
                    for k in range(8):
                        mm(bank(4 + hf)[:, 0:512], ogT[:, k, :], Wout[:, k, hf * 512:(hf + 1) * 512], k == 0, k == 7,
                           [bogT, bWout], [bB[4 + hf]])
                for hf in range(2):
                    tt("dve", Xj[:, hf * 512:(hf + 1) * 512], bank(4 + hf)[:, 0:512], Xj[:, hf * 512:(hf + 1) * 512], ALU.add,
                       [bB[4 + hf], bX[slot]], [bX[slot]])
                if gi == 15:
                    for h in range(8):
                        dst = nhp_d[h] if h < 4 else nrp_d[h - 4]
                        dma(dst, S[:, h, :], [bS[h]], [], bS[h])

            return [K1, K2, K3a, K3b, K4, K3a1]

        sq = mem.alloc("sq", [128, 1024], F32); bsq = Buf("sq")
        P.op("pool", lambda e: e.memset(mem_mv[:, 44:48], 0.0), [], [bmv])
        fronts = [front(i) for i in range(ntl)]
        backs = [back(i) for i in range(ntl)]
        for c in fronts[0]:
            c()
        if ntl > 1:
            fronts[1][0]()
        ck(2)
        for idx in range(ntl):
            K1, K2, K3a, K3b, K4, _K3a1 = backs[idx]
            nf = fronts[idx + 1] if idx + 1 < ntl else None
            if nf:
                nf[1]()
            K1()
            if nf:
                nf[2]()
            K2()
            if nf:
                nf[3]()
            K3a()
            if idx + 2 < ntl:
                fronts[idx + 2][0]()
            backs[idx][5]()
            if nf:
                nf[4]()
            if idx > 0:
                backs[idx - 1][4]()
            K3b()
            if idx == ntl - 1:
                K4()
            if idx == ntl - 2 or ntl == 1:
                load_wfi(0)
                pref["wfi"] = True

    def phase2(tiles):
        nt = len(tiles)
        h2T = mem.alloc("h2T", [128, 8, nt * 128], BF16)
        bh2 = [Buf("h2T%d" % i) for i in range(nt)]
        Wfo = mem.alloc("Wfo", [128, 11, 1024], BF16); bWfo = Buf("Wfo")
        wnf = mem.alloc("wnf", [128, 1024], F32); bwnf = Buf("wnf")
        dma(wnf[:], wnf_d, [], [bwnf], bwnf)
        wn2b = wn2.rearrange("p (k o) -> p k o", o=1).broadcast_to([128, 8, 128])
        xn2 = [mem.alloc("xn2", [128, 1024], BF16) for _ in range(2)]; bxn2 = [Buf("xn2a"), Buf("xn2b")]
        st2 = [mem.alloc("st2", [128, 16], F32) for _ in range(2)]; bst2 = [Buf("st2a"), Buf("st2b")]
        xn, bxn, st, bst = xn2[0], bxn2[0], st2[0], bst2[0]
        hl = mem.alloc("hl", [128, 4, 4], F32); bhl = [Buf("hl%d" % i) for i in range(4)]
        ext = [mem.alloc("ext", [128, 160], F32) for _ in range(2)]
        bext = [Buf("ext0"), Buf("ext1")]
        NCC = 4
        cc = [mem.alloc("cc", [128, 512], F32) for _ in range(NCC)]
        bcc = [Buf("cc%d" % i) for i in range(NCC)]
        sgl = [mem.alloc("sgl", [128, 512], BF16) for _ in range(2)]; bsgl = [Buf("sgl0"), Buf("sgl1")]
        actT = mem.alloc("actT", [128, 11, 512], BF16); bactT = [Buf("actT%d" % i) for i in range(11)]
        ncT = mem.alloc("ncT", [128, 22, 32], F32); bncT = Buf("ncT")
        scT = mem.alloc("scT", [128, 22, 32], F32); bscT = Buf("scT")
        sc32 = mem.alloc("sc32", [32, 2816], F32); bsc32 = Buf("sc32")
        yb = [mem.alloc("yb", [128, 1024], F32) for _ in range(2)]; byb = [Buf("yb0"), Buf("yb1")]
        has_samp = any(gi == 16 for (_, gi) in tiles)
        sts = []
        pt = [t for t in tiles if t[1] != 16]
        for i in range(0, len(pt), 4):
            sts.append(pt[i:i + 4])
        if has_samp:
            sts.append([t for t in tiles if t[1] == 16])
        slot_pos = {s_: i for i, (s_, _) in enumerate(tiles)}
        ycnt = [0]

        def prepA(slot, pi_):
            Xj = X[:, slot, :]
            xn, bxn, st, bst = xn2[pi_], bxn2[pi_], st2[pi_], bst2[pi_]
            act(xn[:], Xj, AF.Square, [bX[slot]], [bxn, bst], accum=st[:, 0:1])
            ts("dve", st[:, 1:2], st[:, 0:1], 1.0 / 1024, EPS, ALU.mult, ALU.add, [bst], [bst])
            tt("pool", st[:, 2:3], st[:, 1:2], mhalf, ALU.pow, [bst, bcst], [bst])
            act(xn[:], Xj, AF.Identity, [bX[slot], bst], [bxn], scale=st[:, 2:3])

        def prepB(slot, pi_):
            pp = slot_pos[slot]
            xn, bxn = xn2[pi_], bxn2[pi_]
            for k in range(8):
                tr(bankb(7)[:, k * 128:(k + 1) * 128], xn[:, k * 128:(k + 1) * 128], identb, [bxn, bcbf], [bB[7]])
            tt("dve", h2T[:, :, pp * 128:(pp + 1) * 128],
               bankb(7)[:, 0:1024].rearrange("p (k t) -> p k t", k=8), wn2b, ALU.mult, [bB[7], bsmall], [bh2[pp]])

        for half in range(2):
            if not pref["wfi"]:
                load_wfi(half)
            pref["wfi"] = False
            if half == 1 and not has_samp:
                load_win(6, 8, bWinHi)
                pref["win_hi"] = True
            dma_cast(Wfo[:], w_fo_d[half * 1408:(half + 1) * 1408, :].rearrange("(c p) n -> p c n", p=128), [bWfo], bWfo)
            if has_samp:
                for part in range(2):
                    c0_ = part * DFF + half * 1408
                    dma(sc32[:, part * 1408:(part + 1) * 1408], sc_d[:, c0_:c0_ + 1408], [], [bsc32], bsc32)
                for ft in range(22):
                    bk = 4 + (ft % 2)
                    tr(bank(bk)[:, 0:32], sc32[:, ft * 128:(ft + 1) * 128], idf[0:32, 0:32], [bsc32, bidf], [bB[bk]])
                    cp("dve", scT[:, ft, :], bank(bk)[:, 0:32], [bB[bk]], [bscT])
            if half == 0:
                fs = [s_ for (s_, _) in sts[0]]
                prepA(fs[0], 0)
                for j in range(len(fs)):
                    if j + 1 < len(fs):
                        prepA(fs[j + 1], (j + 1) % 2)
                    prepB(fs[j], j % 2)

            for sti, stl in enumerate(sts):
                samp = (stl[0][1] == 16)
                ntk = 128 * len(stl)
                p0 = slot_pos[stl[0][0]] * 128
                nxt = [s_ for (s_, _) in sts[sti + 1]] if (half == 0 and sti + 1 < len(sts)) else []
                rd_h2 = [bh2[slot_pos[s_]] for (s_, _) in stl]
                fi = 0
                tails = []
                for c in range(11):
                    if nxt and c % 2 == 0 and (c // 2) < len(nxt):
                        prepA(nxt[c // 2], (c // 2) % 2)
                    for part in (1, 0):
                        ft = part * 11 + c
                        gft = part * 22 + half * 11 + c
                        bk = fi % 4
                        ci = fi % NCC
                        fi += 1
                        up = bank(bk)[:, 0:ntk]
                        for k in range(8):
                            mm(up, Wfi[:, k, part * 1408 + c * 128: part * 1408 + (c + 1) * 128],
                               h2T[:, k, p0:p0 + ntk], k == 0, k == 7, [bWfi] + rd_h2, [bB[bk]])
                        cv = cc[ci]
                        w0 = convw[:, gft:gft + 1]
                        w1 = convw[:, 44 + gft:45 + gft]
                        w2 = convw[:, 88 + gft:89 + gft]
                        cb = convw[:, 132 + gft:133 + gft]
                        if not samp:
                            hi_ = fi % 4
                            cp_ = cpar[gft]
                            cold = carry[:, cp_, gft, :]
                            act(cv[:, 0:ntk], up, AF.Identity, [bB[bk], bconvw], [bcc[ci]], scale=w2, bias=cb)
                            cp("act", carry[:, 1 - cp_, gft, :], up[:, ntk - 2:ntk], [bB[bk]], [bcarry[1 - cp_][gft]])
                            stt(cv[:, 1:ntk], up[:, 0:ntk - 1], w1, cv[:, 1:ntk], ALU.mult, ALU.add, [bB[bk], bcc[ci], bconvw], [bcc[ci]])
                            stt(cv[:, 2:ntk], up[:, 0:ntk - 2], w0, cv[:, 2:ntk], ALU.mult, ALU.add, [bB[bk], bcc[ci], bconvw], [bcc[ci]])
                            ts("pool", hl[:, hi_, 0:2], cold, w0, None, ALU.mult, None, [bcarry[cp_][gft], bconvw], [bhl[hi_]])
                            ts("pool", hl[:, hi_, 2:3], cold[:, 1:2], w1, None, ALU.mult, None, [bcarry[cp_][gft], bconvw], [bhl[hi_]])
                            tt("pool", cv[:, 0:2], cv[:, 0:2], hl[:, hi_, 0:2], ALU.add, [bcc[ci], bhl[hi_]], [bcc[ci]])
                            tt("pool", cv[:, 0:1], cv[:, 0:1], hl[:, hi_, 2:3], ALU.add, [bcc[ci], bhl[hi_]], [bcc[ci]])
                            cpar[gft] = 1 - cp_
                        else:
                            e_i = fi % 2
                            ex = ext[e_i]
                            ex3 = ex[:, 0:160].rearrange("p (j t) -> p j t", t=10)
                            cv3 = cv[:, 0:128].rearrange("p (j t) -> p j t", t=8)
                            up3 = up.rearrange("p (j t) -> p j t", t=8)
                            cp("pool", ex3[:, :, 0:2], scT[:, ft, :].rearrange("p (j r) -> p j r", r=2), [bscT], [bext[e_i]])
                            cp("act", ex3[:, :, 2:10], up3, [bB[bk]], [bext[e_i]])
                            cp("pool", ncT[:, ft, :].rearrange("p (j r) -> p j r", r=2), ex3[:, :, 8:10], [bext[e_i]], [bncT])
                            act(cv[:, 0:128], up, AF.Identity, [bB[bk], bconvw], [bcc[ci]], scale=w2, bias=cb)
                            stt(cv3, ex3[:, :, 1:9], w1, cv3, ALU.mult, ALU.add, [bext[e_i], bcc[ci], bconvw], [bcc[ci]])
                            stt(cv3, ex3[:, :, 0:8], w0, cv3, ALU.mult, ALU.add, [bext[e_i], bcc[ci], bconvw], [bcc[ci]])
                        if part == 1:
                            tails.append((lambda cv, ci, c, ntk: lambda: act(sgl[c % 2][:, 0:ntk], cv[:, 0:ntk], AF.Silu, [bcc[ci]], [bsgl[c % 2]]))(cv, ci, c, ntk))
                        else:
                            tails.append((lambda cv, ci, c, ntk: lambda: tt("dve", actT[:, c, 0:ntk], cv[:, 0:ntk], sgl[c % 2][:, 0:ntk], ALU.mult, [bcc[ci], bsgl[c % 2]], [bactT[c]]))(cv, ci, c, ntk))
                        while len(tails) > 3:
                            tails.pop(0)()
                    if nxt and c in (1, 3, 5, 7) and (c // 2) < len(nxt):
                        prepB(nxt[c // 2], (c // 2) % 2)
                if sti == len(sts) - 1:
                    if half == 0:
                        load_wfi(1)
                        pref["wfi"] = True
                    elif not has_samp:
                        if pref.get("win_hi"):
                            load_win(0, 6)
                        else:
                            load_win()
                        pref["win"] = True
                while tails:
                    tails.pop(0)()
                for ti, (slot, gi) in enumerate(stl):
                    Xj = X[:, slot, :]
                    for hf in range(2):
                        bk = 4 + (2 * ti + hf) % 3
                        for c in range(11):
                            mm(bank(bk)[:, 0:512], actT[:, c, ti * 128:(ti + 1) * 128], Wfo[:, c, hf * 512:(hf + 1) * 512],
                               c == 0, c == 10, [bactT[c], bWfo], [bB[bk]])
                        tt("dve", Xj[:, hf * 512:(hf + 1) * 512], bank(bk)[:, 0:512], Xj[:, hf * 512:(hf + 1) * 512], ALU.add,
                           [bB[bk], bX[slot]], [bX[slot]])
                    if half == 1:
                        yi = ycnt[0] % 2
                        ycnt[0] += 1
                        act(xn[:], Xj, AF.Square, [bX[slot]], [bxn, bst], accum=st[:, 4:5])
                        ts("dve", st[:, 5:6], st[:, 4:5], 1.0 / 1024, EPS, ALU.mult, ALU.add, [bst], [bst])
                        tt("pool", st[:, 6:7], st[:, 5:6], mhalf, ALU.pow, [bst, bcst], [bst])
                        stt(yb[yi][:], Xj, st[:, 6:7], wnf[:], ALU.mult, ALU.mult, [bX[slot], bst, bwnf], [byb[yi]])
                        dma(y_d[gi * 128:(gi + 1) * 128, :], yb[yi][:], [byb[yi]], [], byb[yi])
                last_prompt = (not samp) and stl[-1][1] == 15
                if last_prompt or samp:
                    ncols = 32 if samp else 2
                    for part in range(2):
                        for c in range(11):
                            ft = part * 11 + c
                            gft = part * 22 + half * 11 + c
                            bk = 4 + (ft % 3)
                            if samp:
                                src, rdb = ncT[:, ft, :], [bncT]
                            else:
                                src, rdb = carry[:, cpar[gft], gft, :], [bcarry[cpar[gft]][gft]]
                            tr(bank(bk)[0:ncols, 0:128], src, idf[:], rdb + [bidf], [bB[bk]])
                            cp("act", sc32[0:ncols, ft * 128:(ft + 1) * 128], bank(bk)[0:ncols, 0:128], [bB[bk]], [bsc32])
                    outd = ncs_d if samp else ncp_d
                    for part in range(2):
                        c0_ = part * DFF + half * 1408
                        dma(outd[:, c0_:c0_ + 1408], sc32[0:ncols, part * 1408:(part + 1) * 1408], [bsc32], [], bsc32)

    mem_mv = mem.alloc("mv", [128, 64], F32); bmv = Buf("mv")
    m0 = mem.mark()
    groups = [
        [(i, i) for i in range(8)],
        [(i - 8, i) for i in range(8, 17)],
    ]
    if "groups" in dbg:
        groups = dbg["groups"]
    try:
        ck(0)
        for tiles in groups:
            mem.reset(m0)
            if not dbg.get("skip1"):
                phase1(tiles)
            barrier()
            mem.reset(m0)
            if not dbg.get("skip2"):
                phase2(tiles)
            barrier()
    except _Stop:
        pass
    P.emit()
    return nc, mem.peak


def _consts():
    bf = ml_dtypes.bfloat16
    s = np.arange(128)
    ident = np.eye(128, dtype=np.float32)
    maskP = (s[:, None] <= s[None, :]).astype(np.float32)
    maskS = ((s[:, None] <= s[None, :]) & ((s[:, None] // 8) == (s[None, :] // 8))).astype(np.float32)
    cbf = np.concatenate([ident, maskP, maskS], axis=1).astype(bf)
    j = np.arange(16)
    smt = np.broadcast_to(((s[:, None] // 8) == j[None, :])[:, :, None], (128, 16, 128)).astype(bf).reshape(128, 2048)
    smb = np.broadcast_to(((s[None, :] // 8) == j[:, None])[None, :, :], (128, 16, 128)).astype(bf).reshape(128, 2048)
    half = 64
    inv = (1.0 / (np.float32(10000.0) ** (np.arange(half, dtype=np.float32) / np.float32(half)))).astype(np.float32)
    rope = np.zeros((NT, 128, 4, 4, 64), dtype=np.float32)
    for gi in range(NT):
        if gi < 16:
            pos = (gi * 128 + s).astype(np.float32)
            tau = s
        else:
            pos = (PAST_LEN + (s % 8)).astype(np.float32)
            tau = s % 8
        ang = (pos[:, None] * inv[None, :]).astype(np.float32).astype(np.float64)
        cs, sn = np.cos(ang), np.sin(ang)
        for h in range(4):
            dq = GAM[h] ** (tau + 1.0)
            dk = GAM[h] ** (-(tau + 1.0)) * (128.0 ** -0.5)
            rope[gi, :, 0, h] = cs * dq[:, None]
            rope[gi, :, 1, h] = sn * dq[:, None]
            rope[gi, :, 2, h] = cs * dk[:, None]
            rope[gi, :, 3, h] = sn * dk[:, None]
    rope = rope.reshape(NT, 128, 1024)
    return cbf, smt, smb, rope, ident


_CACHE = {}


def kernel(x_prompt, x_sample, state_hgrn, state_ret, state_conv, w_norm1, w_in, hgrn_lb,
           hgrn_norm_w, ret_norm_w, w_out, w_norm2, w_ffn_in, conv_w, conv_b, w_ffn_out, w_norm_f):
    f32 = np.float32
    if "nc" not in _CACHE:
        _CACHE["nc"] = build_program()[0]
        _CACHE["consts"] = _consts()
    nc = _CACHE["nc"]
    cbf, smt, smb, rope, ident = _CACHE["consts"]

    small = np.zeros((128, 64), dtype=f32)
    small[:, 0:8] = np.asarray(w_norm1[0], f32).reshape(8, 128).T
    small[:, 8:16] = np.asarray(w_norm2[0], f32).reshape(8, 128).T
    small[:, 16] = np.asarray(hgrn_norm_w[0], f32)
    small[:, 17] = np.asarray(ret_norm_w[0], f32)
    small[:, 18:22] = np.asarray(hgrn_lb[0], f32).reshape(4, 128).T
    small[:, 22:26] = np.asarray(hgrn_lb[1], f32).reshape(4, 128).T
    convw = np.zeros((128, 176), dtype=f32)
    for jj in range(3):
        convw[:, jj * 44:(jj + 1) * 44] = np.asarray(conv_w[0, jj], f32).reshape(44, 128).T
    convw[:, 132:176] = np.asarray(conv_b[0], f32).reshape(44, 128).T
    wnf = np.ascontiguousarray(np.broadcast_to(np.asarray(w_norm_f, f32)[None, :], (128, 1024)))

    shared = {
        "w_in": np.ascontiguousarray(w_in[0], dtype=f32), "w_out": np.ascontiguousarray(w_out[0], dtype=f32),
        "w_fi": np.ascontiguousarray(w_ffn_in[0], dtype=f32), "w_fo": np.ascontiguousarray(w_ffn_out[0], dtype=f32),
        "small": small, "convw": convw, "wnf": wnf, "rope": rope, "cbf": cbf, "smt": smt, "smb": smb, "idf": ident,
    }
    in_maps = []
    for c in range(8):
        xs = np.concatenate([np.asarray(x_prompt[c], f32), np.asarray(x_sample[16 * c:16 * c + 16], f32).reshape(128, 1024)], axis=0)
        m = dict(shared)
        m["x"] = np.ascontiguousarray(xs)
        m["sh"] = np.ascontiguousarray(state_hgrn[0, 16 * c:16 * c + 16], dtype=f32)
        m["sr"] = np.ascontiguousarray(state_ret[0, 16 * c:16 * c + 16], dtype=f32)
        m["sc"] = np.ascontiguousarray(np.asarray(state_conv[0, 16 * c:16 * c + 16], f32).reshape(32, 2 * DFF))
        in_maps.append(m)
    res = run_bass_kernel_spmd(nc, in_maps, core_ids=list(range(8)))
    R = res.results
    y_prompt = np.stack([R[c]["y"][:2048] for c in range(8)], axis=0)
    y_sample = np.concatenate([R[c]["y"][2048:].reshape(16, 8, 1024) for c in range(8)], axis=0)
    ha_p = np.stack([R[c]["nhp"] for c in range(8)], axis=0)[None]
    rb_p = np.stack([R[c]["nrp"] for c in range(8)], axis=0)[None]
    cv_p = np.stack([R[c]["ncp"] for c in range(8)], axis=0)[None]
    ha_s = np.concatenate([R[c]["nhs"] for c in range(8)], axis=0)[None]
    rb_s = np.concatenate([R[c]["nrs"] for c in range(8)], axis=0)[None]
    cv_s = np.concatenate([R[c]["ncs"].reshape(16, 2, 2 * DFF) for c in range(8)], axis=0)[None]
    return (y_prompt.astype(f32), y_sample.astype(f32), ha_p.astype(f32), rb_p.astype(f32), cv_p.astype(f32),
            ha_s.astype(f32), rb_s.astype(f32), cv_s.astype(f32))
```

```python
import contextlib
import numpy as np
import ml_dtypes
import concourse.bass as bass
import concourse.mybir as mybir
from concourse.bass_utils import run_bass_kernel_spmd

F32 = mybir.dt.float32
BF16 = mybir.dt.bfloat16
AF = mybir.ActivationFunctionType
ALU = mybir.AluOpType

ENGS = ["pe", "act", "dve", "pool", "sp"]
SKIP_SAME_ENGINE_WAW = False
EPS = 1e-6
NT = 17
DFF = 2816
NFT = 22
PAST_LEN = 16384


class Buf:
    __slots__ = ("name", "w", "r", "sem", "semcnt", "excl")

    def __init__(self, name="", excl=False):
        self.name = name
        self.excl = excl
        self.w = None
        self.r = []
        self.sem = None
        self.semcnt = 0


class Op:
    __slots__ = ("eng", "fn", "deps", "dma", "inc", "incval", "dmasem", "dmaval", "dmadeps")

    def __init__(self, eng, fn, dma):
        self.eng = eng
        self.fn = fn
        self.deps = []
        self.dmadeps = {}
        self.dma = dma
        self.inc = False
        self.incval = 0
        self.dmasem = None
        self.dmaval = 0


class Prog:
    def __init__(self, nc):
        self.nc = nc
        self.ops = {e: [] for e in ENGS}
        self.dma_bufs = []
        self.all_ops = []
        self.pending = {e: [] for e in ENGS}
        self.recent_dmas = []

    def _add(self, eng, fn, reads, writes, dma=False, key=None):
        op = Op(eng, fn, dma)
        deps = list(self.pending[eng])
        self.pending[eng] = []
        for b in reads:
            if b.w is not None:
                deps.append(b.w)
            if b.excl:
                deps.extend(r for r in b.r if r.eng != eng)
        for b in writes:
            if b.w is not None and not (dma and b.w.dma) and (not SKIP_SAME_ENGINE_WAW or b.w.eng != eng or b.w.dma or dma):
                deps.append(b.w)
            deps.extend(r for r in b.r if (not SKIP_SAME_ENGINE_WAW or r.eng != eng or r.dma or dma))
        seen = set()
        for d in deps:
            if id(d) in seen:
                continue
            seen.add(id(d))
            if d.dma:
                kb = d.dmasem
                op.dmadeps[id(kb)] = (kb, kb.semcnt)
            else:
                op.deps.append(d)
        for b in reads:
            b.r.append(op)
        for b in writes:
            b.w = op
            b.r = []
        if dma:
            if key.sem is None:
                key.sem = len(self.dma_bufs)
                self.dma_bufs.append(key)
            key.semcnt += 16
            op.dmasem = key
            op.dmaval = key.semcnt
            self.recent_dmas.append(op)
        self.ops[eng].append(op)
        self.all_ops.append(op)
        return op

    def op(self, eng, fn, reads=(), writes=()):
        return self._add(eng, fn, list(reads), list(writes))

    def dma(self, eng, fn, reads=(), writes=(), key=None):
        return self._add(eng, fn, list(reads), list(writes), dma=True, key=key)

    def barrier(self, markers):
        mops = []
        for e, fn in markers.items():
            mops.append(self.op(e, fn, writes=[Buf("bar")]))
        for e in ENGS:
            self.pending[e] = self.pending[e] + mops + self.recent_dmas
        self.recent_dmas = []

    def emit(self):
        nc = self.nc
        for op in self.all_ops:
            for d in op.deps:
                if d.eng == op.eng and op.eng in ("pe", "sp") and not op.dma:
                    continue
                d.inc = True
        for e in ENGS:
            c = 0
            for op in self.ops[e]:
                if op.inc and not op.dma:
                    c += 1
                    op.incval = c
        with contextlib.ExitStack() as st:
            esem = {e: st.enter_context(nc.semaphore("s_" + e)) for e in ENGS}
            dsem = [st.enter_context(nc.semaphore("d%d" % i)) for i in range(len(self.dma_bufs))]
            block = st.enter_context(nc.Block())
            engobj = {"pe": "tensor", "act": "scalar", "dve": "vector", "pool": "gpsimd", "sp": "sync"}
            ops = self.ops
            dma_bufs = self.dma_bufs

            def make(e):
                def body(eng):
                    waited = {}
                    for op in ops[e]:
                        need = {}
                        for (kb, v) in op.dmadeps.values():
                            need[("d", kb.sem)] = (dsem[kb.sem], v)
                        for d in op.deps:
                            if not d.inc:
                                continue
                            if d.eng == e and e in ("pe", "sp") and not op.dma:
                                continue
                            k = ("e", d.eng)
                            if k not in need or need[k][1] < d.incval:
                                need[k] = (esem[d.eng], d.incval)
                        for k, (s, v) in need.items():
                            if waited.get(k, 0) >= v:
                                continue
                            eng.wait_ge(s, v)
                            waited[k] = v
                        ins = op.fn(eng)
                        if op.dma:
                            ins.then_inc(dsem[op.dmasem.sem], 16)
                        elif op.inc:
                            ins.then_inc(esem[e], 1)
                    if e == "sp":
                        for b in dma_bufs:
                            eng.wait_ge(dsem[b.sem], b.semcnt)
                        for e2 in ENGS:
                            if e2 == "sp":
                                continue
                            tot = sum(1 for o in ops[e2] if o.inc and not o.dma)
                            if tot:
                                eng.wait_ge(esem[e2], tot)
                return body

            for e in ENGS:
                getattr(block, engobj[e])(make(e))


class Mem:
    def __init__(self, nc, base=16640, limit=229312):
        self.nc = nc
        self.off = base
        self.limit = limit
        self.n = 0
        self.peak = 0

    def alloc(self, name, shape, dtype):
        sz = int(np.prod(shape[1:])) * mybir.dt.size(dtype)
        off = (self.off + 63) // 64 * 64
        self.n += 1
        t = self.nc.alloc_sbuf_tensor_at("%s_%d" % (name, self.n), list(shape), dtype, offset=off)
        self.off = off + sz
        self.peak = max(self.peak, self.off)
        assert self.off <= self.limit, (name, self.off, self.limit)
        return t

    def mark(self):
        return self.off

    def reset(self, m):
        self.off = m


GAM = [1.0 - 2.0 ** (-5 - h) for h in range(4)]


class _Stop(Exception):
    pass


def build_program(dbg=None):
    dbg = dbg or {}

    def ck(n):
        if dbg.get("stop") == n:
            raise _Stop()

    nc = bass.Bass("TRN2", target_bir_lowering=False)

    def din(name, shape, dt=F32):
        return nc.dram_tensor(name, list(shape), dt, kind="ExternalInput").ap()

    def dout(name, shape, dt=F32):
        return nc.dram_tensor(name, list(shape), dt, kind="ExternalOutput").ap()

    x_d = din("x", [NT * 128, 1024])
    sh_d = din("sh", [16, 4, 128, 128])
    sr_d = din("sr", [16, 4, 128, 128])
    sc_d = din("sc", [32, 2 * DFF])
    w_in_d = din("w_in", [1024, 4096])
    w_out_d = din("w_out", [1024, 1024])
    w_fi_d = din("w_fi", [1024, 2 * DFF])
    w_fo_d = din("w_fo", [DFF, 1024])
    small_d = din("small", [128, 64])
    convw_d = din("convw", [128, 4 * 44])
    wnf_d = din("wnf", [128, 1024])
    rope_d = din("rope", [NT, 128, 1024])
    cbf_d = din("cbf", [128, 128 * 3], BF16)
    smt_d = din("smt", [128, 2048], BF16)
    smb_d = din("smb", [128, 2048], BF16)
    idf_d = din("idf", [128, 128])

    y_d = dout("y", [NT * 128, 1024])
    nhp_d = dout("nhp", [4, 128, 128])
    nrp_d = dout("nrp", [4, 128, 128])
    ncp_d = dout("ncp", [2, 2 * DFF])
    nhs_d = dout("nhs", [16, 4, 128, 128])
    nrs_d = dout("nrs", [16, 4, 128, 128])
    ncs_d = dout("ncs", [32, 2 * DFF])

    mem = Mem(nc)
    P = Prog(nc)

    XS = 9
    X = mem.alloc("X", [128, XS, 1024], F32)
    bX = [Buf("X%d" % i) for i in range(XS)]
    small = mem.alloc("small", [128, 64], F32); bsmall = Buf("small")
    convw = mem.alloc("convw", [128, 176], F32); bconvw = Buf("convw")
    cbf = mem.alloc("cbf", [128, 384], BF16); bcbf = Buf("cbf")
    idf = mem.alloc("idf", [128, 128], F32); bidf = Buf("idf")
    cst = mem.alloc("cst", [128, 32], F32); bcst = Buf("cst")
    S = mem.alloc("S", [128, 8, 128], F32)
    Sb = mem.alloc("Sb", [128, 8, 128], BF16)
    bS = [Buf("S%d" % h) for h in range(8)]
    bSb = [Buf("Sb%d" % h) for h in range(8)]
    carry = mem.alloc("carry", [128, 2, 44, 2], F32)
    bcarry = [[Buf("cy%d_%d" % (q, i)) for i in range(44)] for q in range(2)]
    cpar = [0] * 44
    scrA = mem.alloc("scrA", [128, 2], F32)
    scrD = mem.alloc("scrD", [128, 2], F32)
    scrP = mem.alloc("scrP", [128, 2], F32)
    zeros = mem.alloc("zeros", [128, 128], F32); bzeros = Buf("zeros")
    ra_off = (mem.off + 63) // 64 * 64
    Win = nc.alloc_sbuf_tensor_at("RA_win", [128, 8, 4096], BF16, offset=ra_off)
    Wfi = nc.alloc_sbuf_tensor_at("RA_wfi", [128, 8, 2816], BF16, offset=ra_off)
    mem.off = ra_off + 65536
    mem.peak = max(mem.peak, mem.off)
    bRA = Buf("RA")
    bWin = bRA
    bWfi = bRA
    bWinHi = Buf("WinHi")
    pref = {"win": False, "wfi": False}

    def load_win(k0=0, k1=8, buf=None):
        buf = buf or bRA
        for (ca, cb_) in [(0, 1024), (2048, 3072), (1024, 2048), (3072, 4096)]:
            o_ = P.dma("pool", (lambda ca, cb_: lambda e: e.dma_start(
                out=Win[:, k0:k1, ca:cb_], in_=w_in_d[k0 * 128:k1 * 128, ca:cb_].rearrange("(k p) n -> p k n", p=128),
                max_dma_last_dim=8192))(ca, cb_), [], [buf], buf)
            P.recent_dmas.remove(o_)

    def load_wfi(half):
        for k in range(8):
            for part in (1, 0):
                c0_ = part * DFF + half * 1408
                o_ = P.dma("pool", (lambda k, part, c0_: lambda e: e.dma_start(
                    out=Wfi[:, k, part * 1408:(part + 1) * 1408], in_=w_fi_d[k * 128:(k + 1) * 128, c0_:c0_ + 1408],
                    max_dma_last_dim=8192))(k, part, c0_), [], [bRA], bRA)
                P.recent_dmas.remove(o_)

    identb = cbf[:, 0:128]
    maskP = cbf[:, 128:256]
    maskS = cbf[:, 256:384]
    wn1 = small[:, 0:8]
    wn2 = small[:, 8:16]
    nwa = small[:, 16:17]
    nwb = small[:, 17:18]
    lb0 = small[:, 18:22]
    lb1 = small[:, 22:26]
    c0 = cst[:, 0:4]
    c1 = cst[:, 4:8]
    nc1 = cst[:, 8:12]
    mhalf = cst[:, 12:13]

    PS = nc.alloc_psum_tensor("ps", [128, 4096], F32)
    PSb16 = PS[:].bitcast(BF16)
    bB = [Buf("B%d" % i, excl=True) for i in range(8)]

    def bank(i):
        return PS[:, i * 512:(i + 1) * 512]

    def bankb(i):
        return PSb16[:, i * 1024:(i + 1) * 1024]

    def mm(out, lhsT, rhs, start, stop, reads, writes):
        P.op("pe", lambda e: e.matmul(out, lhsT=lhsT, rhs=rhs, start=start, stop=stop), reads, writes)

    def tr(out, in_, ident, reads, writes):
        P.op("pe", lambda e: e.transpose(out=out, in_=in_, identity=ident), reads, writes)

    def act(out, in_, func, reads, writes, scale=None, bias=None, accum=None):
        kw = {}
        if scale is not None:
            kw["scale"] = scale
        if bias is not None:
            kw["bias"] = bias
        if accum is not None:
            kw["accum_out"] = accum
        P.op("act", lambda e: e.activation(out=out, in_=in_, func=func, **kw), reads, writes)

    def ts(eng, out, in0, s1, s2, op0, op1, reads, writes):
        if op1 is None:
            P.op(eng, lambda e: e.tensor_scalar(out=out, in0=in0, scalar1=s1, scalar2=None, op0=op0), reads, writes)
        else:
            P.op(eng, lambda e: e.tensor_scalar(out=out, in0=in0, scalar1=s1, scalar2=s2, op0=op0, op1=op1), reads, writes)

    def tt(eng, out, in0, in1, op, reads, writes):
        P.op(eng, lambda e: e.tensor_tensor(out=out, in0=in0, in1=in1, op=op), reads, writes)

    def stt(out, in0, scalar, in1, op0, op1, reads, writes):
        P.op("dve", lambda e: e.scalar_tensor_tensor(out=out, in0=in0, scalar=scalar, in1=in1, op0=op0, op1=op1), reads, writes)

    def cp(eng, out, in_, reads, writes):
        if eng == "act":
            P.op("act", lambda e: e.activation(out=out, in_=in_, func=AF.Copy), reads, writes)
        else:
            P.op(eng, lambda e: e.tensor_copy(out=out, in_=in_), reads, writes)

    def dma(out, in_, reads, writes, key, eng="sp", slow=False):
        if slow:
            P.dma(eng, lambda e: e.dma_start(out=out, in_=in_, allow_slow_non_contiguous=True), reads, writes, key)
        else:
            P.dma(eng, lambda e: e.dma_start(out=out, in_=in_), reads, writes, key)

    def dma_cast(out, in_, writes, key):
        P.dma("pool", lambda e: e.dma_start(out=out, in_=in_, max_dma_last_dim=8192), [], writes, key)

    def barrier():
        P.barrier({
            "act": lambda e: e.activation(out=scrA[:, 0:1], in_=scrA[:, 1:2], func=AF.Copy),
            "dve": lambda e: e.tensor_copy(out=scrD[:, 0:1], in_=scrD[:, 1:2]),
            "pool": lambda e: e.memset(scrP[:, 0:1], 0.0),
        })

    dma(small[:], small_d, [], [bsmall], bsmall)
    dma(convw[:], convw_d, [], [bconvw], bconvw)
    dma(cbf[:], cbf_d, [], [bcbf], bcbf)
    dma(idf[:], idf_d, [], [bidf], bidf)
    P.op("pool", lambda e: e.memset(zeros[:], 0.0), [], [bzeros])
    P.op("pool", lambda e: e.memset(scrA[:], 0.0), [], [])
    P.op("pool", lambda e: e.memset(scrD[:], 0.0), [], [])
    P.op("pool", lambda e: e.memset(scrP[:], 0.0), [], [])
    P.op("pool", lambda e: e.memset(cst[:], 0.0), [], [bcst])
    P.op("pool", lambda e: e.memset(cst[:, 12:13], -0.5), [], [bcst])
    for h in range(8):
        P.op("pool", (lambda hh: lambda e: e.memset(S[:, hh, :], 0.0))(h), [], [bS[h]])
        P.op("pool", (lambda hh: lambda e: e.memset(Sb[:, hh, :], 0.0))(h), [], [bSb[h]])
    P.op("pool", lambda e: e.memset(carry[:].rearrange("p a b c -> p (a b c)"), 0.0), [], [b for q in range(2) for b in bcarry[q]])
    tt("dve", cst[:, 24:28], lb0, lb1, ALU.subtract, [bsmall, bcst], [bcst])
    act(cst[:, 28:32], cst[:, 24:28], AF.Tanh, [bcst], [bcst], scale=0.5)
    ts("dve", c0, cst[:, 28:32], 0.25, 0.75, ALU.mult, ALU.add, [bcst], [bcst])
    ts("dve", c1, cst[:, 28:32], -0.25, 0.25, ALU.mult, ALU.add, [bcst], [bcst])
    ts("dve", nc1, cst[:, 28:32], 0.25, -0.25, ALU.mult, ALU.add, [bcst], [bcst])

    m0 = mem.mark()
    cast_rr = [0]

    def cast_scaled(out, in_, scal, reads, writes):
        i = cast_rr[0]
        cast_rr[0] += 1
        eng = ("dve", "pool", "act")[i % 3]
        if scal is None:
            cp(eng, out, in_, reads, writes)
        elif eng == "act":
            act(out, in_, AF.Identity, reads, writes, scale=scal)
        else:
            ts(eng, out, in_, scal, None, ALU.mult, None, reads, writes)

    def phase1(tiles):
        ntl = len(tiles)
        has_samp = any(gi == 16 for (_, gi) in tiles)
        Wout = mem.alloc("Wout", [128, 8, 1024], BF16); bWout = Buf("Wout")
        if not pref["win"]:
            load_win()
        pref["win"] = False
        dma_cast(Wout[:], w_out_d.rearrange("(k p) n -> p k n", p=128), [bWout], bWout)
        ck(1)
        rope1 = mem.alloc("rope", [128, 4, 256], F32); rope = [rope1, rope1]; brope1 = Buf("rope"); brope = [brope1, brope1]
        xn = mem.alloc("xn", [128, 1024], BF16); bxn = Buf("xn")
        hT = mem.alloc("hT", [128, 8, 128], BF16); bhT = Buf("hT")
        st = mem.alloc("st", [128, 16], F32); bst = Buf("st")
        th = mem.alloc("th", [128, 4, 128], F32); bth = Buf("th")
        ff = mem.alloc("ff", [128, 128], F32); bff = Buf("ff")
        kk = mem.alloc("kk", [128, 128], F32); bkk = Buf("kk")
        RR = mem.alloc("RR", [128, 128], F32); bRR = Buf("RR")
        qtok = mem.alloc("qtok", [128, 4, 128], BF16); bqtok = Buf("qtok")
        khT = mem.alloc("khT", [128, 4, 128], BF16); bkhT = [Buf("khT%d" % h) for h in range(4)]
        vvh = [mem.alloc("vvh", [128, 4, 128], BF16) for _ in range(2)]; bvvh = [Buf("vvh0"), Buf("vvh1")]
        rt = [mem.alloc("rt", [128, 256], BF16) for _ in range(8)]; brt = [Buf("rt%d" % i) for i in range(8)]
        scm = mem.alloc("scm", [128, 8, 128], BF16); bscm = [Buf("scm%d" % h) for h in range(8)]
        og = mem.alloc("og", [128, 1024], BF16); bog = Buf("og")
        on = og; bon = bog
        ogT = mem.alloc("ogT", [128, 8, 128], BF16); bogT = Buf("ogT")
        Pc = [mem.alloc("Pc", [128, 4, 128], F32) for _ in range(2)]
        bPc = [[Buf("Pc%d_%d" % (p, h)) for h in range(4)] for p in range(2)]
        qT = [mem.alloc("qT", [128, 8, 128], BF16) for _ in range(2)]
        bqT = [[Buf("qT") for h in range(8)] for p in range(2)]
        kT = [mem.alloc("kT", [128, 8, 128], BF16) for _ in range(2)]
        bkT = [[Buf("kT") for h in range(8)] for p in range(2)]
        ktok = [mem.alloc("ktok", [128, 8, 128], BF16) for _ in range(2)]
        bktok = [[Buf("ktok") for h in range(8)] for p in range(2)]
        vv = [mem.alloc("vv", [128, 8, 128], BF16) for _ in range(2)]
        bvv = [[Buf("vva"), Buf("vvb")] for p in range(2)]
        sg = [mem.alloc("sg", [128, 8, 128], BF16) for _ in range(2)]
        bsg = [[Buf("sga"), Buf("sgb")] for p in range(2)]
        if has_samp:
            smt = mem.alloc("smt", [128, 16, 128], BF16); bsmt = Buf("smt")
            smb = mem.alloc("smb", [128, 16, 128], BF16); bsmb = Buf("smb")
            dma(smt[:].rearrange("p j v -> p (j v)"), smt_d, [], [bsmt], bsmt)
            dma(smb[:].rearrange("p j v -> p (j v)"), smb_d, [], [bsmb], bsmb)
            Sin = [mem.alloc("Sin", [128, 4, 128], F32) for _ in range(4)]; bSin = [Buf("Sin%d" % i) for i in range(4)]
            Sinb = [mem.alloc("Sinb", [128, 4, 128], BF16) for _ in range(4)]; bSinb = [Buf("Sinb%d" % i) for i in range(4)]
            vblk = [mem.alloc("vblk", [128, 4, 128], BF16) for _ in range(4)]; bvblk = [Buf("vblk%d" % i) for i in range(4)]
            qblk = [mem.alloc("qblk", [128, 4, 128], BF16) for _ in range(4)]; bqblk = [Buf("qblk%d" % i) for i in range(4)]
        wn1b = wn1.rearrange("p (k o) -> p k o", o=1).broadcast_to([128, 8, 128])
        has_first = any(gi == 0 for (_, gi) in tiles)
        if has_first:
            q32 = mem.alloc("q32", [128, 8, 128], F32); bq32 = [Buf("q32_%d" % h) for h in range(8)]
            k32 = mem.alloc("k32", [128, 8, 128], F32); bk32 = [Buf("k32_%d" % h) for h in range(8)]
            q32tok = mem.alloc("q32tok", [128, 4, 128], F32); bq32tok = Buf("q32tok")
            k32tok = mem.alloc("k32tok", [128, 4, 128], F32); bk32tok = Buf("k32tok")
            rtf = [mem.alloc("rtf", [128, 256], F32) for _ in range(4)]; brtf = [Buf("rtf%d" % i) for i in range(4)]

        def front(idx):
            slot, gi = tiles[idx]
            p = idx % 2
            samp = (gi == 16)
            Xj = X[:, slot, :]
            rp = rope[p]

            def F1a():
                dma(Xj, x_d[gi * 128:(gi + 1) * 128, :], [], [bX[slot]], bX[slot])
                dma(rp[:].rearrange("p a c -> p (a c)"), rope_d[gi], [], [brope[p]], brope[p])
                act(xn[:], Xj, AF.Square, [bX[slot]], [bxn, bst], accum=st[:, 0:1])
                ts("dve", st[:, 1:2], st[:, 0:1], 1.0 / 1024, EPS, ALU.mult, ALU.add, [bst], [bst])
                tt("pool", st[:, 2:3], st[:, 1:2], mhalf, ALU.pow, [bst, bcst], [bst])
                act(xn[:], Xj, AF.Identity, [bX[slot], bst], [bxn], scale=st[:, 2:3])

            def F1b():
                for k in range(8):
                    tr(bankb(0)[:, k * 128:(k + 1) * 128], xn[:, k * 128:(k + 1) * 128], identb, [bxn, bcbf], [bB[0]])
                tt("dve", hT[:], bankb(0)[:, 0:1024].rearrange("p (k t) -> p k t", k=8), wn1b, ALU.mult, [bB[0], bsmall], [bhT])

            def F2():
                for c in (4, 5, 6, 7, 0, 1, 2, 3):
                    bk_ = 1 if c < 4 else 2
                    hh = c % 4
                    for k in range(8):
                        mm(bank(bk_)[:, hh * 128:(hh + 1) * 128], Win[:, k, c * 128:(c + 1) * 128], hT[:, k, :],
                           k == 0, k == 7, [bWin, bWinHi, bhT], [bB[bk_]])
                act(th[:].rearrange("p h t -> p (h t)"), bank(2)[:, 0:512], AF.Tanh, [bB[2]], [bth], scale=0.5)
                nseg = 16 if samp else 1
                seglen = 128 // nseg
                for h in range(4):
                    qa = bank(1)[:, h * 128:(h + 1) * 128]
                    act(ff[:], th[:, h, :], AF.Identity, [bth, bcst], [bff], scale=c1[:, h:h + 1], bias=c0[:, h:h + 1])
                    act(kk[:], th[:, h, :], AF.Identity, [bth, bcst], [bkk], scale=nc1[:, h:h + 1], bias=c1[:, h:h + 1])
                    for sgi in range(nseg):
                        a, b_ = sgi * seglen, (sgi + 1) * seglen
                        P.op("dve", (lambda a, b_, h: lambda e: e.tensor_tensor_scan(
                            out=Pc[p][:, h, a:b_], data0=ff[:, a:b_], data1=zeros[:, a:b_], initial=1.0,
                            op0=ALU.mult, op1=ALU.add))(a, b_, h), [bff, bzeros], [bPc[p][h]])
                    P.op("dve", (lambda h: lambda e: e.reciprocal(out=RR[:], in_=Pc[p][:, h, :]))(h), [bPc[p][h]], [bRR])
                    tt("dve", qT[p][:, h, :], qa, Pc[p][:, h, :], ALU.mult, [bB[1], bPc[p][h]], [bqT[p][h]])
                    tt("dve", kT[p][:, h, :], kk[:], RR[:], ALU.mult, [bkk, bRR], [bkT[p][h]])
                    if gi == 0:
                        tt("dve", q32[:, h, :], qa, Pc[p][:, h, :], ALU.mult, [bB[1], bPc[p][h]], [bq32[h]])
                        tt("dve", k32[:, h, :], kk[:], RR[:], ALU.mult, [bkk, bRR], [bk32[h]])
                    if not samp:
                        ts("dve", khT[:, h, :], kT[p][:, h, :], Pc[p][:, h, 127:128], None, ALU.mult, None, [bkT[p][h], bPc[p][h]], [bkhT[h]])

            def F3():
                for (c0_, bk_) in [(2048, 3), (2560, 0)]:
                    for k in range(8):
                        mm(bank(bk_)[:, 0:512], hT[:, k, :], Win[:, k, c0_:c0_ + 512], k == 0, k == 7, [bWin, bWinHi, bhT], [bB[bk_]])
                for h in range(4):
                    if samp:
                        tr(bankb(2)[:, h * 128:(h + 1) * 128], kT[p][:, h, :], identb, [bkT[p][h], bcbf], [bB[2]])
                    else:
                        tr(bankb(2)[:, h * 128:(h + 1) * 128], khT[:, h, :], identb, [bkhT[h], bcbf], [bB[2]])
                for h in range(4):
                    cp("act", ktok[p][:, h, :], bankb(2)[:, h * 128:(h + 1) * 128], [bB[2]], [bktok[p][h]])
                for (bk_, ci, si_, isq) in [(3, 0, 1, True), (0, 2, 3, False)]:
                    src = bank(bk_)[:, 0:512].rearrange("p (h d) -> p h d", h=4)
                    x1 = src[:, :, 0:64]
                    x2 = src[:, :, 64:128]
                    cs = rp[:, ci, :].rearrange("p (h d) -> p h d", h=4)
                    sn = rp[:, si_, :].rearrange("p (h d) -> p h d", h=4)
                    ro = 0 if isq else 4
                    tv = [rt[ro + i][:].rearrange("p (h d) -> p h d", h=4) for i in range(4)]
                    if isq:
                        d1, d2, bdst = qtok[:, :, 0:64], qtok[:, :, 64:128], [bqtok]
                    else:
                        d1, d2, bdst = ktok[p][:, 4:8, 0:64], ktok[p][:, 4:8, 64:128], bktok[p][4:8]
                    tt("dve", tv[0], x1, cs, ALU.mult, [bB[bk_], brope[p]], [brt[ro]])
                    tt("dve", tv[1], x2, sn, ALU.mult, [bB[bk_], brope[p]], [brt[ro + 1]])
                    tt("dve", tv[2], x1, sn, ALU.mult, [bB[bk_], brope[p]], [brt[ro + 2]])
                    tt("dve", tv[3], x2, cs, ALU.mult, [bB[bk_], brope[p]], [brt[ro + 3]])
                    tt("pool", d1, tv[0], tv[1], ALU.subtract, [brt[ro], brt[ro + 1]], bdst)
                    tt("pool", d2, tv[2], tv[3], ALU.add, [brt[ro + 2], brt[ro + 3]], bdst)
                    if gi == 0:
                        tf = [rtf[i][:].rearrange("p (h d) -> p h d", h=4) for i in range(4)]
                        dst32, bd32 = (q32tok, bq32tok) if isq else (k32tok, bk32tok)
                        tt("dve", tf[0], x1, cs, ALU.mult, [bB[bk_], brope[p]], [brtf[0]])
                        tt("dve", tf[1], x2, sn, ALU.mult, [bB[bk_], brope[p]], [brtf[1]])
                        tt("dve", tf[2], x1, sn, ALU.mult, [bB[bk_], brope[p]], [brtf[2]])
                        tt("dve", tf[3], x2, cs, ALU.mult, [bB[bk_], brope[p]], [brtf[3]])
                        tt("dve", dst32[:, :, 0:64], tf[0], tf[1], ALU.subtract, [brtf[0], brtf[1]], [bd32])
                        tt("dve", dst32[:, :, 64:128], tf[2], tf[3], ALU.add, [brtf[2], brtf[3]], [bd32])

            def F4():
                for (c0_, bk_) in [(1024, 1), (3072, 2)]:
                    for k in range(8):
                        mm(bank(bk_)[:, 0:512], hT[:, k, :], Win[:, k, c0_:c0_ + 512], k == 0, k == 7, [bWin, bWinHi, bhT], [bB[bk_]])
                cp("act", vv[p][:, 0:4, :].rearrange("p h d -> p (h d)"), bank(1)[:, 0:512], [bB[1]], [bvv[p][0]])
                cp("act", vv[p][:, 4:8, :].rearrange("p h d -> p (h d)"), bank(2)[:, 0:512], [bB[2]], [bvv[p][1]])
                if not samp:
                    for h in range(4):
                        act(vvh[p][:, h, :], bank(2)[:, h * 128:(h + 1) * 128], AF.Copy, [bB[2]], [bvvh[p]], scale=float(GAM[h] ** 128))
                for (c0_, bk_) in [(1536, 3), (3584, 0)]:
                    for k in range(8):
                        mm(bank(bk_)[:, 0:512], hT[:, k, :], Win[:, k, c0_:c0_ + 512], k == 0, k == 7, [bWin, bWinHi, bhT], [bB[bk_]])
                act(sg[p][:, 0:4, :].rearrange("p h d -> p (h d)"), bank(3)[:, 0:512], AF.Silu, [bB[3]], [bsg[p][0]])
                act(sg[p][:, 4:8, :].rearrange("p h d -> p (h d)"), bank(0)[:, 0:512], AF.Silu, [bB[0]], [bsg[p][1]])
                for h in range(4):
                    tr(bankb(1)[:, h * 128:(h + 1) * 128], qtok[:, h, :], identb, [bqtok, bcbf], [bB[1]])
                    tr(bankb(1)[:, (4 + h) * 128:(5 + h) * 128], ktok[p][:, 4 + h, :], identb, [bktok[p][4 + h], bcbf], [bB[1]])
                for h in range(4):
                    cp("act", qT[p][:, 4 + h, :], bankb(1)[:, h * 128:(h + 1) * 128], [bB[1]], [bqT[p][4 + h]])
                    cp("act", kT[p][:, 4 + h, :], bankb(1)[:, (4 + h) * 128:(5 + h) * 128], [bB[1]], [bkT[p][4 + h]])
                if gi == 0:
                    for h in range(4):
                        tr(bank(2)[:, h * 128:(h + 1) * 128], q32tok[:, h, :], idf[:], [bq32tok, bidf], [bB[2]])
                        tr(bank(3)[:, h * 128:(h + 1) * 128], k32tok[:, h, :], idf[:], [bk32tok, bidf], [bB[3]])
                    for h in range(4):
                        cp("dve", q32[:, 4 + h, :], bank(2)[:, h * 128:(h + 1) * 128], [bB[2]], [bq32[4 + h]])
                        cp("dve", k32[:, 4 + h, :], bank(3)[:, h * 128:(h + 1) * 128], [bB[3]], [bk32[4 + h]])

            return [F1a, F1b, F2, F3, F4]

        def back(idx):
            slot, gi = tiles[idx]
            p = idx % 2
            samp = (gi == 16)
            Xj = X[:, slot, :]
            mask = maskS if samp else maskP

            def K1():
                for h in range(8):
                    bk_ = 4 + h // 4
                    sc = bank(bk_)[:, (h % 4) * 128:(h % 4 + 1) * 128]
                    if gi == 0:
                        mm(sc, k32[:, h, :], q32[:, h, :], True, True, [bk32[h], bq32[h]], [bB[bk_]])
                    else:
                        mm(sc, kT[p][:, h, :], qT[p][:, h, :], True, True, [bkT[p][h], bqT[p][h]], [bB[bk_]])
                for h in range(8):
                    bk_ = 4 + h // 4
                    sc = bank(bk_)[:, (h % 4) * 128:(h % 4 + 1) * 128]
                    tt("dve", scm[:, h, :], sc, mask, ALU.mult, [bB[bk_], bcbf], [bscm[h]])

            def K2():
                if not samp:
                    for h in range(8):
                        bo = 6 + h // 4
                        ov = bank(bo)[:, (h % 4) * 128:(h % 4 + 1) * 128]
                        mm(ov, scm[:, h, :], vv[p][:, h, :], True, False, [bscm[h], bvv[p][h // 4]], [bB[bo]])
                        mm(ov, qT[p][:, h, :], Sb[:, h, :], False, True, [bqT[p][h], bSb[h]], [bB[bo]])
                    for h in range(8):
                        bu = 4 + h // 4
                        uv = bank(bu)[:, (h % 4) * 128:(h % 4 + 1) * 128]
                        if h < 4:
                            mm(uv, ktok[p][:, h, :], vv[p][:, h, :], True, True, [bktok[p][h], bvv[p][0]], [bB[bu]])
                        else:
                            mm(uv, ktok[p][:, h, :], vvh[p][:, h - 4, :], True, True, [bktok[p][h], bvvh[p]], [bB[bu]])
                    for h in range(8):
                        bu = 4 + h // 4
                        uv = bank(bu)[:, (h % 4) * 128:(h % 4 + 1) * 128]
                        E = Pc[p][:, h, 127:128] if h < 4 else float(GAM[h - 4] ** 128)
                        rd = [bS[h], bB[bu]] + ([bPc[p][h]] if h < 4 else [])
                        stt(S[:, h, :], S[:, h, :], E, uv, ALU.mult, ALU.add, rd, [bS[h]])
                        cp("pool", Sb[:, h, :], S[:, h, :], [bS[h]], [bSb[h]])
                else:
                    def src_of(i):
                        h, q = i // 4, i % 4
                        st_d = sh_d if h < 4 else sr_d
                        return st_d[4 * q:4 * q + 4, h % 4].rearrange("j d v -> d j v")

                    def load(i):
                        sl = i % 4
                        sv = src_of(i)
                        dma(Sin[sl][:], sv, [], [bSin[sl]], bSin[sl])

                    def cast(i):
                        sl = i % 4
                        cp("act", Sinb[sl][:].rearrange("p j v -> p (j v)"), Sin[sl][:].rearrange("p j v -> p (j v)"), [bSin[sl]], [bSinb[sl]])

                    def pre(i):
                        h, q = i // 4, i % 4
                        sl = i % 4
                        tt("dve", qblk[sl][:], qT[p][:, h:h + 1, :].broadcast_to([128, 4, 128]), smb[:, 4 * q:4 * q + 4, :], ALU.mult,
                           [bqT[p][h], bsmb], [bqblk[sl]])
                        tt("dve", vblk[sl][:], vv[p][:, h:h + 1, :].broadcast_to([128, 4, 128]), smt[:, 4 * q:4 * q + 4, :], ALU.mult,
                           [bvv[p][h // 4], bsmt], [bvblk[sl]])

                    load(0)
                    load(1)
                    cast(0)
                    pre(0)
                    for i in range(32):
                        h, q = i // 4, i % 4
                        sl = i % 4
                        if i + 2 < 32:
                            load(i + 2)
                        if i + 1 < 32:
                            cast(i + 1)
                            pre(i + 1)
                        bo = 6 + h // 4
                        ov = bank(bo)[:, (h % 4) * 128:(h % 4 + 1) * 128]
                        ns_d = nhs_d if h < 4 else nrs_d
                        if q == 0:
                            mm(ov, scm[:, h, :], vv[p][:, h, :], True, False, [bscm[h], bvv[p][h // 4]], [bB[bo]])
                        for j in range(4):
                            mm(ov, qblk[sl][:, j, :], Sinb[sl][:, j, :], False, (q == 3 and j == 3), [bqblk[sl], bSinb[sl]], [bB[bo]])
                        mm(bank(q)[:, 0:512], ktok[p][:, h, :], vblk[sl][:].rearrange("p j v -> p (j v)"), True, True,
                           [bktok[p][h], bvblk[sl]], [bB[q]])
                        Sf = Sin[sl][:].rearrange("p j v -> p (j v)")
                        tt("dve", Sf, bank(q)[:, 0:512], Sf, ALU.add, [bB[q], bSin[sl]], [bSin[sl]])
                        if h < 4:
                            Eb = Pc[p][:, h, :].rearrange("p (j t) -> p j t", t=8)[:, 4 * q:4 * q + 4, 7:8].broadcast_to([128, 4, 128])
                            tt("dve", Sin[sl][:], Sin[sl][:], Eb, ALU.mult, [bSin[sl], bPc[p][h]], [bSin[sl]])
                        else:
                            act(Sf, Sf, AF.Identity, [bSin[sl]], [bSin[sl]], scale=float(GAM[h - 4] ** 8))
                        dma(ns_d[4 * q:4 * q + 4, h % 4].rearrange("j d v -> d j v"), Sin[sl][:], [bSin[sl]], [], bSin[sl])

            def K3a():
                mv = mem_mv
                o_all = PS[:, 6 * 512:8 * 512]
                act(sq[:], o_all, AF.Square, [bB[6], bB[7]], [bsq])
                P.op("dve", lambda e: e.tensor_reduce(out=mv[:, 0:8], in_=o_all.rearrange("p (h d) -> p h d", h=8),
                                                       axis=mybir.AxisListType.X, op=ALU.add), [bB[6], bB[7]], [bmv])
                P.op("dve", lambda e: e.tensor_reduce(out=mv[:, 8:16], in_=sq[:].rearrange("p (h d) -> p h d", h=8),
                                                       axis=mybir.AxisListType.X, op=ALU.add), [bsq], [bmv])
            def K3a1():
                mv = mem_mv
                o_all = PS[:, 6 * 512:8 * 512]
                ts("dve", mv[:, 16:24], mv[:, 0:8], 1.0 / 128, None, ALU.mult, None, [bmv], [bmv])
                ts("dve", mv[:, 24:32], mv[:, 8:16], 1.0 / 128, EPS, ALU.mult, ALU.add, [bmv], [bmv])
                tt("dve", mv[:, 32:36], mv[:, 20:24], mv[:, 20:24], ALU.mult, [bmv], [bmv])
                tt("dve", mv[:, 28:32], mv[:, 28:32], mv[:, 32:36], ALU.subtract, [bmv], [bmv])
                tt("pool", mv[:, 36:44], mv[:, 24:32], mhalf.broadcast_to([128, 8]), ALU.pow, [bmv, bcst], [bmv])
                stt(mv[:, 48:52], mv[:, 20:24], -1.0, mv[:, 40:44], ALU.mult, ALU.mult, [bmv], [bmv])
                og3 = og[:].rearrange("p (h d) -> p h d", h=8)
                rs_b = mv[:, 36:44].rearrange("p (h o) -> p h o", o=1).broadcast_to([128, 8, 128])
                nm_b = mv[:, 48:52].rearrange("p (h o) -> p h o", o=1).broadcast_to([128, 4, 128])
                tt("dve", og3, o_all.rearrange("p (h d) -> p h d", h=8), rs_b, ALU.mult, [bB[6], bB[7], bmv], [bog])
                tt("dve", og3[:, 4:8, :], og3[:, 4:8, :], nm_b, ALU.add, [bog, bmv], [bog])
                tt("dve", og[:, 0:512], og[:, 0:512], sg[p][:, 0:4, :].rearrange("p h d -> p (h d)"), ALU.mult, [bog, bsg[p][0]], [bog])
                tt("dve", og[:, 512:1024], og[:, 512:1024], sg[p][:, 4:8, :].rearrange("p h d -> p (h d)"), ALU.mult, [bog, bsg[p][1]], [bog])

            def K3b():
                for k in range(8):
                    tr(bankb(7)[:, k * 128:(k + 1) * 128], og[:, k * 128:(k + 1) * 128], identb, [bog, bcbf], [bB[7]])
                act(ogT[:, 0:4, :].rearrange("p k t -> p (k t)"), bankb(7)[:, 0:512], AF.Identity, [bB[7], bsmall], [bogT], scale=nwa)
                act(ogT[:, 4:8, :].rearrange("p k t -> p (k t)"), bankb(7)[:, 512:1024], AF.Identity, [bB[7], bsmall], [bogT], scale=nwb)

            def K4():
                for hf in range(2):
                    for k in range(8):
                        mm(bank(4 + hf)[:, 0:512], ogT[:, k, :], Wout[:, k, hf * 512:(hf + 1) * 512], k == 0, k == 7,
                           [bogT, bWout], [bB[4 + hf]])
                for hf in range(2):
                    tt("dve", Xj[:, hf * 512:(hf + 1) * 512], bank(4 + hf)[:, 0:512], Xj[:, hf * 512:(hf + 1) * 512], ALU.add,
                       [bB[4 + hf], bX[slot]], [bX[slot]])
                if gi == 15:
                    for h in range(8):
                        dst = nhp_d[h] if h < 4 else nrp_d[h - 4]
                        dma(dst, S[:, h, :], [bS[h]], [], bS[h])

            return [K1, K2, K3a, K3b, K4, K3a1]

        sq = mem.alloc("sq", [128, 1024], F32); bsq = Buf("sq")
        P.op("pool", lambda e: e.memset(mem_mv[:, 44:48], 0.0), [], [bmv])
        fronts = [front(i) for i in range(ntl)]
        backs = [back(i) for i in range(ntl)]
        for c in fronts[0]:
            c()
        if ntl > 1:
            fronts[1][0]()
        ck(2)
        for idx in range(ntl):
            K1, K2, K3a, K3b, K4, _K3a1 = backs[idx]
            nf = fronts[idx + 1] if idx + 1 < ntl else None
            if nf:
                nf[1]()
            K1()
            if nf:
                nf[2]()
            K2()
            if nf:
                nf[3]()
            K3a()
            if idx + 2 < ntl:
                fronts[idx + 2][0]()
            backs[idx][5]()
            if nf:
                nf[4]()
            if idx > 0:
                backs[idx - 1][4]()
            K3b()
            if idx == ntl - 1:
                K4()
            if idx == ntl - 2 or ntl == 1:
                load_wfi(0)
                pref["wfi"] = True

    def phase2(tiles):
        nt = len(tiles)
        h2T = mem.alloc("h2T", [128, 8, nt * 128], BF16)
        bh2 = [Buf("h2T%d" % i) for i in range(nt)]
        Wfo = mem.alloc("Wfo", [128, 11, 1024], BF16); bWfo = Buf("Wfo")
        wnf = mem.alloc("wnf", [128, 1024], F32); bwnf = Buf("wnf")
        dma(wnf[:], wnf_d, [], [bwnf], bwnf)
        wn2b = wn2.rearrange("p (k o) -> p k o", o=1).broadcast_to([128, 8, 128])
        xn2 = [mem.alloc("xn2", [128, 1024], BF16) for _ in range(2)]; bxn2 = [Buf("xn2a"), Buf("xn2b")]
        st2 = [mem.alloc("st2", [128, 16], F32) for _ in range(2)]; bst2 = [Buf("st2a"), Buf("st2b")]
        xn, bxn, st, bst = xn2[0], bxn2[0], st2[0], bst2[0]
        hl = mem.alloc("hl", [128, 4, 4], F32); bhl = [Buf("hl%d" % i) for i in range(4)]
        ext = [mem.alloc("ext", [128, 160], F32) for _ in range(2)]
        bext = [Buf("ext0"), Buf("ext1")]
        NCC = 4
        cc = [mem.alloc("cc", [128, 512], F32) for _ in range(NCC)]
        bcc = [Buf("cc%d" % i) for i in range(NCC)]
        sgl = [mem.alloc("sgl", [128, 512], BF16) for _ in range(2)]; bsgl = [Buf("sgl0"), Buf("sgl1")]
        actT = mem.alloc("actT", [128, 11, 512], BF16); bactT = [Buf("actT%d" % i) for i in range(11)]
        ncT = mem.alloc("ncT", [128, 22, 32], F32); bncT = Buf("ncT")
        scT = mem.alloc("scT", [128, 22, 32], F32); bscT = Buf("scT")
        sc32 = mem.alloc("sc32", [32, 2816], F32); bsc32 = Buf("sc32")
        yb = [mem.alloc("yb", [128, 1024], F32) for _ in range(2)]; byb = [Buf("yb0"), Buf("yb1")]
        has_samp = any(gi == 16 for (_, gi) in tiles)
        sts = []
        pt = [t for t in tiles if t[1] != 16]
        for i in range(0, len(pt), 4):
            sts.append(pt[i:i + 4])
        if has_samp:
            sts.append([t for t in tiles if t[1] == 16])
        slot_pos = {s_: i for i, (s_, _) in enumerate(tiles)}
        ycnt = [0]

        def prepA(slot, pi_):
            Xj = X[:, slot, :]
            xn, bxn, st, bst = xn2[pi_], bxn2[pi_], st2[pi_], bst2[pi_]
            act(xn[:], Xj, AF.Square, [bX[slot]], [bxn, bst], accum=st[:, 0:1])
            ts("dve", st[:, 1:2], st[:, 0:1], 1.0 / 1024, EPS, ALU.mult, ALU.add, [bst], [bst])
            tt("pool", st[:, 2:3], st[:, 1:2], mhalf, ALU.pow, [bst, bcst], [bst])
            act(xn[:], Xj, AF.Identity, [bX[slot], bst], [bxn], scale=st[:, 2:3])

        def prepB(slot, pi_):
            pp = slot_pos[slot]
            xn, bxn = xn2[pi_], bxn2[pi_]
            for k in range(8):
                tr(bankb(7)[:, k * 128:(k + 1) * 128], xn[:, k * 128:(k + 1) * 128], identb, [bxn, bcbf], [bB[7]])
            tt("dve", h2T[:, :, pp * 128:(pp + 1) * 128],
               bankb(7)[:, 0:1024].rearrange("p (k t) -> p k t", k=8), wn2b, ALU.mult, [bB[7], bsmall], [bh2[pp]])

        for half in range(2):
            if not pref["wfi"]:
                load_wfi(half)
            pref["wfi"] = False
            if half == 1 and not has_samp:
                load_win(6, 8, bWinHi)
                pref["win_hi"] = True
            dma_cast(Wfo[:], w_fo_d[half * 1408:(half + 1) * 1408, :].rearrange("(c p) n -> p c n", p=128), [bWfo], bWfo)
            if has_samp:
                for part in range(2):
                    c0_ = part * DFF + half * 1408
                    dma(sc32[:, part * 1408:(part + 1) * 1408], sc_d[:, c0_:c0_ + 1408], [], [bsc32], bsc32)

            def sc_transposes():
                for ft in range(22):
                    bk = 4 + (ft % 2)
                    tr(bank(bk)[:, 0:32], sc32[:, ft * 128:(ft + 1) * 128], idf[0:32, 0:32], [bsc32, bidf], [bB[bk]])
                    cp("dve", scT[:, ft, :], bank(bk)[:, 0:32], [bB[bk]], [bscT])
            if half == 0:
                fs = [s_ for (s_, _) in sts[0]]
                prepA(fs[0], 0)
                for j in range(len(fs)):
                    if j + 1 < len(fs):
                        prepA(fs[j + 1], (j + 1) % 2)
                    prepB(fs[j], j % 2)

            for sti, stl in enumerate(sts):
                samp = (stl[0][1] == 16)
                ntk = 128 * len(stl)
                if has_samp and sti == max(len(sts) - 2, 0):
                    sc_transposes()
                p0 = slot_pos[stl[0][0]] * 128
                nxt = [s_ for (s_, _) in sts[sti + 1]] if (half == 0 and sti + 1 < len(sts)) else []
                rd_h2 = [bh2[slot_pos[s_]] for (s_, _) in stl]
                fi = 0
                tails = []
                for c in range(11):
                    if nxt and c % 2 == 0 and (c // 2) < len(nxt):
                        prepA(nxt[c // 2], (c // 2) % 2)
                    for part in (1, 0):
                        ft = part * 11 + c
                        gft = part * 22 + half * 11 + c
                        bk = fi % 4
                        ci = fi % NCC
                        fi += 1
                        up = bank(bk)[:, 0:ntk]
                        for k in range(8):
                            mm(up, Wfi[:, k, part * 1408 + c * 128: part * 1408 + (c + 1) * 128],
                               h2T[:, k, p0:p0 + ntk], k == 0, k == 7, [bWfi] + rd_h2, [bB[bk]])
                        cv = cc[ci]
                        w0 = convw[:, gft:gft + 1]
                        w1 = convw[:, 44 + gft:45 + gft]
                        w2 = convw[:, 88 + gft:89 + gft]
                        cb = convw[:, 132 + gft:133 + gft]
                        if not samp:
                            hi_ = fi % 4
                            cp_ = cpar[gft]
                            cold = carry[:, cp_, gft, :]
                            act(cv[:, 0:ntk], up, AF.Identity, [bB[bk], bconvw], [bcc[ci]], scale=w2, bias=cb)
                            cp("act", carry[:, 1 - cp_, gft, :], up[:, ntk - 2:ntk], [bB[bk]], [bcarry[1 - cp_][gft]])
                            stt(cv[:, 1:ntk], up[:, 0:ntk - 1], w1, cv[:, 1:ntk], ALU.mult, ALU.add, [bB[bk], bcc[ci], bconvw], [bcc[ci]])
                            stt(cv[:, 2:ntk], up[:, 0:ntk - 2], w0, cv[:, 2:ntk], ALU.mult, ALU.add, [bB[bk], bcc[ci], bconvw], [bcc[ci]])
                            ts("pool", hl[:, hi_, 0:2], cold, w0, None, ALU.mult, None, [bcarry[cp_][gft], bconvw], [bhl[hi_]])
                            ts("pool", hl[:, hi_, 2:3], cold[:, 1:2], w1, None, ALU.mult, None, [bcarry[cp_][gft], bconvw], [bhl[hi_]])
                            tt("pool", cv[:, 0:2], cv[:, 0:2], hl[:, hi_, 0:2], ALU.add, [bcc[ci], bhl[hi_]], [bcc[ci]])
                            tt("pool", cv[:, 0:1], cv[:, 0:1], hl[:, hi_, 2:3], ALU.add, [bcc[ci], bhl[hi_]], [bcc[ci]])
                            cpar[gft] = 1 - cp_
                        else:
                            e_i = fi % 2
                            ex = ext[e_i]
                            ex3 = ex[:, 0:160].rearrange("p (j t) -> p j t", t=10)
                            cv3 = cv[:, 0:128].rearrange("p (j t) -> p j t", t=8)
                            up3 = up.rearrange("p (j t) -> p j t", t=8)
                            cp("pool", ex3[:, :, 0:2], scT[:, ft, :].rearrange("p (j r) -> p j r", r=2), [bscT], [bext[e_i]])
                            cp("act", ex3[:, :, 2:10], up3, [bB[bk]], [bext[e_i]])
                            cp("pool", ncT[:, ft, :].rearrange("p (j r) -> p j r", r=2), ex3[:, :, 8:10], [bext[e_i]], [bncT])
                            act(cv[:, 0:128], up, AF.Identity, [bB[bk], bconvw], [bcc[ci]], scale=w2, bias=cb)
                            stt(cv3, ex3[:, :, 1:9], w1, cv3, ALU.mult, ALU.add, [bext[e_i], bcc[ci], bconvw], [bcc[ci]])
                            stt(cv3, ex3[:, :, 0:8], w0, cv3, ALU.mult, ALU.add, [bext[e_i], bcc[ci], bconvw], [bcc[ci]])
                        if part == 1:
                            tails.append((lambda cv, ci, c, ntk: lambda: act(sgl[c % 2][:, 0:ntk], cv[:, 0:ntk], AF.Silu, [bcc[ci]], [bsgl[c % 2]]))(cv, ci, c, ntk))
                        else:
                            tails.append((lambda cv, ci, c, ntk: lambda: tt("dve", actT[:, c, 0:ntk], cv[:, 0:ntk], sgl[c % 2][:, 0:ntk], ALU.mult, [bcc[ci], bsgl[c % 2]], [bactT[c]]))(cv, ci, c, ntk))
                        while len(tails) > 2:
                            tails.pop(0)()
                    if nxt and c in (1, 3, 5, 7) and (c // 2) < len(nxt):
                        prepB(nxt[c // 2], (c // 2) % 2)
                if sti == len(sts) - 1:
                    if half == 0:
                        load_wfi(1)
                        pref["wfi"] = True
                    elif not has_samp:
                        if pref.get("win_hi"):
                            load_win(0, 6)
                        else:
                            load_win()
                        pref["win"] = True
                while tails:
                    tails.pop(0)()
                for ti, (slot, gi) in enumerate(stl):
                    Xj = X[:, slot, :]
                    for hf in range(2):
                        bk = 4 + (2 * ti + hf) % 3
                        for c in range(11):
                            mm(bank(bk)[:, 0:512], actT[:, c, ti * 128:(ti + 1) * 128], Wfo[:, c, hf * 512:(hf + 1) * 512],
                               c == 0, c == 10, [bactT[c], bWfo], [bB[bk]])
                        tt("dve", Xj[:, hf * 512:(hf + 1) * 512], bank(bk)[:, 0:512], Xj[:, hf * 512:(hf + 1) * 512], ALU.add,
                           [bB[bk], bX[slot]], [bX[slot]])
                    if half == 1:
                        yi = ycnt[0] % 2
                        ycnt[0] += 1
                        act(xn[:], Xj, AF.Square, [bX[slot]], [bxn, bst], accum=st[:, 4:5])
                        ts("dve", st[:, 5:6], st[:, 4:5], 1.0 / 1024, EPS, ALU.mult, ALU.add, [bst], [bst])
                        tt("pool", st[:, 6:7], st[:, 5:6], mhalf, ALU.pow, [bst, bcst], [bst])
                        stt(yb[yi][:], Xj, st[:, 6:7], wnf[:], ALU.mult, ALU.mult, [bX[slot], bst, bwnf], [byb[yi]])
                        dma(y_d[gi * 128:(gi + 1) * 128, :], yb[yi][:], [byb[yi]], [], byb[yi])
                last_prompt = (not samp) and stl[-1][1] == 15
                if last_prompt or samp:
                    ncols = 32 if samp else 2
                    for part in range(2):
                        for c in range(11):
                            ft = part * 11 + c
                            gft = part * 22 + half * 11 + c
                            bk = 4 + (ft % 3)
                            if samp:
                                src, rdb = ncT[:, ft, :], [bncT]
                            else:
                                src, rdb = carry[:, cpar[gft], gft, :], [bcarry[cpar[gft]][gft]]
                            tr(bank(bk)[0:ncols, 0:128], src, idf[:], rdb + [bidf], [bB[bk]])
                            cp("act", sc32[0:ncols, ft * 128:(ft + 1) * 128], bank(bk)[0:ncols, 0:128], [bB[bk]], [bsc32])
                    outd = ncs_d if samp else ncp_d
                    for part in range(2):
                        c0_ = part * DFF + half * 1408
                        dma(outd[:, c0_:c0_ + 1408], sc32[0:ncols, part * 1408:(part + 1) * 1408], [bsc32], [], bsc32)

    mem_mv = mem.alloc("mv", [128, 64], F32); bmv = Buf("mv")
    m0 = mem.mark()
    groups = [
        [(i, i) for i in range(8)],
        [(i - 8, i) for i in range(8, 17)],
    ]
    if "groups" in dbg:
        groups = dbg["groups"]
    try:
        ck(0)
        for tiles in groups:
            mem.reset(m0)
            if not dbg.get("skip1"):
                phase1(tiles)
            barrier()
            mem.reset(m0)
            if not dbg.get("skip2"):
                phase2(tiles)
            barrier()
    except _Stop:
        pass
    P.emit()
    return nc, mem.peak


def _consts():
    bf = ml_dtypes.bfloat16
    s = np.arange(128)
    ident = np.eye(128, dtype=np.float32)
    maskP = (s[:, None] <= s[None, :]).astype(np.float32)
    maskS = ((s[:, None] <= s[None, :]) & ((s[:, None] // 8) == (s[None, :] // 8))).astype(np.float32)
    cbf = np.concatenate([ident, maskP, maskS], axis=1).astype(bf)
    j = np.arange(16)
    smt = np.broadcast_to(((s[:, None] // 8) == j[None, :])[:, :, None], (128, 16, 128)).astype(bf).reshape(128, 2048)
    smb = np.broadcast_to(((s[None, :] // 8) == j[:, None])[None, :, :], (128, 16, 128)).astype(bf).reshape(128, 2048)
    half = 64
    inv = (1.0 / (np.float32(10000.0) ** (np.arange(half, dtype=np.float32) / np.float32(half)))).astype(np.float32)
    rope = np.zeros((NT, 128, 4, 4, 64), dtype=np.float32)
    for gi in range(NT):
        if gi < 16:
            pos = (gi * 128 + s).astype(np.float32)
            tau = s
        else:
            pos = (PAST_LEN + (s % 8)).astype(np.float32)
            tau = s % 8
        ang = (pos[:, None] * inv[None, :]).astype(np.float32).astype(np.float64)
        cs, sn = np.cos(ang), np.sin(ang)
        for h in range(4):
            dq = GAM[h] ** (tau + 1.0)
            dk = GAM[h] ** (-(tau + 1.0)) * (128.0 ** -0.5)
            rope[gi, :, 0, h] = cs * dq[:, None]
            rope[gi, :, 1, h] = sn * dq[:, None]
            rope[gi, :, 2, h] = cs * dk[:, None]
            rope[gi, :, 3, h] = sn * dk[:, None]
    rope = rope.reshape(NT, 128, 1024)
    return cbf, smt, smb, rope, ident


_CACHE = {}


def kernel(x_prompt, x_sample, state_hgrn, state_ret, state_conv, w_norm1, w_in, hgrn_lb,
           hgrn_norm_w, ret_norm_w, w_out, w_norm2, w_ffn_in, conv_w, conv_b, w_ffn_out, w_norm_f):
    f32 = np.float32
    if "nc" not in _CACHE:
        _CACHE["nc"] = build_program()[0]
        _CACHE["consts"] = _consts()
    nc = _CACHE["nc"]
    cbf, smt, smb, rope, ident = _CACHE["consts"]

    small = np.zeros((128, 64), dtype=f32)
    small[:, 0:8] = np.asarray(w_norm1[0], f32).reshape(8, 128).T
    small[:, 8:16] = np.asarray(w_norm2[0], f32).reshape(8, 128).T
    small[:, 16] = np.asarray(hgrn_norm_w[0], f32)
    small[:, 17] = np.asarray(ret_norm_w[0], f32)
    small[:, 18:22] = np.asarray(hgrn_lb[0], f32).reshape(4, 128).T
    small[:, 22:26] = np.asarray(hgrn_lb[1], f32).reshape(4, 128).T
    convw = np.zeros((128, 176), dtype=f32)
    for jj in range(3):
        convw[:, jj * 44:(jj + 1) * 44] = np.asarray(conv_w[0, jj], f32).reshape(44, 128).T
    convw[:, 132:176] = np.asarray(conv_b[0], f32).reshape(44, 128).T
    wnf = np.ascontiguousarray(np.broadcast_to(np.asarray(w_norm_f, f32)[None, :], (128, 1024)))

    shared = {
        "w_in": np.ascontiguousarray(w_in[0], dtype=f32), "w_out": np.ascontiguousarray(w_out[0], dtype=f32),
        "w_fi": np.ascontiguousarray(w_ffn_in[0], dtype=f32), "w_fo": np.ascontiguousarray(w_ffn_out[0], dtype=f32),
        "small": small, "convw": convw, "wnf": wnf, "rope": rope, "cbf": cbf, "smt": smt, "smb": smb, "idf": ident,
    }
    in_maps = []
    for c in range(8):
        xs = np.concatenate([np.asarray(x_prompt[c], f32), np.asarray(x_sample[16 * c:16 * c + 16], f32).reshape(128, 1024)], axis=0)
        m = dict(shared)
        m["x"] = np.ascontiguousarray(xs)
        m["sh"] = np.ascontiguousarray(state_hgrn[0, 16 * c:16 * c + 16], dtype=f32)
        m["sr"] = np.ascontiguousarray(state_ret[0, 16 * c:16 * c + 16], dtype=f32)
        m["sc"] = np.ascontiguousarray(np.asarray(state_conv[0, 16 * c:16 * c + 16], f32).reshape(32, 2 * DFF))
        in_maps.append(m)
    res = run_bass_kernel_spmd(nc, in_maps, core_ids=list(range(8)))
    R = res.results
    y_prompt = np.stack([R[c]["y"][:2048] for c in range(8)], axis=0)
    y_sample = np.concatenate([R[c]["y"][2048:].reshape(16, 8, 1024) for c in range(8)], axis=0)
    ha_p = np.stack([R[c]["nhp"] for c in range(8)], axis=0)[None]
    rb_p = np.stack([R[c]["nrp"] for c in range(8)], axis=0)[None]
    cv_p = np.stack([R[c]["ncp"] for c in range(8)], axis=0)[None]
    ha_s = np.concatenate([R[c]["nhs"] for c in range(8)], axis=0)[None]
    rb_s = np.concatenate([R[c]["nrs"] for c in range(8)], axis=0)[None]
    cv_s = np.concatenate([R[c]["ncs"].reshape(16, 2, 2 * DFF) for c in range(8)], axis=0)[None]
    return (y_prompt.astype(f32), y_sample.astype(f32), ha_p.astype(f32), rb_p.astype(f32), cv_p.astype(f32),
            ha_s.astype(f32), rb_s.astype(f32), cv_s.astype(f32))
```

```python
import contextlib
import numpy as np
import ml_dtypes
import concourse.bass as bass
import concourse.mybir as mybir
from concourse.bass_utils import run_bass_kernel_spmd

F32 = mybir.dt.float32
BF16 = mybir.dt.bfloat16
AF = mybir.ActivationFunctionType
ALU = mybir.AluOpType

ENGS = ["pe", "act", "dve", "pool", "sp"]
SKIP_SAME_ENGINE_WAW = False
EPS = 1e-6
NT = 17
DFF = 2816
NFT = 22
PAST_LEN = 16384


class Buf:
    __slots__ = ("name", "w", "r", "sem", "semcnt", "excl")

    def __init__(self, name="", excl=False):
        self.name = name
        self.excl = excl
        self.w = None
        self.r = []
        self.sem = None
        self.semcnt = 0


class Op:
    __slots__ = ("eng", "fn", "deps", "dma", "inc", "incval", "dmasem", "dmaval", "dmadeps")

    def __init__(self, eng, fn, dma):
        self.eng = eng
        self.fn = fn
        self.deps = []
        self.dmadeps = {}
        self.dma = dma
        self.inc = False
        self.incval = 0
        self.dmasem = None
        self.dmaval = 0


class Prog:
    def __init__(self, nc):
        self.nc = nc
        self.ops = {e: [] for e in ENGS}
        self.dma_bufs = []
        self.all_ops = []
        self.pending = {e: [] for e in ENGS}
        self.recent_dmas = []

    def _add(self, eng, fn, reads, writes, dma=False, key=None):
        op = Op(eng, fn, dma)
        deps = list(self.pending[eng])
        self.pending[eng] = []
        for b in reads:
            if b.w is not None:
                deps.append(b.w)
            if b.excl:
                deps.extend(r for r in b.r if r.eng != eng)
        for b in writes:
            if b.w is not None and not (dma and b.w.dma) and (not SKIP_SAME_ENGINE_WAW or b.w.eng != eng or b.w.dma or dma):
                deps.append(b.w)
            deps.extend(r for r in b.r if (not SKIP_SAME_ENGINE_WAW or r.eng != eng or r.dma or dma))
        seen = set()
        for d in deps:
            if id(d) in seen:
                continue
            seen.add(id(d))
            if d.dma:
                kb = d.dmasem
                op.dmadeps[id(kb)] = (kb, kb.semcnt)
            else:
                op.deps.append(d)
        for b in reads:
            b.r.append(op)
        for b in writes:
            b.w = op
            b.r = []
        if dma:
            if key.sem is None:
                key.sem = len(self.dma_bufs)
                self.dma_bufs.append(key)
            key.semcnt += 16
            op.dmasem = key
            op.dmaval = key.semcnt
            self.recent_dmas.append(op)
        self.ops[eng].append(op)
        self.all_ops.append(op)
        return op

    def op(self, eng, fn, reads=(), writes=()):
        return self._add(eng, fn, list(reads), list(writes))

    def dma(self, eng, fn, reads=(), writes=(), key=None):
        return self._add(eng, fn, list(reads), list(writes), dma=True, key=key)

    def barrier(self, markers):
        mops = []
        for e, fn in markers.items():
            mops.append(self.op(e, fn, writes=[Buf("bar")]))
        for e in ENGS:
            self.pending[e] = self.pending[e] + mops + self.recent_dmas
        self.recent_dmas = []

    def emit(self):
        nc = self.nc
        for op in self.all_ops:
            for d in op.deps:
                if d.eng == op.eng and op.eng in ("pe", "sp") and not op.dma:
                    continue
                d.inc = True
        for e in ENGS:
            c = 0
            for op in self.ops[e]:
                if op.inc and not op.dma:
                    c += 1
                    op.incval = c
        with contextlib.ExitStack() as st:
            esem = {e: st.enter_context(nc.semaphore("s_" + e)) for e in ENGS}
            dsem = [st.enter_context(nc.semaphore("d%d" % i)) for i in range(len(self.dma_bufs))]
            block = st.enter_context(nc.Block())
            engobj = {"pe": "tensor", "act": "scalar", "dve": "vector", "pool": "gpsimd", "sp": "sync"}
            ops = self.ops
            dma_bufs = self.dma_bufs

            def make(e):
                def body(eng):
                    waited = {}
                    for op in ops[e]:
                        need = {}
                        for (kb, v) in op.dmadeps.values():
                            need[("d", kb.sem)] = (dsem[kb.sem], v)
                        for d in op.deps:
                            if not d.inc:
                                continue
                            if d.eng == e and e in ("pe", "sp") and not op.dma:
                                continue
                            k = ("e", d.eng)
                            if k not in need or need[k][1] < d.incval:
                                need[k] = (esem[d.eng], d.incval)
                        for k, (s, v) in need.items():
                            if waited.get(k, 0) >= v:
                                continue
                            eng.wait_ge(s, v)
                            waited[k] = v
                        ins = op.fn(eng)
                        if op.dma:
                            ins.then_inc(dsem[op.dmasem.sem], 16)
                        elif op.inc:
                            ins.then_inc(esem[e], 1)
                    if e == "sp":
                        for b in dma_bufs:
                            eng.wait_ge(dsem[b.sem], b.semcnt)
                        for e2 in ENGS:
                            if e2 == "sp":
                                continue
                            tot = sum(1 for o in ops[e2] if o.inc and not o.dma)
                            if tot:
                                eng.wait_ge(esem[e2], tot)
                return body

            for e in ENGS:
                getattr(block, engobj[e])(make(e))


class Mem:
    def __init__(self, nc, base=16640, limit=229312):
        self.nc = nc
        self.off = base
        self.limit = limit
        self.n = 0
        self.peak = 0

    def alloc(self, name, shape, dtype):
        sz = int(np.prod(shape[1:])) * mybir.dt.size(dtype)
        off = (self.off + 63) // 64 * 64
        self.n += 1
        t = self.nc.alloc_sbuf_tensor_at("%s_%d" % (name, self.n), list(shape), dtype, offset=off)
        self.off = off + sz
        self.peak = max(self.peak, self.off)
        assert self.off <= self.limit, (name, self.off, self.limit)
        return t

    def mark(self):
        return self.off

    def reset(self, m):
        self.off = m


GAM = [1.0 - 2.0 ** (-5 - h) for h in range(4)]


class _Stop(Exception):
    pass


def build_program(dbg=None):
    dbg = dbg or {}

    def ck(n):
        if dbg.get("stop") == n:
            raise _Stop()

    nc = bass.Bass("TRN2", target_bir_lowering=False)

    def din(name, shape, dt=F32):
        return nc.dram_tensor(name, list(shape), dt, kind="ExternalInput").ap()

    def dout(name, shape, dt=F32):
        return nc.dram_tensor(name, list(shape), dt, kind="ExternalOutput").ap()

    x_d = din("x", [NT * 128, 1024])
    sh_d = din("sh", [16, 4, 128, 128])
    sr_d = din("sr", [16, 4, 128, 128])
    sc_d = din("sc", [32, 2 * DFF])
    w_in_d = din("w_in", [1024, 4096])
    w_out_d = din("w_out", [1024, 1024])
    w_fi_d = din("w_fi", [1024, 2 * DFF])
    w_fo_d = din("w_fo", [DFF, 1024])
    small_d = din("small", [128, 64])
    convw_d = din("convw", [128, 4 * 44])
    wnf_d = din("wnf", [128, 1024])
    rope_d = din("rope", [NT, 128, 1024])
    cbf_d = din("cbf", [128, 128 * 3], BF16)
    smt_d = din("smt", [128, 2048], BF16)
    smb_d = din("smb", [128, 2048], BF16)
    idf_d = din("idf", [128, 128])

    y_d = dout("y", [NT * 128, 1024])
    nhp_d = dout("nhp", [4, 128, 128])
    nrp_d = dout("nrp", [4, 128, 128])
    ncp_d = dout("ncp", [2, 2 * DFF])
    nhs_d = dout("nhs", [16, 4, 128, 128])
    nrs_d = dout("nrs", [16, 4, 128, 128])
    ncs_d = dout("ncs", [32, 2 * DFF])

    mem = Mem(nc)
    P = Prog(nc)

    XS = 9
    X = mem.alloc("X", [128, XS, 1024], F32)
    bX = [Buf("X%d" % i) for i in range(XS)]
    small = mem.alloc("small", [128, 64], F32); bsmall = Buf("small")
    convw = mem.alloc("convw", [128, 176], F32); bconvw = Buf("convw")
    cbf = mem.alloc("cbf", [128, 384], BF16); bcbf = Buf("cbf")
    idf = mem.alloc("idf", [128, 128], F32); bidf = Buf("idf")
    cst = mem.alloc("cst", [128, 32], F32); bcst = Buf("cst")
    S = mem.alloc("S", [128, 8, 128], F32)
    Sb = mem.alloc("Sb", [128, 8, 128], BF16)
    bS = [Buf("S%d" % h) for h in range(8)]
    bSb = [Buf("Sb%d" % h) for h in range(8)]
    carry = mem.alloc("carry", [128, 2, 44, 2], F32)
    bcarry = [[Buf("cy%d_%d" % (q, i)) for i in range(44)] for q in range(2)]
    cpar = [0] * 44
    scrA = mem.alloc("scrA", [128, 2], F32)
    scrD = mem.alloc("scrD", [128, 2], F32)
    scrP = mem.alloc("scrP", [128, 2], F32)
    zeros = mem.alloc("zeros", [128, 128], F32); bzeros = Buf("zeros")
    ra_off = (mem.off + 63) // 64 * 64
    Win = nc.alloc_sbuf_tensor_at("RA_win", [128, 8, 4096], BF16, offset=ra_off)
    Wfi = nc.alloc_sbuf_tensor_at("RA_wfi", [128, 8, 2816], BF16, offset=ra_off)
    mem.off = ra_off + 65536
    mem.peak = max(mem.peak, mem.off)
    bRA = Buf("RA")
    bWin = bRA
    bWfi = bRA
    bWinHi = Buf("WinHi")
    pref = {"win": False, "wfi": False}

    def load_win(k0=0, k1=8, buf=None):
        buf = buf or bRA
        for (ca, cb_) in [(0, 1024), (2048, 3072), (1024, 2048), (3072, 4096)]:
            o_ = P.dma("pool", (lambda ca, cb_: lambda e: e.dma_start(
                out=Win[:, k0:k1, ca:cb_], in_=w_in_d[k0 * 128:k1 * 128, ca:cb_].rearrange("(k p) n -> p k n", p=128),
                max_dma_last_dim=8192))(ca, cb_), [], [buf], buf)
            P.recent_dmas.remove(o_)

    def load_wfi(half):
        for k in range(8):
            for part in (1, 0):
                c0_ = part * DFF + half * 1408
                o_ = P.dma("pool", (lambda k, part, c0_: lambda e: e.dma_start(
                    out=Wfi[:, k, part * 1408:(part + 1) * 1408], in_=w_fi_d[k * 128:(k + 1) * 128, c0_:c0_ + 1408],
                    max_dma_last_dim=8192))(k, part, c0_), [], [bRA], bRA)
                P.recent_dmas.remove(o_)

    identb = cbf[:, 0:128]
    maskP = cbf[:, 128:256]
    maskS = cbf[:, 256:384]
    wn1 = small[:, 0:8]
    wn2 = small[:, 8:16]
    nwa = small[:, 16:17]
    nwb = small[:, 17:18]
    lb0 = small[:, 18:22]
    lb1 = small[:, 22:26]
    c0 = cst[:, 0:4]
    c1 = cst[:, 4:8]
    nc1 = cst[:, 8:12]
    mhalf = cst[:, 12:13]

    PS = nc.alloc_psum_tensor("ps", [128, 4096], F32)
    PSb16 = PS[:].bitcast(BF16)
    bB = [Buf("B%d" % i, excl=True) for i in range(8)]

    def bank(i):
        return PS[:, i * 512:(i + 1) * 512]

    def bankb(i):
        return PSb16[:, i * 1024:(i + 1) * 1024]

    def mm(out, lhsT, rhs, start, stop, reads, writes):
        P.op("pe", lambda e: e.matmul(out, lhsT=lhsT, rhs=rhs, start=start, stop=stop), reads, writes)

    def tr(out, in_, ident, reads, writes):
        P.op("pe", lambda e: e.transpose(out=out, in_=in_, identity=ident), reads, writes)

    def act(out, in_, func, reads, writes, scale=None, bias=None, accum=None):
        kw = {}
        if scale is not None:
            kw["scale"] = scale
        if bias is not None:
            kw["bias"] = bias
        if accum is not None:
            kw["accum_out"] = accum
        P.op("act", lambda e: e.activation(out=out, in_=in_, func=func, **kw), reads, writes)

    def ts(eng, out, in0, s1, s2, op0, op1, reads, writes):
        if op1 is None:
            P.op(eng, lambda e: e.tensor_scalar(out=out, in0=in0, scalar1=s1, scalar2=None, op0=op0), reads, writes)
        else:
            P.op(eng, lambda e: e.tensor_scalar(out=out, in0=in0, scalar1=s1, scalar2=s2, op0=op0, op1=op1), reads, writes)

    def tt(eng, out, in0, in1, op, reads, writes):
        P.op(eng, lambda e: e.tensor_tensor(out=out, in0=in0, in1=in1, op=op), reads, writes)

    def stt(out, in0, scalar, in1, op0, op1, reads, writes):
        P.op("dve", lambda e: e.scalar_tensor_tensor(out=out, in0=in0, scalar=scalar, in1=in1, op0=op0, op1=op1), reads, writes)

    def cp(eng, out, in_, reads, writes):
        if eng == "act":
            P.op("act", lambda e: e.activation(out=out, in_=in_, func=AF.Copy), reads, writes)
        else:
            P.op(eng, lambda e: e.tensor_copy(out=out, in_=in_), reads, writes)

    def dma(out, in_, reads, writes, key, eng="sp", slow=False):
        if slow:
            P.dma(eng, lambda e: e.dma_start(out=out, in_=in_, allow_slow_non_contiguous=True), reads, writes, key)
        else:
            P.dma(eng, lambda e: e.dma_start(out=out, in_=in_), reads, writes, key)

    def dma_cast(out, in_, writes, key):
        P.dma("pool", lambda e: e.dma_start(out=out, in_=in_, max_dma_last_dim=8192), [], writes, key)

    def barrier():
        P.barrier({
            "act": lambda e: e.activation(out=scrA[:, 0:1], in_=scrA[:, 1:2], func=AF.Copy),
            "dve": lambda e: e.tensor_copy(out=scrD[:, 0:1], in_=scrD[:, 1:2]),
            "pool": lambda e: e.memset(scrP[:, 0:1], 0.0),
        })

    dma(small[:], small_d, [], [bsmall], bsmall)
    dma(convw[:], convw_d, [], [bconvw], bconvw)
    dma(cbf[:], cbf_d, [], [bcbf], bcbf)
    dma(idf[:], idf_d, [], [bidf], bidf)
    P.op("pool", lambda e: e.memset(zeros[:], 0.0), [], [bzeros])
    P.op("pool", lambda e: e.memset(scrA[:], 0.0), [], [])
    P.op("pool", lambda e: e.memset(scrD[:], 0.0), [], [])
    P.op("pool", lambda e: e.memset(scrP[:], 0.0), [], [])
    P.op("pool", lambda e: e.memset(cst[:], 0.0), [], [bcst])
    P.op("pool", lambda e: e.memset(cst[:, 12:13], -0.5), [], [bcst])
    for h in range(8):
        P.op("pool", (lambda hh: lambda e: e.memset(S[:, hh, :], 0.0))(h), [], [bS[h]])
        P.op("pool", (lambda hh: lambda e: e.memset(Sb[:, hh, :], 0.0))(h), [], [bSb[h]])
    P.op("pool", lambda e: e.memset(carry[:].rearrange("p a b c -> p (a b c)"), 0.0), [], [b for q in range(2) for b in bcarry[q]])
    tt("dve", cst[:, 24:28], lb0, lb1, ALU.subtract, [bsmall, bcst], [bcst])
    act(cst[:, 28:32], cst[:, 24:28], AF.Tanh, [bcst], [bcst], scale=0.5)
    ts("dve", c0, cst[:, 28:32], 0.25, 0.75, ALU.mult, ALU.add, [bcst], [bcst])
    ts("dve", c1, cst[:, 28:32], -0.25, 0.25, ALU.mult, ALU.add, [bcst], [bcst])
    ts("dve", nc1, cst[:, 28:32], 0.25, -0.25, ALU.mult, ALU.add, [bcst], [bcst])

    m0 = mem.mark()
    cast_rr = [0]

    def cast_scaled(out, in_, scal, reads, writes):
        i = cast_rr[0]
        cast_rr[0] += 1
        eng = ("dve", "pool", "act")[i % 3]
        if scal is None:
            cp(eng, out, in_, reads, writes)
        elif eng == "act":
            act(out, in_, AF.Identity, reads, writes, scale=scal)
        else:
            ts(eng, out, in_, scal, None, ALU.mult, None, reads, writes)

    def phase1(tiles):
        ntl = len(tiles)
        has_samp = any(gi == 16 for (_, gi) in tiles)
        Wout = mem.alloc("Wout", [128, 8, 1024], BF16); bWout = Buf("Wout")
        if not pref["win"]:
            load_win()
        pref["win"] = False
        dma_cast(Wout[:], w_out_d.rearrange("(k p) n -> p k n", p=128), [bWout], bWout)
        ck(1)
        rope1 = mem.alloc("rope", [128, 4, 256], F32); rope = [rope1, rope1]; brope1 = Buf("rope"); brope = [brope1, brope1]
        xn = mem.alloc("xn", [128, 1024], BF16); bxn = Buf("xn")
        hT = mem.alloc("hT", [128, 8, 128], BF16); bhT = Buf("hT")
        st = mem.alloc("st", [128, 16], F32); bst = Buf("st")
        th = mem.alloc("th", [128, 4, 128], F32); bth = Buf("th")
        ff = mem.alloc("ff", [128, 128], F32); bff = Buf("ff")
        kk = mem.alloc("kk", [128, 128], F32); bkk = Buf("kk")
        RR = mem.alloc("RR", [128, 128], F32); bRR = Buf("RR")
        qtok = mem.alloc("qtok", [128, 4, 128], BF16); bqtok = Buf("qtok")
        khT = mem.alloc("khT", [128, 4, 128], BF16); bkhT = [Buf("khT%d" % h) for h in range(4)]
        vvh = [mem.alloc("vvh", [128, 4, 128], BF16) for _ in range(2)]; bvvh = [Buf("vvh0"), Buf("vvh1")]
        rt = [mem.alloc("rt", [128, 256], BF16) for _ in range(8)]; brt = [Buf("rt%d" % i) for i in range(8)]
        scm = mem.alloc("scm", [128, 8, 128], BF16); bscm = [Buf("scm%d" % h) for h in range(8)]
        og = mem.alloc("og", [128, 1024], BF16); bog = Buf("og")
        on = og; bon = bog
        ogT = mem.alloc("ogT", [128, 8, 128], BF16); bogT = Buf("ogT")
        Pc = [mem.alloc("Pc", [128, 4, 128], F32) for _ in range(2)]
        bPc = [[Buf("Pc%d_%d" % (p, h)) for h in range(4)] for p in range(2)]
        qT = [mem.alloc("qT", [128, 8, 128], BF16) for _ in range(2)]
        bqT = [[Buf("qT") for h in range(8)] for p in range(2)]
        kT = [mem.alloc("kT", [128, 8, 128], BF16) for _ in range(2)]
        bkT = [[Buf("kT") for h in range(8)] for p in range(2)]
        ktok = [mem.alloc("ktok", [128, 8, 128], BF16) for _ in range(2)]
        bktok = [[Buf("ktok") for h in range(8)] for p in range(2)]
        vv = [mem.alloc("vv", [128, 8, 128], BF16) for _ in range(2)]
        bvv = [[Buf("vva"), Buf("vvb")] for p in range(2)]
        sg = [mem.alloc("sg", [128, 8, 128], BF16) for _ in range(2)]
        bsg = [[Buf("sga"), Buf("sgb")] for p in range(2)]
        if has_samp:
            smt = mem.alloc("smt", [128, 16, 128], BF16); bsmt = Buf("smt")
            smb = mem.alloc("smb", [128, 16, 128], BF16); bsmb = Buf("smb")
            dma(smt[:].rearrange("p j v -> p (j v)"), smt_d, [], [bsmt], bsmt)
            dma(smb[:].rearrange("p j v -> p (j v)"), smb_d, [], [bsmb], bsmb)
            Sin = [mem.alloc("Sin", [128, 4, 128], F32) for _ in range(4)]; bSin = [Buf("Sin%d" % i) for i in range(4)]
            Sinb = [mem.alloc("Sinb", [128, 4, 128], BF16) for _ in range(4)]; bSinb = [Buf("Sinb%d" % i) for i in range(4)]
            vblk = [mem.alloc("vblk", [128, 4, 128], BF16) for _ in range(4)]; bvblk = [Buf("vblk%d" % i) for i in range(4)]
            qblk = [mem.alloc("qblk", [128, 4, 128], BF16) for _ in range(4)]; bqblk = [Buf("qblk%d" % i) for i in range(4)]
        wn1b = wn1.rearrange("p (k o) -> p k o", o=1).broadcast_to([128, 8, 128])
        has_first = any(gi == 0 for (_, gi) in tiles)
        if has_first:
            q32 = mem.alloc("q32", [128, 8, 128], F32); bq32 = [Buf("q32_%d" % h) for h in range(8)]
            k32 = mem.alloc("k32", [128, 8, 128], F32); bk32 = [Buf("k32_%d" % h) for h in range(8)]
            q32tok = mem.alloc("q32tok", [128, 4, 128], F32); bq32tok = Buf("q32tok")
            k32tok = mem.alloc("k32tok", [128, 4, 128], F32); bk32tok = Buf("k32tok")
            rtf = [mem.alloc("rtf", [128, 256], F32) for _ in range(4)]; brtf = [Buf("rtf%d" % i) for i in range(4)]

        def front(idx):
            slot, gi = tiles[idx]
            p = idx % 2
            samp = (gi == 16)
            Xj = X[:, slot, :]
            rp = rope[p]

            def F1a():
                dma(Xj, x_d[gi * 128:(gi + 1) * 128, :], [], [bX[slot]], bX[slot])
                dma(rp[:].rearrange("p a c -> p (a c)"), rope_d[gi], [], [brope[p]], brope[p])
                act(xn[:], Xj, AF.Square, [bX[slot]], [bxn, bst], accum=st[:, 0:1])
                ts("dve", st[:, 1:2], st[:, 0:1], 1.0 / 1024, EPS, ALU.mult, ALU.add, [bst], [bst])
                tt("pool", st[:, 2:3], st[:, 1:2], mhalf, ALU.pow, [bst, bcst], [bst])
                act(xn[:], Xj, AF.Identity, [bX[slot], bst], [bxn], scale=st[:, 2:3])

            def F1b():
                for k in range(8):
                    tr(bankb(0)[:, k * 128:(k + 1) * 128], xn[:, k * 128:(k + 1) * 128], identb, [bxn, bcbf], [bB[0]])
                tt("dve", hT[:], bankb(0)[:, 0:1024].rearrange("p (k t) -> p k t", k=8), wn1b, ALU.mult, [bB[0], bsmall], [bhT])

            def F2():
                for c in (4, 5, 6, 7, 0, 1, 2, 3):
                    bk_ = 1 if c < 4 else 2
                    hh = c % 4
                    for k in range(8):
                        mm(bank(bk_)[:, hh * 128:(hh + 1) * 128], Win[:, k, c * 128:(c + 1) * 128], hT[:, k, :],
                           k == 0, k == 7, [bWin, bWinHi, bhT], [bB[bk_]])
                act(th[:].rearrange("p h t -> p (h t)"), bank(2)[:, 0:512], AF.Tanh, [bB[2]], [bth], scale=0.5)
                nseg = 16 if samp else 1
                seglen = 128 // nseg
                for h in range(4):
                    qa = bank(1)[:, h * 128:(h + 1) * 128]
                    act(ff[:], th[:, h, :], AF.Identity, [bth, bcst], [bff], scale=c1[:, h:h + 1], bias=c0[:, h:h + 1])
                    act(kk[:], th[:, h, :], AF.Identity, [bth, bcst], [bkk], scale=nc1[:, h:h + 1], bias=c1[:, h:h + 1])
                    for sgi in range(nseg):
                        a, b_ = sgi * seglen, (sgi + 1) * seglen
                        P.op("dve", (lambda a, b_, h: lambda e: e.tensor_tensor_scan(
                            out=Pc[p][:, h, a:b_], data0=ff[:, a:b_], data1=zeros[:, a:b_], initial=1.0,
                            op0=ALU.mult, op1=ALU.add))(a, b_, h), [bff, bzeros], [bPc[p][h]])
                    P.op("dve", (lambda h: lambda e: e.reciprocal(out=RR[:], in_=Pc[p][:, h, :]))(h), [bPc[p][h]], [bRR])
                    tt("dve", qT[p][:, h, :], qa, Pc[p][:, h, :], ALU.mult, [bB[1], bPc[p][h]], [bqT[p][h]])
                    tt("dve", kT[p][:, h, :], kk[:], RR[:], ALU.mult, [bkk, bRR], [bkT[p][h]])
                    if gi == 0:
                        tt("dve", q32[:, h, :], qa, Pc[p][:, h, :], ALU.mult, [bB[1], bPc[p][h]], [bq32[h]])
                        tt("dve", k32[:, h, :], kk[:], RR[:], ALU.mult, [bkk, bRR], [bk32[h]])
                    if not samp:
                        ts("dve", khT[:, h, :], kT[p][:, h, :], Pc[p][:, h, 127:128], None, ALU.mult, None, [bkT[p][h], bPc[p][h]], [bkhT[h]])

            def F3():
                for (c0_, bk_) in [(2048, 3), (2560, 0)]:
                    for k in range(8):
                        mm(bank(bk_)[:, 0:512], hT[:, k, :], Win[:, k, c0_:c0_ + 512], k == 0, k == 7, [bWin, bWinHi, bhT], [bB[bk_]])
                for h in range(4):
                    if samp:
                        tr(bankb(2)[:, h * 128:(h + 1) * 128], kT[p][:, h, :], identb, [bkT[p][h], bcbf], [bB[2]])
                    else:
                        tr(bankb(2)[:, h * 128:(h + 1) * 128], khT[:, h, :], identb, [bkhT[h], bcbf], [bB[2]])
                for h in range(4):
                    cp("act", ktok[p][:, h, :], bankb(2)[:, h * 128:(h + 1) * 128], [bB[2]], [bktok[p][h]])
                for (bk_, ci, si_, isq) in [(3, 0, 1, True), (0, 2, 3, False)]:
                    src = bank(bk_)[:, 0:512].rearrange("p (h d) -> p h d", h=4)
                    x1 = src[:, :, 0:64]
                    x2 = src[:, :, 64:128]
                    cs = rp[:, ci, :].rearrange("p (h d) -> p h d", h=4)
                    sn = rp[:, si_, :].rearrange("p (h d) -> p h d", h=4)
                    ro = 0 if isq else 4
                    tv = [rt[ro + i][:].rearrange("p (h d) -> p h d", h=4) for i in range(4)]
                    if isq:
                        d1, d2, bdst = qtok[:, :, 0:64], qtok[:, :, 64:128], [bqtok]
                    else:
                        d1, d2, bdst = ktok[p][:, 4:8, 0:64], ktok[p][:, 4:8, 64:128], bktok[p][4:8]
                    tt("dve", tv[0], x1, cs, ALU.mult, [bB[bk_], brope[p]], [brt[ro]])
                    tt("dve", tv[1], x2, sn, ALU.mult, [bB[bk_], brope[p]], [brt[ro + 1]])
                    tt("dve", tv[2], x1, sn, ALU.mult, [bB[bk_], brope[p]], [brt[ro + 2]])
                    tt("dve", tv[3], x2, cs, ALU.mult, [bB[bk_], brope[p]], [brt[ro + 3]])
                    tt("pool", d1, tv[0], tv[1], ALU.subtract, [brt[ro], brt[ro + 1]], bdst)
                    tt("pool", d2, tv[2], tv[3], ALU.add, [brt[ro + 2], brt[ro + 3]], bdst)
                    if gi == 0:
                        tf = [rtf[i][:].rearrange("p (h d) -> p h d", h=4) for i in range(4)]
                        dst32, bd32 = (q32tok, bq32tok) if isq else (k32tok, bk32tok)
                        tt("dve", tf[0], x1, cs, ALU.mult, [bB[bk_], brope[p]], [brtf[0]])
                        tt("dve", tf[1], x2, sn, ALU.mult, [bB[bk_], brope[p]], [brtf[1]])
                        tt("dve", tf[2], x1, sn, ALU.mult, [bB[bk_], brope[p]], [brtf[2]])
                        tt("dve", tf[3], x2, cs, ALU.mult, [bB[bk_], brope[p]], [brtf[3]])
                        tt("dve", dst32[:, :, 0:64], tf[0], tf[1], ALU.subtract, [brtf[0], brtf[1]], [bd32])
                        tt("dve", dst32[:, :, 64:128], tf[2], tf[3], ALU.add, [brtf[2], brtf[3]], [bd32])

            def F4():
                for (c0_, bk_) in [(1024, 1), (3072, 2)]:
                    for k in range(8):
                        mm(bank(bk_)[:, 0:512], hT[:, k, :], Win[:, k, c0_:c0_ + 512], k == 0, k == 7, [bWin, bWinHi, bhT], [bB[bk_]])
                cp("act", vv[p][:, 0:4, :].rearrange("p h d -> p (h d)"), bank(1)[:, 0:512], [bB[1]], [bvv[p][0]])
                cp("act", vv[p][:, 4:8, :].rearrange("p h d -> p (h d)"), bank(2)[:, 0:512], [bB[2]], [bvv[p][1]])
                if not samp:
                    for h in range(4):
                        act(vvh[p][:, h, :], bank(2)[:, h * 128:(h + 1) * 128], AF.Copy, [bB[2]], [bvvh[p]], scale=float(GAM[h] ** 128))
                for (c0_, bk_) in [(1536, 3), (3584, 0)]:
                    for k in range(8):
                        mm(bank(bk_)[:, 0:512], hT[:, k, :], Win[:, k, c0_:c0_ + 512], k == 0, k == 7, [bWin, bWinHi, bhT], [bB[bk_]])
                act(sg[p][:, 0:4, :].rearrange("p h d -> p (h d)"), bank(3)[:, 0:512], AF.Silu, [bB[3]], [bsg[p][0]])
                act(sg[p][:, 4:8, :].rearrange("p h d -> p (h d)"), bank(0)[:, 0:512], AF.Silu, [bB[0]], [bsg[p][1]])
                for h in range(4):
                    tr(bankb(1)[:, h * 128:(h + 1) * 128], qtok[:, h, :], identb, [bqtok, bcbf], [bB[1]])
                    tr(bankb(1)[:, (4 + h) * 128:(5 + h) * 128], ktok[p][:, 4 + h, :], identb, [bktok[p][4 + h], bcbf], [bB[1]])
                for h in range(4):
                    cp("act", qT[p][:, 4 + h, :], bankb(1)[:, h * 128:(h + 1) * 128], [bB[1]], [bqT[p][4 + h]])
                    cp("act", kT[p][:, 4 + h, :], bankb(1)[:, (4 + h) * 128:(5 + h) * 128], [bB[1]], [bkT[p][4 + h]])
                if gi == 0:
                    for h in range(4):
                        tr(bank(2)[:, h * 128:(h + 1) * 128], q32tok[:, h, :], idf[:], [bq32tok, bidf], [bB[2]])
                        tr(bank(3)[:, h * 128:(h + 1) * 128], k32tok[:, h, :], idf[:], [bk32tok, bidf], [bB[3]])
                    for h in range(4):
                        cp("dve", q32[:, 4 + h, :], bank(2)[:, h * 128:(h + 1) * 128], [bB[2]], [bq32[4 + h]])
                        cp("dve", k32[:, 4 + h, :], bank(3)[:, h * 128:(h + 1) * 128], [bB[3]], [bk32[4 + h]])

            return [F1a, F1b, F2, F3, F4]

        def back(idx):
            slot, gi = tiles[idx]
            p = idx % 2
            samp = (gi == 16)
            Xj = X[:, slot, :]
            mask = maskS if samp else maskP

            def K1():
                for h in range(8):
                    bk_ = 4 + h // 4
                    sc = bank(bk_)[:, (h % 4) * 128:(h % 4 + 1) * 128]
                    if gi == 0:
                        mm(sc, k32[:, h, :], q32[:, h, :], True, True, [bk32[h], bq32[h]], [bB[bk_]])
                    else:
                        mm(sc, kT[p][:, h, :], qT[p][:, h, :], True, True, [bkT[p][h], bqT[p][h]], [bB[bk_]])
                for h in range(8):
                    bk_ = 4 + h // 4
                    sc = bank(bk_)[:, (h % 4) * 128:(h % 4 + 1) * 128]
                    tt("dve", scm[:, h, :], sc, mask, ALU.mult, [bB[bk_], bcbf], [bscm[h]])

            def K2():
                if not samp:
                    for h in range(8):
                        bo = 6 + h // 4
                        ov = bank(bo)[:, (h % 4) * 128:(h % 4 + 1) * 128]
                        mm(ov, scm[:, h, :], vv[p][:, h, :], True, False, [bscm[h], bvv[p][h // 4]], [bB[bo]])
                        mm(ov, qT[p][:, h, :], Sb[:, h, :], False, True, [bqT[p][h], bSb[h]], [bB[bo]])
                    for h in range(8):
                        bu = 4 + h // 4
                        uv = bank(bu)[:, (h % 4) * 128:(h % 4 + 1) * 128]
                        if h < 4:
                            mm(uv, ktok[p][:, h, :], vv[p][:, h, :], True, True, [bktok[p][h], bvv[p][0]], [bB[bu]])
                        else:
                            mm(uv, ktok[p][:, h, :], vvh[p][:, h - 4, :], True, True, [bktok[p][h], bvvh[p]], [bB[bu]])
                    for h in range(8):
                        bu = 4 + h // 4
                        uv = bank(bu)[:, (h % 4) * 128:(h % 4 + 1) * 128]
                        E = Pc[p][:, h, 127:128] if h < 4 else float(GAM[h - 4] ** 128)
                        rd = [bS[h], bB[bu]] + ([bPc[p][h]] if h < 4 else [])
                        stt(S[:, h, :], S[:, h, :], E, uv, ALU.mult, ALU.add, rd, [bS[h]])
                        cp("pool", Sb[:, h, :], S[:, h, :], [bS[h]], [bSb[h]])
                else:
                    def src_of(i):
                        h, q = i // 4, i % 4
                        st_d = sh_d if h < 4 else sr_d
                        return st_d[4 * q:4 * q + 4, h % 4].rearrange("j d v -> d j v")

                    def load(i):
                        sl = i % 4
                        sv = src_of(i)
                        dma(Sin[sl][:], sv, [], [bSin[sl]], bSin[sl])

                    def cast(i):
                        sl = i % 4
                        cp("act", Sinb[sl][:].rearrange("p j v -> p (j v)"), Sin[sl][:].rearrange("p j v -> p (j v)"), [bSin[sl]], [bSinb[sl]])

                    def pre(i):
                        h, q = i // 4, i % 4
                        sl = i % 4
                        tt("dve", qblk[sl][:], qT[p][:, h:h + 1, :].broadcast_to([128, 4, 128]), smb[:, 4 * q:4 * q + 4, :], ALU.mult,
                           [bqT[p][h], bsmb], [bqblk[sl]])
                        tt("dve", vblk[sl][:], vv[p][:, h:h + 1, :].broadcast_to([128, 4, 128]), smt[:, 4 * q:4 * q + 4, :], ALU.mult,
                           [bvv[p][h // 4], bsmt], [bvblk[sl]])

                    load(0)
                    load(1)
                    cast(0)
                    pre(0)
                    for i in range(32):
                        h, q = i // 4, i % 4
                        sl = i % 4
                        if i + 2 < 32:
                            load(i + 2)
                        if i + 1 < 32:
                            cast(i + 1)
                            pre(i + 1)
                        bo = 6 + h // 4
                        ov = bank(bo)[:, (h % 4) * 128:(h % 4 + 1) * 128]
                        ns_d = nhs_d if h < 4 else nrs_d
                        if q == 0:
                            mm(ov, scm[:, h, :], vv[p][:, h, :], True, False, [bscm[h], bvv[p][h // 4]], [bB[bo]])
                        for j in range(4):
                            mm(ov, qblk[sl][:, j, :], Sinb[sl][:, j, :], False, (q == 3 and j == 3), [bqblk[sl], bSinb[sl]], [bB[bo]])
                        mm(bank(q)[:, 0:512], ktok[p][:, h, :], vblk[sl][:].rearrange("p j v -> p (j v)"), True, True,
                           [bktok[p][h], bvblk[sl]], [bB[q]])
                        Sf = Sin[sl][:].rearrange("p j v -> p (j v)")
                        tt("dve", Sf, bank(q)[:, 0:512], Sf, ALU.add, [bB[q], bSin[sl]], [bSin[sl]])
                        if h < 4:
                            Eb = Pc[p][:, h, :].rearrange("p (j t) -> p j t", t=8)[:, 4 * q:4 * q + 4, 7:8].broadcast_to([128, 4, 128])
                            tt("dve", Sin[sl][:], Sin[sl][:], Eb, ALU.mult, [bSin[sl], bPc[p][h]], [bSin[sl]])
                        else:
                            act(Sf, Sf, AF.Identity, [bSin[sl]], [bSin[sl]], scale=float(GAM[h - 4] ** 8))
                        dma(ns_d[4 * q:4 * q + 4, h % 4].rearrange("j d v -> d j v"), Sin[sl][:], [bSin[sl]], [], bSin[sl])

            def K3a():
                mv = mem_mv
                o_all = PS[:, 6 * 512:8 * 512]
                act(sq[:], o_all, AF.Square, [bB[6], bB[7]], [bsq])
                P.op("dve", lambda e: e.tensor_reduce(out=mv[:, 0:8], in_=o_all.rearrange("p (h d) -> p h d", h=8),
                                                       axis=mybir.AxisListType.X, op=ALU.add), [bB[6], bB[7]], [bmv])
                P.op("dve", lambda e: e.tensor_reduce(out=mv[:, 8:16], in_=sq[:].rearrange("p (h d) -> p h d", h=8),
                                                       axis=mybir.AxisListType.X, op=ALU.add), [bsq], [bmv])
            def K3a1():
                mv = mem_mv
                o_all = PS[:, 6 * 512:8 * 512]
                ts("dve", mv[:, 16:24], mv[:, 0:8], 1.0 / 128, None, ALU.mult, None, [bmv], [bmv])
                ts("dve", mv[:, 24:32], mv[:, 8:16], 1.0 / 128, EPS, ALU.mult, ALU.add, [bmv], [bmv])
                tt("dve", mv[:, 32:36], mv[:, 20:24], mv[:, 20:24], ALU.mult, [bmv], [bmv])
                tt("dve", mv[:, 28:32], mv[:, 28:32], mv[:, 32:36], ALU.subtract, [bmv], [bmv])
                tt("pool", mv[:, 36:44], mv[:, 24:32], mhalf.broadcast_to([128, 8]), ALU.pow, [bmv, bcst], [bmv])
                stt(mv[:, 48:52], mv[:, 20:24], -1.0, mv[:, 40:44], ALU.mult, ALU.mult, [bmv], [bmv])
                og3 = og[:].rearrange("p (h d) -> p h d", h=8)
                rs_b = mv[:, 36:44].rearrange("p (h o) -> p h o", o=1).broadcast_to([128, 8, 128])
                nm_b = mv[:, 48:52].rearrange("p (h o) -> p h o", o=1).broadcast_to([128, 4, 128])
                tt("dve", og3, o_all.rearrange("p (h d) -> p h d", h=8), rs_b, ALU.mult, [bB[6], bB[7], bmv], [bog])
                tt("dve", og3[:, 4:8, :], og3[:, 4:8, :], nm_b, ALU.add, [bog, bmv], [bog])
                tt("dve", og[:, 0:512], og[:, 0:512], sg[p][:, 0:4, :].rearrange("p h d -> p (h d)"), ALU.mult, [bog, bsg[p][0]], [bog])
                tt("dve", og[:, 512:1024], og[:, 512:1024], sg[p][:, 4:8, :].rearrange("p h d -> p (h d)"), ALU.mult, [bog, bsg[p][1]], [bog])

            def K3b():
                for k in range(8):
                    tr(bankb(7)[:, k * 128:(k + 1) * 128], og[:, k * 128:(k + 1) * 128], identb, [bog, bcbf], [bB[7]])
                act(ogT[:, 0:4, :].rearrange("p k t -> p (k t)"), bankb(7)[:, 0:512], AF.Identity, [bB[7], bsmall], [bogT], scale=nwa)
                act(ogT[:, 4:8, :].rearrange("p k t -> p (k t)"), bankb(7)[:, 512:1024], AF.Identity, [bB[7], bsmall], [bogT], scale=nwb)

            def K4():
                for hf in range(2):
                    for k in range(8):
                        mm(bank(4 + hf)[:, 0:512], ogT[:, k, :], Wout[:, k, hf * 512:(hf + 1) * 512], k == 0, k == 7,
                           [bogT, bWout], [bB[4 + hf]])
                for hf in range(2):
                    tt("dve", Xj[:, hf * 512:(hf + 1) * 512], bank(4 + hf)[:, 0:512], Xj[:, hf * 512:(hf + 1) * 512], ALU.add,
                       [bB[4 + hf], bX[slot]], [bX[slot]])
                if gi == 15:
                    for h in range(8):
                        dst = nhp_d[h] if h < 4 else nrp_d[h - 4]
                        dma(dst, S[:, h, :], [bS[h]], [], bS[h])

            return [K1, K2, K3a, K3b, K4, K3a1]

        sq = mem.alloc("sq", [128, 1024], F32); bsq = Buf("sq")
        P.op("pool", lambda e: e.memset(mem_mv[:, 44:48], 0.0), [], [bmv])
        fronts = [front(i) for i in range(ntl)]
        backs = [back(i) for i in range(ntl)]
        for c in fronts[0]:
            c()
        if ntl > 1:
            fronts[1][0]()
        ck(2)
        for idx in range(ntl):
            K1, K2, K3a, K3b, K4, _K3a1 = backs[idx]
            nf = fronts[idx + 1] if idx + 1 < ntl else None
            if nf:
                nf[1]()
            K1()
            if nf:
                nf[2]()
            K2()
            if nf:
                nf[3]()
            K3a()
            if idx + 2 < ntl:
                fronts[idx + 2][0]()
            backs[idx][5]()
            if nf:
                nf[4]()
            if idx > 0:
                backs[idx - 1][4]()
            K3b()
            if idx == ntl - 1:
                K4()
            if idx == ntl - 2 or ntl == 1:
                load_wfi(0)
                pref["wfi"] = True

    def phase2(tiles):
        nt = len(tiles)
        h2T = mem.alloc("h2T", [128, 8, nt * 128], BF16)
        bh2 = [Buf("h2T%d" % i) for i in range(nt)]
        Wfo = mem.alloc("Wfo", [128, 11, 1024], BF16); bWfo = Buf("Wfo")
        wnf = mem.alloc("wnf", [128, 1024], F32); bwnf = Buf("wnf")
        dma(wnf[:], wnf_d, [], [bwnf], bwnf)
        wn2b = wn2.rearrange("p (k o) -> p k o", o=1).broadcast_to([128, 8, 128])
        xn2 = [mem.alloc("xn2", [128, 1024], BF16) for _ in range(2)]; bxn2 = [Buf("xn2a"), Buf("xn2b")]
        st2 = [mem.alloc("st2", [128, 16], F32) for _ in range(2)]; bst2 = [Buf("st2a"), Buf("st2b")]
        xn, bxn, st, bst = xn2[0], bxn2[0], st2[0], bst2[0]
        hl = mem.alloc("hl", [128, 4, 4], F32); bhl = [Buf("hl%d" % i) for i in range(4)]
        ext = [mem.alloc("ext", [128, 160], F32) for _ in range(2)]
        bext = [Buf("ext0"), Buf("ext1")]
        NCC = 4
        cc = [mem.alloc("cc", [128, 512], F32) for _ in range(NCC)]
        bcc = [Buf("cc%d" % i) for i in range(NCC)]
        sgl = [mem.alloc("sgl", [128, 512], BF16) for _ in range(2)]; bsgl = [Buf("sgl0"), Buf("sgl1")]
        actT = mem.alloc("actT", [128, 11, 512], BF16); bactT = [Buf("actT%d" % i) for i in range(11)]
        ncT = mem.alloc("ncT", [128, 22, 32], F32); bncT = Buf("ncT")
        scT = mem.alloc("scT", [128, 22, 32], F32); bscT = Buf("scT")
        sc32 = mem.alloc("sc32", [32, 2816], F32); bsc32 = Buf("sc32")
        yb = [mem.alloc("yb", [128, 1024], F32) for _ in range(2)]; byb = [Buf("yb0"), Buf("yb1")]
        has_samp = any(gi == 16 for (_, gi) in tiles)
        sts = []
        pt = [t for t in tiles if t[1] != 16]
        for i in range(0, len(pt), 4):
            sts.append(pt[i:i + 4])
        if has_samp:
            sts.append([t for t in tiles if t[1] == 16])
        slot_pos = {s_: i for i, (s_, _) in enumerate(tiles)}
        ycnt = [0]

        def prepA(slot, pi_):
            Xj = X[:, slot, :]
            xn, bxn, st, bst = xn2[pi_], bxn2[pi_], st2[pi_], bst2[pi_]
            act(xn[:], Xj, AF.Square, [bX[slot]], [bxn, bst], accum=st[:, 0:1])
            ts("dve", st[:, 1:2], st[:, 0:1], 1.0 / 1024, EPS, ALU.mult, ALU.add, [bst], [bst])
            tt("pool", st[:, 2:3], st[:, 1:2], mhalf, ALU.pow, [bst, bcst], [bst])
            act(xn[:], Xj, AF.Identity, [bX[slot], bst], [bxn], scale=st[:, 2:3])

        def prepB(slot, pi_):
            pp = slot_pos[slot]
            xn, bxn = xn2[pi_], bxn2[pi_]
            for k in range(8):
                tr(bankb(7)[:, k * 128:(k + 1) * 128], xn[:, k * 128:(k + 1) * 128], identb, [bxn, bcbf], [bB[7]])
            tt("dve", h2T[:, :, pp * 128:(pp + 1) * 128],
               bankb(7)[:, 0:1024].rearrange("p (k t) -> p k t", k=8), wn2b, ALU.mult, [bB[7], bsmall], [bh2[pp]])

        for half in range(2):
            if not pref["wfi"]:
                load_wfi(half)
            pref["wfi"] = False
            if half == 1 and not has_samp:
                load_win(6, 8, bWinHi)
                pref["win_hi"] = True
            dma_cast(Wfo[:], w_fo_d[half * 1408:(half + 1) * 1408, :].rearrange("(c p) n -> p c n", p=128), [bWfo], bWfo)
            if has_samp:
                for part in range(2):
                    c0_ = part * DFF + half * 1408
                    dma(sc32[:, part * 1408:(part + 1) * 1408], sc_d[:, c0_:c0_ + 1408], [], [bsc32], bsc32)
                for ft in range(22):
                    bk = 4 + (ft % 2)
                    tr(bank(bk)[:, 0:32], sc32[:, ft * 128:(ft + 1) * 128], idf[0:32, 0:32], [bsc32, bidf], [bB[bk]])
                    cp("dve", scT[:, ft, :], bank(bk)[:, 0:32], [bB[bk]], [bscT])
            if half == 0:
                fs = [s_ for (s_, _) in sts[0]]
                prepA(fs[0], 0)
                for j in range(len(fs)):
                    if j + 1 < len(fs):
                        prepA(fs[j + 1], (j + 1) % 2)
                    prepB(fs[j], j % 2)

            for sti, stl in enumerate(sts):
                samp = (stl[0][1] == 16)
                ntk = 128 * len(stl)
                p0 = slot_pos[stl[0][0]] * 128
                nxt = [s_ for (s_, _) in sts[sti + 1]] if (half == 0 and sti + 1 < len(sts)) else []
                rd_h2 = [bh2[slot_pos[s_]] for (s_, _) in stl]
                fi = 0
                tails = []
                for c in range(11):
                    if nxt and c % 2 == 0 and (c // 2) < len(nxt):
                        prepA(nxt[c // 2], (c // 2) % 2)
                    for part in (1, 0):
                        ft = part * 11 + c
                        gft = part * 22 + half * 11 + c
                        bk = fi % 4
                        ci = fi % NCC
                        fi += 1
                        up = bank(bk)[:, 0:ntk]
                        for k in range(8):
                            mm(up, Wfi[:, k, part * 1408 + c * 128: part * 1408 + (c + 1) * 128],
                               h2T[:, k, p0:p0 + ntk], k == 0, k == 7, [bWfi] + rd_h2, [bB[bk]])
                        cv = cc[ci]
                        w0 = convw[:, gft:gft + 1]
                        w1 = convw[:, 44 + gft:45 + gft]
                        w2 = convw[:, 88 + gft:89 + gft]
                        cb = convw[:, 132 + gft:133 + gft]
                        if not samp:
                            hi_ = fi % 4
                            cp_ = cpar[gft]
                            cold = carry[:, cp_, gft, :]
                            act(cv[:, 0:ntk], up, AF.Identity, [bB[bk], bconvw], [bcc[ci]], scale=w2, bias=cb)
                            cp("act", carry[:, 1 - cp_, gft, :], up[:, ntk - 2:ntk], [bB[bk]], [bcarry[1 - cp_][gft]])
                            stt(cv[:, 1:ntk], up[:, 0:ntk - 1], w1, cv[:, 1:ntk], ALU.mult, ALU.add, [bB[bk], bcc[ci], bconvw], [bcc[ci]])
                            stt(cv[:, 2:ntk], up[:, 0:ntk - 2], w0, cv[:, 2:ntk], ALU.mult, ALU.add, [bB[bk], bcc[ci], bconvw], [bcc[ci]])
                            ts("pool", hl[:, hi_, 0:2], cold, w0, None, ALU.mult, None, [bcarry[cp_][gft], bconvw], [bhl[hi_]])
                            ts("pool", hl[:, hi_, 2:3], cold[:, 1:2], w1, None, ALU.mult, None, [bcarry[cp_][gft], bconvw], [bhl[hi_]])
                            tt("pool", cv[:, 0:2], cv[:, 0:2], hl[:, hi_, 0:2], ALU.add, [bcc[ci], bhl[hi_]], [bcc[ci]])
                            tt("pool", cv[:, 0:1], cv[:, 0:1], hl[:, hi_, 2:3], ALU.add, [bcc[ci], bhl[hi_]], [bcc[ci]])
                            cpar[gft] = 1 - cp_
                        else:
                            e_i = fi % 2
                            ex = ext[e_i]
                            ex3 = ex[:, 0:160].rearrange("p (j t) -> p j t", t=10)
                            cv3 = cv[:, 0:128].rearrange("p (j t) -> p j t", t=8)
                            up3 = up.rearrange("p (j t) -> p j t", t=8)
                            cp("pool", ex3[:, :, 0:2], scT[:, ft, :].rearrange("p (j r) -> p j r", r=2), [bscT], [bext[e_i]])
                            cp("act", ex3[:, :, 2:10], up3, [bB[bk]], [bext[e_i]])
                            cp("pool", ncT[:, ft, :].rearrange("p (j r) -> p j r", r=2), ex3[:, :, 8:10], [bext[e_i]], [bncT])
                            act(cv[:, 0:128], up, AF.Identity, [bB[bk], bconvw], [bcc[ci]], scale=w2, bias=cb)
                            stt(cv3, ex3[:, :, 1:9], w1, cv3, ALU.mult, ALU.add, [bext[e_i], bcc[ci], bconvw], [bcc[ci]])
                            stt(cv3, ex3[:, :, 0:8], w0, cv3, ALU.mult, ALU.add, [bext[e_i], bcc[ci], bconvw], [bcc[ci]])
                        if part == 1:
                            tails.append((lambda cv, ci, c, ntk: lambda: act(sgl[c % 2][:, 0:ntk], cv[:, 0:ntk], AF.Silu, [bcc[ci]], [bsgl[c % 2]]))(cv, ci, c, ntk))
                        else:
                            tails.append((lambda cv, ci, c, ntk: lambda: tt("dve", actT[:, c, 0:ntk], cv[:, 0:ntk], sgl[c % 2][:, 0:ntk], ALU.mult, [bcc[ci], bsgl[c % 2]], [bactT[c]]))(cv, ci, c, ntk))
                        while len(tails) > 2:
                            tails.pop(0)()
                    if nxt and c in (1, 3, 5, 7) and (c // 2) < len(nxt):
                        prepB(nxt[c // 2], (c // 2) % 2)
                if sti == len(sts) - 1:
                    if half == 0:
                        load_wfi(1)
                        pref["wfi"] = True
                    elif not has_samp:
                        if pref.get("win_hi"):
                            load_win(0, 6)
                        else:
                            load_win()
                        pref["win"] = True
                while tails:
                    tails.pop(0)()
                for ti, (slot, gi) in enumerate(stl):
                    Xj = X[:, slot, :]
                    for hf in range(2):
                        bk = 4 + (2 * ti + hf) % 3
                        for c in range(11):
                            mm(bank(bk)[:, 0:512], actT[:, c, ti * 128:(ti + 1) * 128], Wfo[:, c, hf * 512:(hf + 1) * 512],
                               c == 0, c == 10, [bactT[c], bWfo], [bB[bk]])
                        tt("dve", Xj[:, hf * 512:(hf + 1) * 512], bank(bk)[:, 0:512], Xj[:, hf * 512:(hf + 1) * 512], ALU.add,
                           [bB[bk], bX[slot]], [bX[slot]])
                    if half == 1:
                        yi = ycnt[0] % 2
                        ycnt[0] += 1
                        act(xn[:], Xj, AF.Square, [bX[slot]], [bxn, bst], accum=st[:, 4:5])
                        ts("dve", st[:, 5:6], st[:, 4:5], 1.0 / 1024, EPS, ALU.mult, ALU.add, [bst], [bst])
                        tt("pool", st[:, 6:7], st[:, 5:6], mhalf, ALU.pow, [bst, bcst], [bst])
                        stt(yb[yi][:], Xj, st[:, 6:7], wnf[:], ALU.mult, ALU.mult, [bX[slot], bst, bwnf], [byb[yi]])
                        dma(y_d[gi * 128:(gi + 1) * 128, :], yb[yi][:], [byb[yi]], [], byb[yi])
                last_prompt = (not samp) and stl[-1][1] == 15
                if last_prompt or samp:
                    ncols = 32 if samp else 2
                    for part in range(2):
                        for c in range(11):
                            ft = part * 11 + c
                            gft = part * 22 + half * 11 + c
                            bk = 4 + (ft % 3)
                            if samp:
                                src, rdb = ncT[:, ft, :], [bncT]
                            else:
                                src, rdb = carry[:, cpar[gft], gft, :], [bcarry[cpar[gft]][gft]]
                            tr(bank(bk)[0:ncols, 0:128], src, idf[:], rdb + [bidf], [bB[bk]])
                            cp("act", sc32[0:ncols, ft * 128:(ft + 1) * 128], bank(bk)[0:ncols, 0:128], [bB[bk]], [bsc32])
                    outd = ncs_d if samp else ncp_d
                    for part in range(2):
                        c0_ = part * DFF + half * 1408
                        dma(outd[:, c0_:c0_ + 1408], sc32[0:ncols, part * 1408:(part + 1) * 1408], [bsc32], [], bsc32)

    mem_mv = mem.alloc("mv", [128, 64], F32); bmv = Buf("mv")
    m0 = mem.mark()
    groups = [
        [(i, i) for i in range(8)],
        [(8, 16)] + [(i - 8, i) for i in range(8, 16)],
    ]
    if "groups" in dbg:
        groups = dbg["groups"]
    try:
        ck(0)
        for tiles in groups:
            mem.reset(m0)
            if not dbg.get("skip1"):
                phase1(tiles)
            barrier()
            mem.reset(m0)
            if not dbg.get("skip2"):
                phase2(tiles)
            barrier()
    except _Stop:
        pass
    P.emit()
    return nc, mem.peak


def _consts():
    bf = ml_dtypes.bfloat16
    s = np.arange(128)
    ident = np.eye(128, dtype=np.float32)
    maskP = (s[:, None] <= s[None, :]).astype(np.float32)
    maskS = ((s[:, None] <= s[None, :]) & ((s[:, None] // 8) == (s[None, :] // 8))).astype(np.float32)
    cbf = np.concatenate([ident, maskP, maskS], axis=1).astype(bf)
    j = np.arange(16)
    smt = np.broadcast_to(((s[:, None] // 8) == j[None, :])[:, :, None], (128, 16, 128)).astype(bf).reshape(128, 2048)
    smb = np.broadcast_to(((s[None, :] // 8) == j[:, None])[None, :, :], (128, 16, 128)).astype(bf).reshape(128, 2048)
    half = 64
    inv = (1.0 / (np.float32(10000.0) ** (np.arange(half, dtype=np.float32) / np.float32(half)))).astype(np.float32)
    rope = np.zeros((NT, 128, 4, 4, 64), dtype=np.float32)
    for gi in range(NT):
        if gi < 16:
            pos = (gi * 128 + s).astype(np.float32)
            tau = s
        else:
            pos = (PAST_LEN + (s % 8)).astype(np.float32)
            tau = s % 8
        ang = (pos[:, None] * inv[None, :]).astype(np.float32).astype(np.float64)
        cs, sn = np.cos(ang), np.sin(ang)
        for h in range(4):
            dq = GAM[h] ** (tau + 1.0)
            dk = GAM[h] ** (-(tau + 1.0)) * (128.0 ** -0.5)
            rope[gi, :, 0, h] = cs * dq[:, None]
            rope[gi, :, 1, h] = sn * dq[:, None]
            rope[gi, :, 2, h] = cs * dk[:, None]
            rope[gi, :, 3, h] = sn * dk[:, None]
    rope = rope.reshape(NT, 128, 1024)
    return cbf, smt, smb, rope, ident


_CACHE = {}


def kernel(x_prompt, x_sample, state_hgrn, state_ret, state_conv, w_norm1, w_in, hgrn_lb,
           hgrn_norm_w, ret_norm_w, w_out, w_norm2, w_ffn_in, conv_w, conv_b, w_ffn_out, w_norm_f):
    f32 = np.float32
    if "nc" not in _CACHE:
        _CACHE["nc"] = build_program()[0]
        _CACHE["consts"] = _consts()
    nc = _CACHE["nc"]
    cbf, smt, smb, rope, ident = _CACHE["consts"]

    small = np.zeros((128, 64), dtype=f32)
    small[:, 0:8] = np.asarray(w_norm1[0], f32).reshape(8, 128).T
    small[:, 8:16] = np.asarray(w_norm2[0], f32).reshape(8, 128).T
    small[:, 16] = np.asarray(hgrn_norm_w[0], f32)
    small[:, 17] = np.asarray(ret_norm_w[0], f32)
    small[:, 18:22] = np.asarray(hgrn_lb[0], f32).reshape(4, 128).T
    small[:, 22:26] = np.asarray(hgrn_lb[1], f32).reshape(4, 128).T
    convw = np.zeros((128, 176), dtype=f32)
    for jj in range(3):
        convw[:, jj * 44:(jj + 1) * 44] = np.asarray(conv_w[0, jj], f32).reshape(44, 128).T
    convw[:, 132:176] = np.asarray(conv_b[0], f32).reshape(44, 128).T
    wnf = np.ascontiguousarray(np.broadcast_to(np.asarray(w_norm_f, f32)[None, :], (128, 1024)))

    shared = {
        "w_in": np.ascontiguousarray(w_in[0], dtype=f32), "w_out": np.ascontiguousarray(w_out[0], dtype=f32),
        "w_fi": np.ascontiguousarray(w_ffn_in[0], dtype=f32), "w_fo": np.ascontiguousarray(w_ffn_out[0], dtype=f32),
        "small": small, "convw": convw, "wnf": wnf, "rope": rope, "cbf": cbf, "smt": smt, "smb": smb, "idf": ident,
    }
    in_maps = []
    for c in range(8):
        xs = np.concatenate([np.asarray(x_prompt[c], f32), np.asarray(x_sample[16 * c:16 * c + 16], f32).reshape(128, 1024)], axis=0)
        m = dict(shared)
        m["x"] = np.ascontiguousarray(xs)
        m["sh"] = np.ascontiguousarray(state_hgrn[0, 16 * c:16 * c + 16], dtype=f32)
        m["sr"] = np.ascontiguousarray(state_ret[0, 16 * c:16 * c + 16], dtype=f32)
        m["sc"] = np.ascontiguousarray(np.asarray(state_conv[0, 16 * c:16 * c + 16], f32).reshape(32, 2 * DFF))
        in_maps.append(m)
    res = run_bass_kernel_spmd(nc, in_maps, core_ids=list(range(8)))
    R = res.results
    y_prompt = np.stack([R[c]["y"][:2048] for c in range(8)], axis=0)
    y_sample = np.concatenate([R[c]["y"][2048:].reshape(16, 8, 1024) for c in range(8)], axis=0)
    ha_p = np.stack([R[c]["nhp"] for c in range(8)], axis=0)[None]
    rb_p = np.stack([R[c]["nrp"] for c in range(8)], axis=0)[None]
    cv_p = np.stack([R[c]["ncp"] for c in range(8)], axis=0)[None]
    ha_s = np.concatenate([R[c]["nhs"] for c in range(8)], axis=0)[None]
    rb_s = np.concatenate([R[c]["nrs"] for c in range(8)], axis=0)[None]
    cv_s = np.concatenate([R[c]["ncs"].reshape(16, 2, 2 * DFF) for c in range(8)], axis=0)[None]
    return (y_prompt.astype(f32), y_sample.astype(f32), ha_p.astype(f32), rb_p.astype(f32), cv_p.astype(f32),
            ha_s.astype(f32), rb_s.astype(f32), cv_s.astype(f32))
```

```python
import contextlib
import numpy as np
import ml_dtypes
import concourse.bass as bass
import concourse.mybir as mybir
from concourse.bass_utils import run_bass_kernel_spmd

F32 = mybir.dt.float32
BF16 = mybir.dt.bfloat16
AF = mybir.ActivationFunctionType
ALU = mybir.AluOpType

ENGS = ["pe", "act", "dve", "pool", "sp"]
SKIP_SAME_ENGINE_WAW = False
EPS = 1e-6
NT = 17
DFF = 2816
NFT = 22
PAST_LEN = 16384


class Buf:
    __slots__ = ("name", "w", "r", "sem", "semcnt", "excl")

    def __init__(self, name="", excl=False):
        self.name = name
        self.excl = excl
        self.w = None
        self.r = []
        self.sem = None
        self.semcnt = 0


class Op:
    __slots__ = ("eng", "fn", "deps", "dma", "inc", "incval", "dmasem", "dmaval", "dmadeps")

    def __init__(self, eng, fn, dma):
        self.eng = eng
        self.fn = fn
        self.deps = []
        self.dmadeps = {}
        self.dma = dma
        self.inc = False
        self.incval = 0
        self.dmasem = None
        self.dmaval = 0


class Prog:
    def __init__(self, nc):
        self.nc = nc
        self.ops = {e: [] for e in ENGS}
        self.dma_bufs = []
        self.all_ops = []
        self.pending = {e: [] for e in ENGS}
        self.recent_dmas = []

    def _add(self, eng, fn, reads, writes, dma=False, key=None):
        op = Op(eng, fn, dma)
        deps = list(self.pending[eng])
        self.pending[eng] = []
        for b in reads:
            if b.w is not None:
                deps.append(b.w)
            if b.excl:
                deps.extend(r for r in b.r if r.eng != eng)
        for b in writes:
            if b.w is not None and not (dma and b.w.dma) and (not SKIP_SAME_ENGINE_WAW or b.w.eng != eng or b.w.dma or dma):
                deps.append(b.w)
            deps.extend(r for r in b.r if (not SKIP_SAME_ENGINE_WAW or r.eng != eng or r.dma or dma))
        seen = set()
        for d in deps:
            if id(d) in seen:
                continue
            seen.add(id(d))
            if d.dma:
                kb = d.dmasem
                op.dmadeps[id(kb)] = (kb, kb.semcnt)
            else:
                op.deps.append(d)
        for b in reads:
            b.r.append(op)
        for b in writes:
            b.w = op
            b.r = []
        if dma:
            if key.sem is None:
                key.sem = len(self.dma_bufs)
                self.dma_bufs.append(key)
            key.semcnt += 16
            op.dmasem = key
            op.dmaval = key.semcnt
            self.recent_dmas.append(op)
        self.ops[eng].append(op)
        self.all_ops.append(op)
        return op

    def op(self, eng, fn, reads=(), writes=()):
        return self._add(eng, fn, list(reads), list(writes))

    def dma(self, eng, fn, reads=(), writes=(), key=None):
        return self._add(eng, fn, list(reads), list(writes), dma=True, key=key)

    def barrier(self, markers):
        mops = []
        for e, fn in markers.items():
            mops.append(self.op(e, fn, writes=[Buf("bar")]))
        for e in ENGS:
            self.pending[e] = self.pending[e] + mops + self.recent_dmas
        self.recent_dmas = []

    def emit(self):
        nc = self.nc
        for op in self.all_ops:
            for d in op.deps:
                if d.eng == op.eng and op.eng in ("pe", "sp") and not op.dma:
                    continue
                d.inc = True
        for e in ENGS:
            c = 0
            for op in self.ops[e]:
                if op.inc and not op.dma:
                    c += 1
                    op.incval = c
        with contextlib.ExitStack() as st:
            esem = {e: st.enter_context(nc.semaphore("s_" + e)) for e in ENGS}
            dsem = [st.enter_context(nc.semaphore("d%d" % i)) for i in range(len(self.dma_bufs))]
            block = st.enter_context(nc.Block())
            engobj = {"pe": "tensor", "act": "scalar", "dve": "vector", "pool": "gpsimd", "sp": "sync"}
            ops = self.ops
            dma_bufs = self.dma_bufs

            def make(e):
                def body(eng):
                    waited = {}
                    for op in ops[e]:
                        need = {}
                        for (kb, v) in op.dmadeps.values():
                            need[("d", kb.sem)] = (dsem[kb.sem], v)
                        for d in op.deps:
                            if not d.inc:
                                continue
                            if d.eng == e and e in ("pe", "sp") and not op.dma:
                                continue
                            k = ("e", d.eng)
                            if k not in need or need[k][1] < d.incval:
                                need[k] = (esem[d.eng], d.incval)
                        for k, (s, v) in need.items():
                            if waited.get(k, 0) >= v:
                                continue
                            eng.wait_ge(s, v)
                            waited[k] = v
                        ins = op.fn(eng)
                        if op.dma:
                            ins.then_inc(dsem[op.dmasem.sem], 16)
                        elif op.inc:
                            ins.then_inc(esem[e], 1)
                    if e == "sp":
                        for b in dma_bufs:
                            eng.wait_ge(dsem[b.sem], b.semcnt)
                        for e2 in ENGS:
                            if e2 == "sp":
                                continue
                            tot = sum(1 for o in ops[e2] if o.inc and not o.dma)
                            if tot:
                                eng.wait_ge(esem[e2], tot)
                return body

            for e in ENGS:
                getattr(block, engobj[e])(make(e))


class Mem:
    def __init__(self, nc, base=16640, limit=229312):
        self.nc = nc
        self.off = base
        self.limit = limit
        self.n = 0
        self.peak = 0

    def alloc(self, name, shape, dtype):
        sz = int(np.prod(shape[1:])) * mybir.dt.size(dtype)
        off = (self.off + 63) // 64 * 64
        self.n += 1
        t = self.nc.alloc_sbuf_tensor_at("%s_%d" % (name, self.n), list(shape), dtype, offset=off)
        self.off = off + sz
        self.peak = max(self.peak, self.off)
        assert self.off <= self.limit, (name, self.off, self.limit)
        return t

    def mark(self):
        return self.off

    def reset(self, m):
        self.off = m


GAM = [1.0 - 2.0 ** (-5 - h) for h in range(4)]


class _Stop(Exception):
    pass


def build_program(dbg=None):
    dbg = dbg or {}

    def ck(n):
        if dbg.get("stop") == n:
            raise _Stop()

    nc = bass.Bass("TRN2", target_bir_lowering=False)

    def din(name, shape, dt=F32):
        return nc.dram_tensor(name, list(shape), dt, kind="ExternalInput").ap()

    def dout(name, shape, dt=F32):
        return nc.dram_tensor(name, list(shape), dt, kind="ExternalOutput").ap()

    x_d = din("x", [NT * 128, 1024])
    sh_d = din("sh", [16, 4, 128, 128])
    sr_d = din("sr", [16, 4, 128, 128])
    sc_d = din("sc", [32, 2 * DFF])
    w_in_d = din("w_in", [1024, 4096])
    w_out_d = din("w_out", [1024, 1024])
    w_fi_d = din("w_fi", [1024, 2 * DFF])
    w_fo_d = din("w_fo", [DFF, 1024])
    small_d = din("small", [128, 64])
    convw_d = din("convw", [128, 4 * 44])
    wnf_d = din("wnf", [128, 1024])
    rope_d = din("rope", [NT, 128, 1024])
    cbf_d = din("cbf", [128, 128 * 3], BF16)
    smt_d = din("smt", [128, 2048], BF16)
    smb_d = din("smb", [128, 2048], BF16)
    idf_d = din("idf", [128, 128])

    y_d = dout("y", [NT * 128, 1024])
    nhp_d = dout("nhp", [4, 128, 128])
    nrp_d = dout("nrp", [4, 128, 128])
    ncp_d = dout("ncp", [2, 2 * DFF])
    nhs_d = dout("nhs", [16, 4, 128, 128])
    nrs_d = dout("nrs", [16, 4, 128, 128])
    ncs_d = dout("ncs", [32, 2 * DFF])

    mem = Mem(nc)
    P = Prog(nc)

    XS = 9
    X = mem.alloc("X", [128, XS, 1024], F32)
    bX = [Buf("X%d" % i) for i in range(XS)]
    small = mem.alloc("small", [128, 64], F32); bsmall = Buf("small")
    convw = mem.alloc("convw", [128, 176], F32); bconvw = Buf("convw")
    cbf = mem.alloc("cbf", [128, 384], BF16); bcbf = Buf("cbf")
    idf = mem.alloc("idf", [128, 128], F32); bidf = Buf("idf")
    cst = mem.alloc("cst", [128, 32], F32); bcst = Buf("cst")
    S = mem.alloc("S", [128, 8, 128], F32)
    Sb = mem.alloc("Sb", [128, 8, 128], BF16)
    bS = [Buf("S%d" % h) for h in range(8)]
    bSb = [Buf("Sb%d" % h) for h in range(8)]
    carry = mem.alloc("carry", [128, 2, 44, 2], F32)
    bcarry = [[Buf("cy%d_%d" % (q, i)) for i in range(44)] for q in range(2)]
    cpar = [0] * 44
    scrA = mem.alloc("scrA", [128, 2], F32)
    scrD = mem.alloc("scrD", [128, 2], F32)
    scrP = mem.alloc("scrP", [128, 2], F32)
    zeros = mem.alloc("zeros", [128, 128], F32); bzeros = Buf("zeros")
    ra_off = (mem.off + 63) // 64 * 64
    Win = nc.alloc_sbuf_tensor_at("RA_win", [128, 8, 4096], BF16, offset=ra_off)
    Wfi = nc.alloc_sbuf_tensor_at("RA_wfi", [128, 8, 2816], BF16, offset=ra_off)
    mem.off = ra_off + 65536
    mem.peak = max(mem.peak, mem.off)
    bRA = Buf("RA")
    bWin = bRA
    bWfi = bRA
    bWinHi = Buf("WinHi")
    pref = {"win": False, "wfi": False}

    def load_win(k0=0, k1=8, buf=None):
        buf = buf or bRA
        for (ca, cb_) in [(0, 1024), (2048, 3072), (1024, 2048), (3072, 4096)]:
            o_ = P.dma("pool", (lambda ca, cb_: lambda e: e.dma_start(
                out=Win[:, k0:k1, ca:cb_], in_=w_in_d[k0 * 128:k1 * 128, ca:cb_].rearrange("(k p) n -> p k n", p=128),
                max_dma_last_dim=8192))(ca, cb_), [], [buf], buf)
            P.recent_dmas.remove(o_)

    def load_wfi(half):
        for k in range(8):
            for part in (1, 0):
                c0_ = part * DFF + half * 1408
                o_ = P.dma("pool", (lambda k, part, c0_: lambda e: e.dma_start(
                    out=Wfi[:, k, part * 1408:(part + 1) * 1408], in_=w_fi_d[k * 128:(k + 1) * 128, c0_:c0_ + 1408],
                    max_dma_last_dim=8192))(k, part, c0_), [], [bRA], bRA)
                P.recent_dmas.remove(o_)

    identb = cbf[:, 0:128]
    maskP = cbf[:, 128:256]
    maskS = cbf[:, 256:384]
    wn1 = small[:, 0:8]
    wn2 = small[:, 8:16]
    nwa = small[:, 16:17]
    nwb = small[:, 17:18]
    lb0 = small[:, 18:22]
    lb1 = small[:, 22:26]
    c0 = cst[:, 0:4]
    c1 = cst[:, 4:8]
    nc1 = cst[:, 8:12]
    mhalf = cst[:, 12:13]

    PS = nc.alloc_psum_tensor("ps", [128, 4096], F32)
    PSb16 = PS[:].bitcast(BF16)
    bB = [Buf("B%d" % i, excl=True) for i in range(8)]

    def bank(i):
        return PS[:, i * 512:(i + 1) * 512]

    def bankb(i):
        return PSb16[:, i * 1024:(i + 1) * 1024]

    def mm(out, lhsT, rhs, start, stop, reads, writes):
        P.op("pe", lambda e: e.matmul(out, lhsT=lhsT, rhs=rhs, start=start, stop=stop), reads, writes)

    def tr(out, in_, ident, reads, writes):
        P.op("pe", lambda e: e.transpose(out=out, in_=in_, identity=ident), reads, writes)

    def act(out, in_, func, reads, writes, scale=None, bias=None, accum=None):
        kw = {}
        if scale is not None:
            kw["scale"] = scale
        if bias is not None:
            kw["bias"] = bias
        if accum is not None:
            kw["accum_out"] = accum
        P.op("act", lambda e: e.activation(out=out, in_=in_, func=func, **kw), reads, writes)

    def ts(eng, out, in0, s1, s2, op0, op1, reads, writes):
        if op1 is None:
            P.op(eng, lambda e: e.tensor_scalar(out=out, in0=in0, scalar1=s1, scalar2=None, op0=op0), reads, writes)
        else:
            P.op(eng, lambda e: e.tensor_scalar(out=out, in0=in0, scalar1=s1, scalar2=s2, op0=op0, op1=op1), reads, writes)

    def tt(eng, out, in0, in1, op, reads, writes):
        P.op(eng, lambda e: e.tensor_tensor(out=out, in0=in0, in1=in1, op=op), reads, writes)

    def stt(out, in0, scalar, in1, op0, op1, reads, writes):
        P.op("dve", lambda e: e.scalar_tensor_tensor(out=out, in0=in0, scalar=scalar, in1=in1, op0=op0, op1=op1), reads, writes)

    def cp(eng, out, in_, reads, writes):
        if eng == "act":
            P.op("act", lambda e: e.activation(out=out, in_=in_, func=AF.Copy), reads, writes)
        else:
            P.op(eng, lambda e: e.tensor_copy(out=out, in_=in_), reads, writes)

    def dma(out, in_, reads, writes, key, eng="sp", slow=False):
        if slow:
            P.dma(eng, lambda e: e.dma_start(out=out, in_=in_, allow_slow_non_contiguous=True), reads, writes, key)
        else:
            P.dma(eng, lambda e: e.dma_start(out=out, in_=in_), reads, writes, key)

    def dma_cast(out, in_, writes, key):
        P.dma("pool", lambda e: e.dma_start(out=out, in_=in_, max_dma_last_dim=8192), [], writes, key)

    def barrier():
        P.barrier({
            "act": lambda e: e.activation(out=scrA[:, 0:1], in_=scrA[:, 1:2], func=AF.Copy),
            "dve": lambda e: e.tensor_copy(out=scrD[:, 0:1], in_=scrD[:, 1:2]),
            "pool": lambda e: e.memset(scrP[:, 0:1], 0.0),
        })

    dma(small[:], small_d, [], [bsmall], bsmall)
    dma(convw[:], convw_d, [], [bconvw], bconvw)
    dma(cbf[:], cbf_d, [], [bcbf], bcbf)
    dma(idf[:], idf_d, [], [bidf], bidf)
    P.op("pool", lambda e: e.memset(zeros[:], 0.0), [], [bzeros])
    P.op("pool", lambda e: e.memset(scrA[:], 0.0), [], [])
    P.op("pool", lambda e: e.memset(scrD[:], 0.0), [], [])
    P.op("pool", lambda e: e.memset(scrP[:], 0.0), [], [])
    P.op("pool", lambda e: e.memset(cst[:], 0.0), [], [bcst])
    P.op("pool", lambda e: e.memset(cst[:, 12:13], -0.5), [], [bcst])
    for h in range(8):
        P.op("pool", (lambda hh: lambda e: e.memset(S[:, hh, :], 0.0))(h), [], [bS[h]])
        P.op("pool", (lambda hh: lambda e: e.memset(Sb[:, hh, :], 0.0))(h), [], [bSb[h]])
    P.op("pool", lambda e: e.memset(carry[:].rearrange("p a b c -> p (a b c)"), 0.0), [], [b for q in range(2) for b in bcarry[q]])
    tt("dve", cst[:, 24:28], lb0, lb1, ALU.subtract, [bsmall, bcst], [bcst])
    act(cst[:, 28:32], cst[:, 24:28], AF.Tanh, [bcst], [bcst], scale=0.5)
    ts("dve", c0, cst[:, 28:32], 0.25, 0.75, ALU.mult, ALU.add, [bcst], [bcst])
    ts("dve", c1, cst[:, 28:32], -0.25, 0.25, ALU.mult, ALU.add, [bcst], [bcst])
    ts("dve", nc1, cst[:, 28:32], 0.25, -0.25, ALU.mult, ALU.add, [bcst], [bcst])

    m0 = mem.mark()
    cast_rr = [0]

    def cast_scaled(out, in_, scal, reads, writes):
        i = cast_rr[0]
        cast_rr[0] += 1
        eng = ("dve", "pool", "act")[i % 3]
        if scal is None:
            cp(eng, out, in_, reads, writes)
        elif eng == "act":
            act(out, in_, AF.Identity, reads, writes, scale=scal)
        else:
            ts(eng, out, in_, scal, None, ALU.mult, None, reads, writes)

    def phase1(tiles):
        ntl = len(tiles)
        has_samp = any(gi == 16 for (_, gi) in tiles)
        Wout = mem.alloc("Wout", [128, 8, 1024], BF16); bWout = Buf("Wout")
        if not pref["win"]:
            load_win()
        pref["win"] = False
        dma_cast(Wout[:], w_out_d.rearrange("(k p) n -> p k n", p=128), [bWout], bWout)
        ck(1)
        rope1 = mem.alloc("rope", [128, 4, 256], F32); rope = [rope1, rope1]; brope1 = Buf("rope"); brope = [brope1, brope1]
        xn = mem.alloc("xn", [128, 1024], BF16); bxn = Buf("xn")
        hT = mem.alloc("hT", [128, 8, 128], BF16); bhT = Buf("hT")
        st = mem.alloc("st", [128, 16], F32); bst = Buf("st")
        th = mem.alloc("th", [128, 4, 128], F32); bth = Buf("th")
        ff = mem.alloc("ff", [128, 128], F32); bff = Buf("ff")
        kk = mem.alloc("kk", [128, 128], F32); bkk = Buf("kk")
        RR = mem.alloc("RR", [128, 128], F32); bRR = Buf("RR")
        qtok = mem.alloc("qtok", [128, 4, 128], BF16); bqtok = Buf("qtok")
        khT = mem.alloc("khT", [128, 4, 128], BF16); bkhT = [Buf("khT%d" % h) for h in range(4)]
        vvh = [mem.alloc("vvh", [128, 4, 128], BF16) for _ in range(2)]; bvvh = [Buf("vvh0"), Buf("vvh1")]
        rt = [mem.alloc("rt", [128, 256], BF16) for _ in range(8)]; brt = [Buf("rt%d" % i) for i in range(8)]
        scm = mem.alloc("scm", [128, 8, 128], BF16); bscm = [Buf("scm%d" % h) for h in range(8)]
        og = mem.alloc("og", [128, 1024], BF16); bog = Buf("og")
        on = og; bon = bog
        ogT = mem.alloc("ogT", [128, 8, 128], BF16); bogT = Buf("ogT")
        Pc = [mem.alloc("Pc", [128, 4, 128], F32) for _ in range(2)]
        bPc = [[Buf("Pc%d_%d" % (p, h)) for h in range(4)] for p in range(2)]
        qT = [mem.alloc("qT", [128, 8, 128], BF16) for _ in range(2)]
        bqT = [[Buf("qT") for h in range(8)] for p in range(2)]
        kT = [mem.alloc("kT", [128, 8, 128], BF16) for _ in range(2)]
        bkT = [[Buf("kT") for h in range(8)] for p in range(2)]
        ktok = [mem.alloc("ktok", [128, 8, 128], BF16) for _ in range(2)]
        bktok = [[Buf("ktok") for h in range(8)] for p in range(2)]
        vv = [mem.alloc("vv", [128, 8, 128], BF16) for _ in range(2)]
        bvv = [[Buf("vva"), Buf("vvb")] for p in range(2)]
        sg = [mem.alloc("sg", [128, 8, 128], BF16) for _ in range(2)]
        bsg = [[Buf("sga"), Buf("sgb")] for p in range(2)]
        if has_samp:
            smt = mem.alloc("smt", [128, 16, 128], BF16); bsmt = Buf("smt")
            smb = mem.alloc("smb", [128, 16, 128], BF16); bsmb = Buf("smb")
            dma(smt[:].rearrange("p j v -> p (j v)"), smt_d, [], [bsmt], bsmt)
            dma(smb[:].rearrange("p j v -> p (j v)"), smb_d, [], [bsmb], bsmb)
            Sin = [mem.alloc("Sin", [128, 4, 128], F32) for _ in range(4)]; bSin = [Buf("Sin%d" % i) for i in range(4)]
            Sinb = [mem.alloc("Sinb", [128, 4, 128], BF16) for _ in range(4)]; bSinb = [Buf("Sinb%d" % i) for i in range(4)]
            vblk = [mem.alloc("vblk", [128, 4, 128], BF16) for _ in range(4)]; bvblk = [Buf("vblk%d" % i) for i in range(4)]
            qblk = [mem.alloc("qblk", [128, 4, 128], BF16) for _ in range(4)]; bqblk = [Buf("qblk%d" % i) for i in range(4)]
        wn1b = wn1.rearrange("p (k o) -> p k o", o=1).broadcast_to([128, 8, 128])
        has_first = any(gi == 0 for (_, gi) in tiles)
        if has_first:
            q32 = mem.alloc("q32", [128, 8, 128], F32); bq32 = [Buf("q32_%d" % h) for h in range(8)]
            k32 = mem.alloc("k32", [128, 8, 128], F32); bk32 = [Buf("k32_%d" % h) for h in range(8)]
            q32tok = mem.alloc("q32tok", [128, 4, 128], F32); bq32tok = Buf("q32tok")
            k32tok = mem.alloc("k32tok", [128, 4, 128], F32); bk32tok = Buf("k32tok")
            rtf = [mem.alloc("rtf", [128, 256], F32) for _ in range(4)]; brtf = [Buf("rtf%d" % i) for i in range(4)]

        def front(idx):
            slot, gi = tiles[idx]
            p = idx % 2
            samp = (gi == 16)
            Xj = X[:, slot, :]
            rp = rope[p]

            def F1a():
                dma(Xj, x_d[gi * 128:(gi + 1) * 128, :], [], [bX[slot]], bX[slot])
                dma(rp[:].rearrange("p a c -> p (a c)"), rope_d[gi], [], [brope[p]], brope[p])
                act(xn[:], Xj, AF.Square, [bX[slot]], [bxn, bst], accum=st[:, 0:1])
                ts("dve", st[:, 1:2], st[:, 0:1], 1.0 / 1024, EPS, ALU.mult, ALU.add, [bst], [bst])
                tt("pool", st[:, 2:3], st[:, 1:2], mhalf, ALU.pow, [bst, bcst], [bst])
                act(xn[:], Xj, AF.Identity, [bX[slot], bst], [bxn], scale=st[:, 2:3])

            def F1b():
                for k in range(8):
                    tr(bankb(0)[:, k * 128:(k + 1) * 128], xn[:, k * 128:(k + 1) * 128], identb, [bxn, bcbf], [bB[0]])
                tt("dve", hT[:], bankb(0)[:, 0:1024].rearrange("p (k t) -> p k t", k=8), wn1b, ALU.mult, [bB[0], bsmall], [bhT])

            def F2():
                for c in (4, 5, 6, 7, 0, 1, 2, 3):
                    bk_ = 1 if c < 4 else 2
                    hh = c % 4
                    for k in range(8):
                        mm(bank(bk_)[:, hh * 128:(hh + 1) * 128], Win[:, k, c * 128:(c + 1) * 128], hT[:, k, :],
                           k == 0, k == 7, [bWin, bWinHi, bhT], [bB[bk_]])
                act(th[:].rearrange("p h t -> p (h t)"), bank(2)[:, 0:512], AF.Tanh, [bB[2]], [bth], scale=0.5)
                nseg = 16 if samp else 1
                seglen = 128 // nseg
                for h in range(4):
                    qa = bank(1)[:, h * 128:(h + 1) * 128]
                    act(ff[:], th[:, h, :], AF.Identity, [bth, bcst], [bff], scale=c1[:, h:h + 1], bias=c0[:, h:h + 1])
                    act(kk[:], th[:, h, :], AF.Identity, [bth, bcst], [bkk], scale=nc1[:, h:h + 1], bias=c1[:, h:h + 1])
                    for sgi in range(nseg):
                        a, b_ = sgi * seglen, (sgi + 1) * seglen
                        P.op("dve", (lambda a, b_, h: lambda e: e.tensor_tensor_scan(
                            out=Pc[p][:, h, a:b_], data0=ff[:, a:b_], data1=zeros[:, a:b_], initial=1.0,
                            op0=ALU.mult, op1=ALU.add))(a, b_, h), [bff, bzeros], [bPc[p][h]])
                    P.op("dve", (lambda h: lambda e: e.reciprocal(out=RR[:], in_=Pc[p][:, h, :]))(h), [bPc[p][h]], [bRR])
                    tt("dve", qT[p][:, h, :], qa, Pc[p][:, h, :], ALU.mult, [bB[1], bPc[p][h]], [bqT[p][h]])
                    tt("dve", kT[p][:, h, :], kk[:], RR[:], ALU.mult, [bkk, bRR], [bkT[p][h]])
                    if gi == 0:
                        tt("dve", q32[:, h, :], qa, Pc[p][:, h, :], ALU.mult, [bB[1], bPc[p][h]], [bq32[h]])
                        tt("dve", k32[:, h, :], kk[:], RR[:], ALU.mult, [bkk, bRR], [bk32[h]])
                    if not samp:
                        ts("dve", khT[:, h, :], kT[p][:, h, :], Pc[p][:, h, 127:128], None, ALU.mult, None, [bkT[p][h], bPc[p][h]], [bkhT[h]])

            def F3():
                for (c0_, bk_) in [(2048, 3), (2560, 0)]:
                    for k in range(8):
                        mm(bank(bk_)[:, 0:512], hT[:, k, :], Win[:, k, c0_:c0_ + 512], k == 0, k == 7, [bWin, bWinHi, bhT], [bB[bk_]])
                for h in range(4):
                    if samp:
                        tr(bankb(2)[:, h * 128:(h + 1) * 128], kT[p][:, h, :], identb, [bkT[p][h], bcbf], [bB[2]])
                    else:
                        tr(bankb(2)[:, h * 128:(h + 1) * 128], khT[:, h, :], identb, [bkhT[h], bcbf], [bB[2]])
                for h in range(4):
                    cp("act", ktok[p][:, h, :], bankb(2)[:, h * 128:(h + 1) * 128], [bB[2]], [bktok[p][h]])
                for (bk_, ci, si_, isq) in [(3, 0, 1, True), (0, 2, 3, False)]:
                    src = bank(bk_)[:, 0:512].rearrange("p (h d) -> p h d", h=4)
                    x1 = src[:, :, 0:64]
                    x2 = src[:, :, 64:128]
                    cs = rp[:, ci, :].rearrange("p (h d) -> p h d", h=4)
                    sn = rp[:, si_, :].rearrange("p (h d) -> p h d", h=4)
                    ro = 0 if isq else 4
                    tv = [rt[ro + i][:].rearrange("p (h d) -> p h d", h=4) for i in range(4)]
                    if isq:
                        d1, d2, bdst = qtok[:, :, 0:64], qtok[:, :, 64:128], [bqtok]
                    else:
                        d1, d2, bdst = ktok[p][:, 4:8, 0:64], ktok[p][:, 4:8, 64:128], bktok[p][4:8]
                    tt("dve", tv[0], x1, cs, ALU.mult, [bB[bk_], brope[p]], [brt[ro]])
                    tt("dve", tv[1], x2, sn, ALU.mult, [bB[bk_], brope[p]], [brt[ro + 1]])
                    tt("dve", tv[2], x1, sn, ALU.mult, [bB[bk_], brope[p]], [brt[ro + 2]])
                    tt("dve", tv[3], x2, cs, ALU.mult, [bB[bk_], brope[p]], [brt[ro + 3]])
                    tt("pool", d1, tv[0], tv[1], ALU.subtract, [brt[ro], brt[ro + 1]], bdst)
                    tt("pool", d2, tv[2], tv[3], ALU.add, [brt[ro + 2], brt[ro + 3]], bdst)
                    if gi == 0:
                        tf = [rtf[i][:].rearrange("p (h d) -> p h d", h=4) for i in range(4)]
                        dst32, bd32 = (q32tok, bq32tok) if isq else (k32tok, bk32tok)
                        tt("dve", tf[0], x1, cs, ALU.mult, [bB[bk_], brope[p]], [brtf[0]])
                        tt("dve", tf[1], x2, sn, ALU.mult, [bB[bk_], brope[p]], [brtf[1]])
                        tt("dve", tf[2], x1, sn, ALU.mult, [bB[bk_], brope[p]], [brtf[2]])
                        tt("dve", tf[3], x2, cs, ALU.mult, [bB[bk_], brope[p]], [brtf[3]])
                        tt("dve", dst32[:, :, 0:64], tf[0], tf[1], ALU.subtract, [brtf[0], brtf[1]], [bd32])
                        tt("dve", dst32[:, :, 64:128], tf[2], tf[3], ALU.add, [brtf[2], brtf[3]], [bd32])

            def F4():
                for (c0_, bk_) in [(1024, 1), (3072, 2)]:
                    for k in range(8):
                        mm(bank(bk_)[:, 0:512], hT[:, k, :], Win[:, k, c0_:c0_ + 512], k == 0, k == 7, [bWin, bWinHi, bhT], [bB[bk_]])
                cp("act", vv[p][:, 0:4, :].rearrange("p h d -> p (h d)"), bank(1)[:, 0:512], [bB[1]], [bvv[p][0]])
                cp("act", vv[p][:, 4:8, :].rearrange("p h d -> p (h d)"), bank(2)[:, 0:512], [bB[2]], [bvv[p][1]])
                if not samp:
                    for h in range(4):
                        act(vvh[p][:, h, :], bank(2)[:, h * 128:(h + 1) * 128], AF.Copy, [bB[2]], [bvvh[p]], scale=float(GAM[h] ** 128))
                for (c0_, bk_) in [(1536, 3), (3584, 0)]:
                    for k in range(8):
                        mm(bank(bk_)[:, 0:512], hT[:, k, :], Win[:, k, c0_:c0_ + 512], k == 0, k == 7, [bWin, bWinHi, bhT], [bB[bk_]])
                act(sg[p][:, 0:4, :].rearrange("p h d -> p (h d)"), bank(3)[:, 0:512], AF.Silu, [bB[3]], [bsg[p][0]])
                act(sg[p][:, 4:8, :].rearrange("p h d -> p (h d)"), bank(0)[:, 0:512], AF.Silu, [bB[0]], [bsg[p][1]])
                for h in range(4):
                    tr(bankb(1)[:, h * 128:(h + 1) * 128], qtok[:, h, :], identb, [bqtok, bcbf], [bB[1]])
                    tr(bankb(1)[:, (4 + h) * 128:(5 + h) * 128], ktok[p][:, 4 + h, :], identb, [bktok[p][4 + h], bcbf], [bB[1]])
                for h in range(4):
                    cp("act", qT[p][:, 4 + h, :], bankb(1)[:, h * 128:(h + 1) * 128], [bB[1]], [bqT[p][4 + h]])
                    cp("act", kT[p][:, 4 + h, :], bankb(1)[:, (4 + h) * 128:(5 + h) * 128], [bB[1]], [bkT[p][4 + h]])
                if gi == 0:
                    for h in range(4):
                        tr(bank(2)[:, h * 128:(h + 1) * 128], q32tok[:, h, :], idf[:], [bq32tok, bidf], [bB[2]])
                        tr(bank(3)[:, h * 128:(h + 1) * 128], k32tok[:, h, :], idf[:], [bk32tok, bidf], [bB[3]])
                    for h in range(4):
                        cp("dve", q32[:, 4 + h, :], bank(2)[:, h * 128:(h + 1) * 128], [bB[2]], [bq32[4 + h]])
                        cp("dve", k32[:, 4 + h, :], bank(3)[:, h * 128:(h + 1) * 128], [bB[3]], [bk32[4 + h]])

            return [F1a, F1b, F2, F3, F4]

        def back(idx):
            slot, gi = tiles[idx]
            p = idx % 2
            samp = (gi == 16)
            Xj = X[:, slot, :]
            mask = maskS if samp else maskP

            def K1():
                for h in range(8):
                    bk_ = 4 + h // 4
                    sc = bank(bk_)[:, (h % 4) * 128:(h % 4 + 1) * 128]
                    if gi == 0:
                        mm(sc, k32[:, h, :], q32[:, h, :], True, True, [bk32[h], bq32[h]], [bB[bk_]])
                    else:
                        mm(sc, kT[p][:, h, :], qT[p][:, h, :], True, True, [bkT[p][h], bqT[p][h]], [bB[bk_]])
                for h in range(8):
                    bk_ = 4 + h // 4
                    sc = bank(bk_)[:, (h % 4) * 128:(h % 4 + 1) * 128]
                    tt("dve", scm[:, h, :], sc, mask, ALU.mult, [bB[bk_], bcbf], [bscm[h]])

            def K2():
                if not samp:
                    for h in range(8):
                        bo = 6 + h // 4
                        ov = bank(bo)[:, (h % 4) * 128:(h % 4 + 1) * 128]
                        mm(ov, scm[:, h, :], vv[p][:, h, :], True, False, [bscm[h], bvv[p][h // 4]], [bB[bo]])
                        mm(ov, qT[p][:, h, :], Sb[:, h, :], False, True, [bqT[p][h], bSb[h]], [bB[bo]])
                    for h in range(8):
                        bu = 4 + h // 4
                        uv = bank(bu)[:, (h % 4) * 128:(h % 4 + 1) * 128]
                        if h < 4:
                            mm(uv, ktok[p][:, h, :], vv[p][:, h, :], True, True, [bktok[p][h], bvv[p][0]], [bB[bu]])
                        else:
                            mm(uv, ktok[p][:, h, :], vvh[p][:, h - 4, :], True, True, [bktok[p][h], bvvh[p]], [bB[bu]])
                    for h in range(8):
                        bu = 4 + h // 4
                        uv = bank(bu)[:, (h % 4) * 128:(h % 4 + 1) * 128]
                        E = Pc[p][:, h, 127:128] if h < 4 else float(GAM[h - 4] ** 128)
                        rd = [bS[h], bB[bu]] + ([bPc[p][h]] if h < 4 else [])
                        stt(S[:, h, :], S[:, h, :], E, uv, ALU.mult, ALU.add, rd, [bS[h]])
                        cp("pool", Sb[:, h, :], S[:, h, :], [bS[h]], [bSb[h]])
                else:
                    def src_of(i):
                        h, q = i // 4, i % 4
                        st_d = sh_d if h < 4 else sr_d
                        return st_d[4 * q:4 * q + 4, h % 4].rearrange("j d v -> d j v")

                    def load(i):
                        sl = i % 4
                        sv = src_of(i)
                        dma(Sin[sl][:], sv, [], [bSin[sl]], bSin[sl])

                    def cast(i):
                        sl = i % 4
                        cp("act", Sinb[sl][:].rearrange("p j v -> p (j v)"), Sin[sl][:].rearrange("p j v -> p (j v)"), [bSin[sl]], [bSinb[sl]])

                    def pre(i):
                        h, q = i // 4, i % 4
                        sl = i % 4
                        tt("dve", qblk[sl][:], qT[p][:, h:h + 1, :].broadcast_to([128, 4, 128]), smb[:, 4 * q:4 * q + 4, :], ALU.mult,
                           [bqT[p][h], bsmb], [bqblk[sl]])
                        tt("dve", vblk[sl][:], vv[p][:, h:h + 1, :].broadcast_to([128, 4, 128]), smt[:, 4 * q:4 * q + 4, :], ALU.mult,
                           [bvv[p][h // 4], bsmt], [bvblk[sl]])

                    load(0)
                    load(1)
                    cast(0)
                    pre(0)
                    for i in range(32):
                        h, q = i // 4, i % 4
                        sl = i % 4
                        if i + 2 < 32:
                            load(i + 2)
                        if i + 1 < 32:
                            cast(i + 1)
                            pre(i + 1)
                        bo = 6 + h // 4
                        ov = bank(bo)[:, (h % 4) * 128:(h % 4 + 1) * 128]
                        ns_d = nhs_d if h < 4 else nrs_d
                        if q == 0:
                            mm(ov, scm[:, h, :], vv[p][:, h, :], True, False, [bscm[h], bvv[p][h // 4]], [bB[bo]])
                        for j in range(4):
                            mm(ov, qblk[sl][:, j, :], Sinb[sl][:, j, :], False, (q == 3 and j == 3), [bqblk[sl], bSinb[sl]], [bB[bo]])
                        mm(bank(q)[:, 0:512], ktok[p][:, h, :], vblk[sl][:].rearrange("p j v -> p (j v)"), True, True,
                           [bktok[p][h], bvblk[sl]], [bB[q]])
                        Sf = Sin[sl][:].rearrange("p j v -> p (j v)")
                        tt("dve", Sf, bank(q)[:, 0:512], Sf, ALU.add, [bB[q], bSin[sl]], [bSin[sl]])
                        if h < 4:
                            Eb = Pc[p][:, h, :].rearrange("p (j t) -> p j t", t=8)[:, 4 * q:4 * q + 4, 7:8].broadcast_to([128, 4, 128])
                            tt("dve", Sin[sl][:], Sin[sl][:], Eb, ALU.mult, [bSin[sl], bPc[p][h]], [bSin[sl]])
                        else:
                            act(Sf, Sf, AF.Identity, [bSin[sl]], [bSin[sl]], scale=float(GAM[h - 4] ** 8))
                        dma(ns_d[4 * q:4 * q + 4, h % 4].rearrange("j d v -> d j v"), Sin[sl][:], [bSin[sl]], [], bSin[sl])

            def K3a():
                mv = mem_mv
                o_all = PS[:, 6 * 512:8 * 512]
                act(sq[:], o_all, AF.Square, [bB[6], bB[7]], [bsq])
                P.op("dve", lambda e: e.tensor_reduce(out=mv[:, 0:8], in_=o_all.rearrange("p (h d) -> p h d", h=8),
                                                       axis=mybir.AxisListType.X, op=ALU.add), [bB[6], bB[7]], [bmv])
                P.op("dve", lambda e: e.tensor_reduce(out=mv[:, 8:16], in_=sq[:].rearrange("p (h d) -> p h d", h=8),
                                                       axis=mybir.AxisListType.X, op=ALU.add), [bsq], [bmv])
            def K3a1():
                mv = mem_mv
                o_all = PS[:, 6 * 512:8 * 512]
                ts("dve", mv[:, 16:24], mv[:, 0:8], 1.0 / 128, None, ALU.mult, None, [bmv], [bmv])
                ts("dve", mv[:, 24:32], mv[:, 8:16], 1.0 / 128, EPS, ALU.mult, ALU.add, [bmv], [bmv])
                tt("dve", mv[:, 32:36], mv[:, 20:24], mv[:, 20:24], ALU.mult, [bmv], [bmv])
                tt("dve", mv[:, 28:32], mv[:, 28:32], mv[:, 32:36], ALU.subtract, [bmv], [bmv])
                tt("pool", mv[:, 36:44], mv[:, 24:32], mhalf.broadcast_to([128, 8]), ALU.pow, [bmv, bcst], [bmv])
                stt(mv[:, 48:52], mv[:, 20:24], -1.0, mv[:, 40:44], ALU.mult, ALU.mult, [bmv], [bmv])
                og3 = og[:].rearrange("p (h d) -> p h d", h=8)
                rs_b = mv[:, 36:44].rearrange("p (h o) -> p h o", o=1).broadcast_to([128, 8, 128])
                nm_b = mv[:, 48:52].rearrange("p (h o) -> p h o", o=1).broadcast_to([128, 4, 128])
                tt("dve", og3, o_all.rearrange("p (h d) -> p h d", h=8), rs_b, ALU.mult, [bB[6], bB[7], bmv], [bog])
                tt("dve", og3[:, 4:8, :], og3[:, 4:8, :], nm_b, ALU.add, [bog, bmv], [bog])
                tt("dve", og[:, 0:512], og[:, 0:512], sg[p][:, 0:4, :].rearrange("p h d -> p (h d)"), ALU.mult, [bog, bsg[p][0]], [bog])
                tt("dve", og[:, 512:1024], og[:, 512:1024], sg[p][:, 4:8, :].rearrange("p h d -> p (h d)"), ALU.mult, [bog, bsg[p][1]], [bog])

            def K3b():
                for k in range(8):
                    tr(bankb(7)[:, k * 128:(k + 1) * 128], og[:, k * 128:(k + 1) * 128], identb, [bog, bcbf], [bB[7]])
                act(ogT[:, 0:4, :].rearrange("p k t -> p (k t)"), bankb(7)[:, 0:512], AF.Identity, [bB[7], bsmall], [bogT], scale=nwa)
                act(ogT[:, 4:8, :].rearrange("p k t -> p (k t)"), bankb(7)[:, 512:1024], AF.Identity, [bB[7], bsmall], [bogT], scale=nwb)

            def K4():
                for hf in range(2):
                    for k in range(8):
                        mm(bank(4 + hf)[:, 0:512], ogT[:, k, :], Wout[:, k, hf * 512:(hf + 1) * 512], k == 0, k == 7,
                           [bogT, bWout], [bB[4 + hf]])
                for hf in range(2):
                    tt("dve", Xj[:, hf * 512:(hf + 1) * 512], bank(4 + hf)[:, 0:512], Xj[:, hf * 512:(hf + 1) * 512], ALU.add,
                       [bB[4 + hf], bX[slot]], [bX[slot]])
                if gi == 15:
                    for h in range(8):
                        dst = nhp_d[h] if h < 4 else nrp_d[h - 4]
                        dma(dst, S[:, h, :], [bS[h]], [], bS[h])

            return [K1, K2, K3a, K3b, K4, K3a1]

        sq = mem.alloc("sq", [128, 1024], F32); bsq = Buf("sq")
        P.op("pool", lambda e: e.memset(mem_mv[:, 44:48], 0.0), [], [bmv])
        fronts = [front(i) for i in range(ntl)]
        backs = [back(i) for i in range(ntl)]
        for c in fronts[0]:
            c()
        if ntl > 1:
            fronts[1][0]()
        ck(2)
        for idx in range(ntl):
            K1, K2, K3a, K3b, K4, _K3a1 = backs[idx]
            nf = fronts[idx + 1] if idx + 1 < ntl else None
            if nf:
                nf[1]()
            K1()
            if nf:
                nf[2]()
            K2()
            if nf:
                nf[3]()
            K3a()
            if idx + 2 < ntl:
                fronts[idx + 2][0]()
            backs[idx][5]()
            if nf:
                nf[4]()
            if idx > 0:
                backs[idx - 1][4]()
            K3b()
            if idx == ntl - 1:
                K4()
            if idx == ntl - 2 or ntl == 1:
                load_wfi(0)
                pref["wfi"] = True

    def phase2(tiles):
        nt = len(tiles)
        h2T = mem.alloc("h2T", [128, 8, nt * 128], BF16)
        bh2 = [Buf("h2T%d" % i) for i in range(nt)]
        Wfo = mem.alloc("Wfo", [128, 11, 1024], BF16); bWfo = Buf("Wfo")
        wnf = mem.alloc("wnf", [128, 1024], F32); bwnf = Buf("wnf")
        dma(wnf[:], wnf_d, [], [bwnf], bwnf)
        wn2b = wn2.rearrange("p (k o) -> p k o", o=1).broadcast_to([128, 8, 128])
        xn2 = [mem.alloc("xn2", [128, 1024], BF16) for _ in range(2)]; bxn2 = [Buf("xn2a"), Buf("xn2b")]
        st2 = [mem.alloc("st2", [128, 16], F32) for _ in range(2)]; bst2 = [Buf("st2a"), Buf("st2b")]
        xn, bxn, st, bst = xn2[0], bxn2[0], st2[0], bst2[0]
        hl = mem.alloc("hl", [128, 4, 4], F32); bhl = [Buf("hl%d" % i) for i in range(4)]
        ext = [mem.alloc("ext", [128, 160], F32) for _ in range(2)]
        bext = [Buf("ext0"), Buf("ext1")]
        NCC = 4
        cc = [mem.alloc("cc", [128, 512], F32) for _ in range(NCC)]
        bcc = [Buf("cc%d" % i) for i in range(NCC)]
        sgl = [mem.alloc("sgl", [128, 512], BF16) for _ in range(2)]; bsgl = [Buf("sgl0"), Buf("sgl1")]
        actT = mem.alloc("actT", [128, 11, 512], BF16); bactT = [Buf("actT%d" % i) for i in range(11)]
        ncT = mem.alloc("ncT", [128, 22, 32], F32); bncT = Buf("ncT")
        scT = mem.alloc("scT", [128, 22, 32], F32); bscT = Buf("scT")
        sc32 = mem.alloc("sc32", [32, 2816], F32); bsc32 = Buf("sc32")
        yb = [mem.alloc("yb", [128, 1024], F32) for _ in range(2)]; byb = [Buf("yb0"), Buf("yb1")]
        has_samp = any(gi == 16 for (_, gi) in tiles)
        sts = []
        pt = [t for t in tiles if t[1] != 16]
        for i in range(0, len(pt), 4):
            sts.append(pt[i:i + 4])
        if has_samp:
            sts.append([t for t in tiles if t[1] == 16])
        slot_pos = {s_: i for i, (s_, _) in enumerate(tiles)}
        ycnt = [0]

        def prepA(slot, pi_):
            Xj = X[:, slot, :]
            xn, bxn, st, bst = xn2[pi_], bxn2[pi_], st2[pi_], bst2[pi_]
            act(xn[:], Xj, AF.Square, [bX[slot]], [bxn, bst], accum=st[:, 0:1])
            ts("dve", st[:, 1:2], st[:, 0:1], 1.0 / 1024, EPS, ALU.mult, ALU.add, [bst], [bst])
            tt("pool", st[:, 2:3], st[:, 1:2], mhalf, ALU.pow, [bst, bcst], [bst])
            act(xn[:], Xj, AF.Identity, [bX[slot], bst], [bxn], scale=st[:, 2:3])

        def prepB(slot, pi_):
            pp = slot_pos[slot]
            xn, bxn = xn2[pi_], bxn2[pi_]
            for k in range(8):
                tr(bankb(7)[:, k * 128:(k + 1) * 128], xn[:, k * 128:(k + 1) * 128], identb, [bxn, bcbf], [bB[7]])
            tt("dve", h2T[:, :, pp * 128:(pp + 1) * 128],
               bankb(7)[:, 0:1024].rearrange("p (k t) -> p k t", k=8), wn2b, ALU.mult, [bB[7], bsmall], [bh2[pp]])

        for half in range(2):
            if not pref["wfi"]:
                load_wfi(half)
            pref["wfi"] = False
            if half == 1 and not has_samp:
                load_win(6, 8, bWinHi)
                pref["win_hi"] = True
            dma_cast(Wfo[:], w_fo_d[half * 1408:(half + 1) * 1408, :].rearrange("(c p) n -> p c n", p=128), [bWfo], bWfo)
            if has_samp:
                for part in range(2):
                    c0_ = part * DFF + half * 1408
                    dma(sc32[:, part * 1408:(part + 1) * 1408], sc_d[:, c0_:c0_ + 1408], [], [bsc32], bsc32)
                for ft in range(22):
                    bk = 4 + (ft % 2)
                    tr(bank(bk)[:, 0:32], sc32[:, ft * 128:(ft + 1) * 128], idf[0:32, 0:32], [bsc32, bidf], [bB[bk]])
                    cp("dve", scT[:, ft, :], bank(bk)[:, 0:32], [bB[bk]], [bscT])
            if half == 0:
                fs = [s_ for (s_, _) in sts[0]]
                prepA(fs[0], 0)
                for j in range(len(fs)):
                    if j + 1 < len(fs):
                        prepA(fs[j + 1], (j + 1) % 2)
                    prepB(fs[j], j % 2)

            for sti, stl in enumerate(sts):
                samp = (stl[0][1] == 16)
                ntk = 128 * len(stl)
                p0 = slot_pos[stl[0][0]] * 128
                nxt = [s_ for (s_, _) in sts[sti + 1]] if (half == 0 and sti + 1 < len(sts)) else []
                rd_h2 = [bh2[slot_pos[s_]] for (s_, _) in stl]
                fi = 0
                tails = []
                for c in range(11):
                    if nxt and c % 2 == 0 and (c // 2) < len(nxt):
                        prepA(nxt[c // 2], (c // 2) % 2)
                    for part in (1, 0):
                        ft = part * 11 + c
                        gft = part * 22 + half * 11 + c
                        bk = fi % 4
                        ci = fi % NCC
                        fi += 1
                        up = bank(bk)[:, 0:ntk]
                        for k in range(8):
                            mm(up, Wfi[:, k, part * 1408 + c * 128: part * 1408 + (c + 1) * 128],
                               h2T[:, k, p0:p0 + ntk], k == 0, k == 7, [bWfi] + rd_h2, [bB[bk]])
                        cv = cc[ci]
                        w0 = convw[:, gft:gft + 1]
                        w1 = convw[:, 44 + gft:45 + gft]
                        w2 = convw[:, 88 + gft:89 + gft]
                        cb = convw[:, 132 + gft:133 + gft]
                        if not samp:
                            hi_ = fi % 4
                            cp_ = cpar[gft]
                            cold = carry[:, cp_, gft, :]
                            act(cv[:, 0:ntk], up, AF.Identity, [bB[bk], bconvw], [bcc[ci]], scale=w2, bias=cb)
                            cp("act", carry[:, 1 - cp_, gft, :], up[:, ntk - 2:ntk], [bB[bk]], [bcarry[1 - cp_][gft]])
                            stt(cv[:, 1:ntk], up[:, 0:ntk - 1], w1, cv[:, 1:ntk], ALU.mult, ALU.add, [bB[bk], bcc[ci], bconvw], [bcc[ci]])
                            stt(cv[:, 2:ntk], up[:, 0:ntk - 2], w0, cv[:, 2:ntk], ALU.mult, ALU.add, [bB[bk], bcc[ci], bconvw], [bcc[ci]])
                            ts("pool", hl[:, hi_, 0:2], cold, w0, None, ALU.mult, None, [bcarry[cp_][gft], bconvw], [bhl[hi_]])
                            ts("pool", hl[:, hi_, 2:3], cold[:, 1:2], w1, None, ALU.mult, None, [bcarry[cp_][gft], bconvw], [bhl[hi_]])
                            tt("pool", cv[:, 0:2], cv[:, 0:2], hl[:, hi_, 0:2], ALU.add, [bcc[ci], bhl[hi_]], [bcc[ci]])
                            tt("pool", cv[:, 0:1], cv[:, 0:1], hl[:, hi_, 2:3], ALU.add, [bcc[ci], bhl[hi_]], [bcc[ci]])
                            cpar[gft] = 1 - cp_
                        else:
                            e_i = fi % 2
                            ex = ext[e_i]
                            ex3 = ex[:, 0:160].rearrange("p (j t) -> p j t", t=10)
                            cv3 = cv[:, 0:128].rearrange("p (j t) -> p j t", t=8)
                            up3 = up.rearrange("p (j t) -> p j t", t=8)
                            cp("pool", ex3[:, :, 0:2], scT[:, ft, :].rearrange("p (j r) -> p j r", r=2), [bscT], [bext[e_i]])
                            cp("act", ex3[:, :, 2:10], up3, [bB[bk]], [bext[e_i]])
                            cp("pool", ncT[:, ft, :].rearrange("p (j r) -> p j r", r=2), ex3[:, :, 8:10], [bext[e_i]], [bncT])
                            act(cv[:, 0:128], up, AF.Identity, [bB[bk], bconvw], [bcc[ci]], scale=w2, bias=cb)
                            stt(cv3, ex3[:, :, 1:9], w1, cv3, ALU.mult, ALU.add, [bext[e_i], bcc[ci], bconvw], [bcc[ci]])
                            stt(cv3, ex3[:, :, 0:8], w0, cv3, ALU.mult, ALU.add, [bext[e_i], bcc[ci], bconvw], [bcc[ci]])
                        if part == 1:
                            tails.append((lambda cv, ci, c, ntk: lambda: act(sgl[c % 2][:, 0:ntk], cv[:, 0:ntk], AF.Silu, [bcc[ci]], [bsgl[c % 2]]))(cv, ci, c, ntk))
                        else:
                            tails.append((lambda cv, ci, c, ntk: lambda: tt("dve", actT[:, c, 0:ntk], cv[:, 0:ntk], sgl[c % 2][:, 0:ntk], ALU.mult, [bcc[ci], bsgl[c % 2]], [bactT[c]]))(cv, ci, c, ntk))
                        while len(tails) > 2:
                            tails.pop(0)()
                    if nxt and c in (1, 3, 5, 7) and (c // 2) < len(nxt):
                        prepB(nxt[c // 2], (c // 2) % 2)
                if sti == len(sts) - 1:
                    if half == 0:
                        load_wfi(1)
                        pref["wfi"] = True
                    elif not has_samp:
                        if pref.get("win_hi"):
                            load_win(0, 6)
                        else:
                            load_win()
                        pref["win"] = True
                while tails:
                    tails.pop(0)()
                for _ in range(8):
                    mm(bank(7)[:, 0:ntk], h2T[:, 0, p0:p0 + 128], h2T[:, 0, p0:p0 + ntk], True, True, rd_h2, [bB[7]])
                for ti, (slot, gi) in enumerate(stl):
                    Xj = X[:, slot, :]
                    for hf in range(2):
                        bk = 4 + (2 * ti + hf) % 3
                        for c in range(11):
                            mm(bank(bk)[:, 0:512], actT[:, c, ti * 128:(ti + 1) * 128], Wfo[:, c, hf * 512:(hf + 1) * 512],
                               c == 0, c == 10, [bactT[c], bWfo], [bB[bk]])
                        tt("dve", Xj[:, hf * 512:(hf + 1) * 512], bank(bk)[:, 0:512], Xj[:, hf * 512:(hf + 1) * 512], ALU.add,
                           [bB[bk], bX[slot]], [bX[slot]])
                    if half == 1:
                        yi = ycnt[0] % 2
                        ycnt[0] += 1
                        act(xn[:], Xj, AF.Square, [bX[slot]], [bxn, bst], accum=st[:, 4:5])
                        ts("dve", st[:, 5:6], st[:, 4:5], 1.0 / 1024, EPS, ALU.mult, ALU.add, [bst], [bst])
                        tt("pool", st[:, 6:7], st[:, 5:6], mhalf, ALU.pow, [bst, bcst], [bst])
                        stt(yb[yi][:], Xj, st[:, 6:7], wnf[:], ALU.mult, ALU.mult, [bX[slot], bst, bwnf], [byb[yi]])
                        dma(y_d[gi * 128:(gi + 1) * 128, :], yb[yi][:], [byb[yi]], [], byb[yi])
                last_prompt = (not samp) and stl[-1][1] == 15
                if last_prompt or samp:
                    ncols = 32 if samp else 2
                    for part in range(2):
                        for c in range(11):
                            ft = part * 11 + c
                            gft = part * 22 + half * 11 + c
                            bk = 4 + (ft % 3)
                            if samp:
                                src, rdb = ncT[:, ft, :], [bncT]
                            else:
                                src, rdb = carry[:, cpar[gft], gft, :], [bcarry[cpar[gft]][gft]]
                            tr(bank(bk)[0:ncols, 0:128], src, idf[:], rdb + [bidf], [bB[bk]])
                            cp("act", sc32[0:ncols, ft * 128:(ft + 1) * 128], bank(bk)[0:ncols, 0:128], [bB[bk]], [bsc32])
                    outd = ncs_d if samp else ncp_d
                    for part in range(2):
                        c0_ = part * DFF + half * 1408
                        dma(outd[:, c0_:c0_ + 1408], sc32[0:ncols, part * 1408:(part + 1) * 1408], [bsc32], [], bsc32)

    mem_mv = mem.alloc("mv", [128, 64], F32); bmv = Buf("mv")
    m0 = mem.mark()
    groups = [
        [(i, i) for i in range(8)],
        [(i - 8, i) for i in range(8, 17)],
    ]
    if "groups" in dbg:
        groups = dbg["groups"]
    try:
        ck(0)
        for tiles in groups:
            mem.reset(m0)
            if not dbg.get("skip1"):
                phase1(tiles)
            barrier()
            mem.reset(m0)
            if not dbg.get("skip2"):
                phase2(tiles)
            barrier()
    except _Stop:
        pass
    P.emit()
    return nc, mem.peak


def _consts():
    bf = ml_dtypes.bfloat16
    s = np.arange(128)
    ident = np.eye(128, dtype=np.float32)
    maskP = (s[:, None] <= s[None, :]).astype(np.float32)
    maskS = ((s[:, None] <= s[None, :]) & ((s[:, None] // 8) == (s[None, :] // 8))).astype(np.float32)
    cbf = np.concatenate([ident, maskP, maskS], axis=1).astype(bf)
    j = np.arange(16)
    smt = np.broadcast_to(((s[:, None] // 8) == j[None, :])[:, :, None], (128, 16, 128)).astype(bf).reshape(128, 2048)
    smb = np.broadcast_to(((s[None, :] // 8) == j[:, None])[None, :, :], (128, 16, 128)).astype(bf).reshape(128, 2048)
    half = 64
    inv = (1.0 / (np.float32(10000.0) ** (np.arange(half, dtype=np.float32) / np.float32(half)))).astype(np.float32)
    rope = np.zeros((NT, 128, 4, 4, 64), dtype=np.float32)
    for gi in range(NT):
        if gi < 16:
            pos = (gi * 128 + s).astype(np.float32)
            tau = s
        else:
            pos = (PAST_LEN + (s % 8)).astype(np.float32)
            tau = s % 8
        ang = (pos[:, None] * inv[None, :]).astype(np.float32).astype(np.float64)
        cs, sn = np.cos(ang), np.sin(ang)
        for h in range(4):
            dq = GAM[h] ** (tau + 1.0)
            dk = GAM[h] ** (-(tau + 1.0)) * (128.0 ** -0.5)
            rope[gi, :, 0, h] = cs * dq[:, None]
            rope[gi, :, 1, h] = sn * dq[:, None]
            rope[gi, :, 2, h] = cs * dk[:, None]
            rope[gi, :, 3, h] = sn * dk[:, None]
    rope = rope.reshape(NT, 128, 1024)
    return cbf, smt, smb, rope, ident


_CACHE = {}


def kernel(x_prompt, x_sample, state_hgrn, state_ret, state_conv, w_norm1, w_in, hgrn_lb,
           hgrn_norm_w, ret_norm_w, w_out, w_norm2, w_ffn_in, conv_w, conv_b, w_ffn_out, w_norm_f):
    f32 = np.float32
    if "nc" not in _CACHE:
        _CACHE["nc"] = build_program()[0]
        _CACHE["consts"] = _consts()
    nc = _CACHE["nc"]
    cbf, smt, smb, rope, ident = _CACHE["consts"]

    small = np.zeros((128, 64), dtype=f32)
    small[:, 0:8] = np.asarray(w_norm1[0], f32).reshape(8, 128).T
    small[:, 8:16] = np.asarray(w_norm2[0], f32).reshape(8, 128).T
    small[:, 16] = np.asarray(hgrn_norm_w[0], f32)
    small[:, 17] = np.asarray(ret_norm_w[0], f32)
    small[:, 18:22] = np.asarray(hgrn_lb[0], f32).reshape(4, 128).T
    small[:, 22:26] = np.asarray(hgrn_lb[1], f32).reshape(4, 128).T
    convw = np.zeros((128, 176), dtype=f32)
    for jj in range(3):
        convw[:, jj * 44:(jj + 1) * 44] = np.asarray(conv_w[0, jj], f32).reshape(44, 128).T
    convw[:, 132:176] = np.asarray(conv_b[0], f32).reshape(44, 128).T
    wnf = np.ascontiguousarray(np.broadcast_to(np.asarray(w_norm_f, f32)[None, :], (128, 1024)))

    shared = {
        "w_in": np.ascontiguousarray(w_in[0], dtype=f32), "w_out": np.ascontiguousarray(w_out[0], dtype=f32),
        "w_fi": np.ascontiguousarray(w_ffn_in[0], dtype=f32), "w_fo": np.ascontiguousarray(w_ffn_out[0], dtype=f32),
        "small": small, "convw": convw, "wnf": wnf, "rope": rope, "cbf": cbf, "smt": smt, "smb": smb, "idf": ident,
    }
    in_maps = []
    for c in range(8):
        xs = np.concatenate([np.asarray(x_prompt[c], f32), np.asarray(x_sample[16 * c:16 * c + 16], f32).reshape(128, 1024)], axis=0)
        m = dict(shared)
        m["x"] = np.ascontiguousarray(xs)
        m["sh"] = np.ascontiguousarray(state_hgrn[0, 16 * c:16 * c + 16], dtype=f32)
        m["sr"] = np.ascontiguousarray(state_ret[0, 16 * c:16 * c + 16], dtype=f32)
        m["sc"] = np.ascontiguousarray(np.asarray(state_conv[0, 16 * c:16 * c + 16], f32).reshape(32, 2 * DFF))
        in_maps.append(m)
    res = run_bass_kernel_spmd(nc, in_maps, core_ids=list(range(8)))
    R = res.results
    y_prompt = np.stack([R[c]["y"][:2048] for c in range(8)], axis=0)
    y_sample = np.concatenate([R[c]["y"][2048:].reshape(16, 8, 1024) for c in range(8)], axis=0)
    ha_p = np.stack([R[c]["nhp"] for c in range(8)], axis=0)[None]
    rb_p = np.stack([R[c]["nrp"] for c in range(8)], axis=0)[None]
    cv_p = np.stack([R[c]["ncp"] for c in range(8)], axis=0)[None]
    ha_s = np.concatenate([R[c]["nhs"] for c in range(8)], axis=0)[None]
    rb_s = np.concatenate([R[c]["nrs"] for c in range(8)], axis=0)[None]
    cv_s = np.concatenate([R[c]["ncs"].reshape(16, 2, 2 * DFF) for c in range(8)], axis=0)[None]
    return (y_prompt.astype(f32), y_sample.astype(f32), ha_p.astype(f32), rb_p.astype(f32), cv_p.astype(f32),
            ha_s.astype(f32), rb_s.astype(f32), cv_s.astype(f32))
```
